# Optimizing a Trainium2 kernel written in Bass

```python
import math
import jax, jax.numpy as jnp
from jax import lax

D_MODEL = 1024
BATCH = 2
SEQ = 8192
DEPTH = 2
DEC_BATCH = 128
DEC_SEQ = 1
PAST_LEN = 8192
PAGE_SIZE = 128

F32 = jnp.float32
EPS = 1e-6
N_META = 16
CHUNK = 64
A_HEADS = 8
A_KV_HEADS = 2
A_HEAD_DIM = 64
A_GROUP = A_HEADS // A_KV_HEADS
A_WIDTH = A_HEADS * A_HEAD_DIM
A_KV_WIDTH = A_KV_HEADS * A_HEAD_DIM
WINDOW = 128
A_BLOCK = 128
B_HEADS = 4
B_DK = 128
B_DV = 128
B_KEY = B_HEADS * B_DK
B_VAL = B_HEADS * B_DV
CONV_W = 4
B_CONV_CH = 2 * B_KEY + B_VAL
C_HEADS = 4
C_KEY = D_MODEL // 2
C_VAL = D_MODEL
C_DK = C_KEY // C_HEADS
C_DV = C_VAL // C_HEADS
C_GATE_RANK = 16
C_GATE_NORM = 16.0
N_AB_LAYERS = (DEPTH + 1) // 2
N_C_LAYERS = DEPTH // 2
WIN_BUF = min(WINDOW, PAST_LEN)
LEAD_PROMPT = (-N_META) % CHUNK
AB_SIZES = (A_WIDTH, A_KV_WIDTH, A_KV_WIDTH, A_WIDTH, B_CONV_CH, B_VAL, B_HEADS, B_HEADS)
AB_IN = A_WIDTH + 2 * A_KV_WIDTH + A_WIDTH + B_CONV_CH + B_VAL + 2 * B_HEADS
AB_MIX = A_WIDTH + B_VAL
C_SIZES = (C_KEY, C_KEY, C_VAL, C_VAL, C_GATE_RANK)
C_IN = 2 * C_KEY + 2 * C_VAL + C_GATE_RANK

kernel_name = 'hybrid_swa_sink_gdn_gla_meta_step'


def _rmsnorm(x, g):
    xf = x.astype(F32)
    y = xf * lax.rsqrt(jnp.mean(xf * xf, axis=-1, keepdims=True) + EPS)
    return (y * g.astype(F32)).astype(x.dtype)


def _l2norm(x):
    xf = x.astype(F32)
    return xf * lax.rsqrt(jnp.sum(xf * xf, axis=-1, keepdims=True) + EPS)


def _split(h, sizes):
    out, s = [], 0
    for n in sizes:
        out.append(h[..., s:s + n])
        s += n
    return out


def _sink_attend(q, k, v, mask, sink):
    s = jnp.einsum('...qhgd,...khd->...hgqk', q.astype(F32), k.astype(F32)) * (A_HEAD_DIM ** -0.5)
    s = jnp.where(mask[..., None, None, :, :], s, -jnp.inf)
    sk = sink.astype(F32)[:, :, None, None]
    m = jnp.maximum(jnp.max(s, axis=-1, keepdims=True), sk)
    p = jnp.exp(s - m)
    p = p / (jnp.sum(p, axis=-1, keepdims=True) + jnp.exp(sk - m))
    o = jnp.einsum('...hgqk,...khd->...qhgd', p, v.astype(F32))
    return o.astype(q.dtype)


def _swa_prompt(q, k, v, sink):
    Bn, L = q.shape[:2]
    nb = -(-L // A_BLOCK)
    Lp = nb * A_BLOCK
    qb = jnp.pad(q, ((0, 0), (0, Lp - L), (0, 0), (0, 0))).reshape(Bn, nb, A_BLOCK, A_KV_HEADS, A_GROUP, A_HEAD_DIM)
    kvpad = ((0, 0), (A_BLOCK, Lp - L), (0, 0), (0, 0))
    kb = jnp.pad(k, kvpad).reshape(Bn, nb + 1, A_BLOCK, A_KV_HEADS, A_HEAD_DIM)
    vb = jnp.pad(v, kvpad).reshape(Bn, nb + 1, A_BLOCK, A_KV_HEADS, A_HEAD_DIM)
    kband = jnp.concatenate([kb[:, :-1], kb[:, 1:]], axis=2)
    vband = jnp.concatenate([vb[:, :-1], vb[:, 1:]], axis=2)
    blk = jnp.arange(nb)[:, None] * A_BLOCK
    qpos = blk + jnp.arange(A_BLOCK)[None, :]
    kpos = blk - A_BLOCK + jnp.arange(2 * A_BLOCK)[None, :]
    diff = qpos[:, :, None] - kpos[:, None, :]
    mask = (diff >= 0) & (diff < WINDOW) & (kpos[:, None, :] >= 0)
    o = _sink_attend(qb, kband, vband, mask, sink)
    return o.reshape(Bn, Lp, A_HEADS, A_HEAD_DIM)[:, :L]


def _swa_sample(q, k_new, v_new, k_buf, v_buf, sink):
    Bn, T = q.shape[:2]
    nbuf = k_buf.shape[1]
    k = jnp.concatenate([k_buf.astype(k_new.dtype), k_new], axis=1)
    v = jnp.concatenate([v_buf.astype(v_new.dtype), v_new], axis=1)
    qpos = PAST_LEN + jnp.arange(T)
    kpos = PAST_LEN - nbuf + jnp.arange(nbuf + T)
    diff = qpos[:, None] - kpos[None, :]
    mask = (diff >= 0) & (diff < WINDOW)
    o = _sink_attend(q.reshape(Bn, T, A_KV_HEADS, A_GROUP, A_HEAD_DIM), k, v, mask, sink)
    return o.reshape(Bn, T, A_HEADS, A_HEAD_DIM), k[:, T:], v[:, T:]


def _causal_conv_silu(xc, w):
    T = xc.shape[1] - (CONV_W - 1)
    y = xc[:, 0:T] * w[0]
    for i in range(1, CONV_W):
        y = y + xc[:, i:i + T] * w[i]
    return jax.nn.silu(y)


def _to_chunks(a, lead, tail):
    a = jnp.pad(a.astype(F32), ((0, 0), (lead, tail)) + ((0, 0),) * (a.ndim - 2))
    Bn, Tp = a.shape[:2]
    a = a.reshape((Bn, Tp // CHUNK, CHUNK) + a.shape[2:])
    return jnp.moveaxis(a, 2, 3)


def _from_chunks(o, lead, T):
    o = jnp.moveaxis(jnp.moveaxis(o, 0, 1), 2, 3)
    Bn, N, C, H, d = o.shape
    return o.reshape(Bn, N * C, H, d)[:, lead:lead + T]


def _gated_delta_chunked(q, k, v, beta, g, s0, lead):
    T = q.shape[1]
    tail = (-(T + lead)) % CHUNK
    qc, kc, vc = (_to_chunks(a, lead, tail) for a in (q, k, v))
    bc, gcs = (_to_chunks(a, lead, tail) for a in (beta, g))
    gc = jnp.cumsum(gcs, axis=-1)
    idx = jnp.arange(CHUNK)
    lower = idx[:, None] >= idx[None, :]
    strict = idx[:, None] > idx[None, :]
    decay = jnp.exp(jnp.where(lower, gc[..., :, None] - gc[..., None, :], -jnp.inf))
    kbeta = kc * bc[..., None]
    a_mat = jnp.where(strict, jnp.einsum('bnhik,bnhjk->bnhij', kbeta, kc) * decay, 0.0) + jnp.eye(CHUNK, dtype=F32)
    rhs = jnp.concatenate([vc * bc[..., None], kbeta * jnp.exp(gc)[..., None]], axis=-1)
    sol = lax.linalg.triangular_solve(a_mat, rhs, left_side=True, lower=True, unit_diagonal=True)
    u, w = sol[..., :B_DV], sol[..., B_DV:]
    qk = jnp.where(lower, jnp.einsum('bnhik,bnhjk->bnhij', qc, kc) * decay, 0.0)
    q_dec = qc * jnp.exp(gc)[..., None]
    k_dec = kc * jnp.exp(gc[..., -1:] - gc)[..., None]
    g_last = jnp.exp(gc[..., -1])

    def step(S, xs):
        u_c, w_c, qk_c, qd_c, kd_c, gl_c = xs
        v_new = u_c - jnp.einsum('bhck,bhkv->bhcv', w_c, S)
        o = jnp.einsum('bhck,bhkv->bhcv', qd_c, S) + jnp.einsum('bhij,bhjv->bhiv', qk_c, v_new)
        S = S * gl_c[..., None, None] + jnp.einsum('bhck,bhcv->bhkv', kd_c, v_new)
        return S, o

    xs = tuple(jnp.moveaxis(a, 1, 0) for a in (u, w, qk, q_dec, k_dec, g_last))
    S, o = lax.scan(step, s0.astype(F32), xs)
    return _from_chunks(o, lead, T), S


def _gla_chunked(q, k, v, log_a, s0, lead):
    T = q.shape[1]
    tail = (-(T + lead)) % CHUNK
    qc, kc, vc, lc = (_to_chunks(a, lead, tail) for a in (q, k, v, log_a))
    bc = jnp.cumsum(lc, axis=-2)
    idx = jnp.arange(CHUNK)
    lower = idx[:, None] >= idx[None, :]
    q_dec = qc * jnp.exp(bc)
    qk = jnp.einsum('bnhik,bnhjk->bnhij', q_dec, kc * jnp.exp(-bc))
    o_intra = jnp.einsum('bnhij,bnhjv->bnhiv', jnp.where(lower, qk, 0.0), vc)
    k_dec = kc * jnp.exp(bc[..., -1:, :] - bc)
    g_last = jnp.exp(bc[..., -1, :])

    def step(S, xs):
        qd_c, kd_c, v_c, gl_c, oi_c = xs
        o = oi_c + jnp.einsum('bhck,bhkv->bhcv', qd_c, S)
        S = S * gl_c[..., None] + jnp.einsum('bhck,bhcv->bhkv', kd_c, v_c)
        return S, o

    xs = tuple(jnp.moveaxis(a, 1, 0) for a in (q_dec, k_dec, vc, g_last, o_intra))
    S, o = lax.scan(step, s0.astype(F32), xs)
    return _from_chunks(o, lead, T), S


def _ab_layer(x, norm_g, w_in, sink, conv_w, a_log, dt_bias, onorm, w_out, k_buf, v_buf, conv_buf, s0, lead):
    Bn, T, _ = x.shape
    h = _rmsnorm(x, norm_g)
    proj = jnp.einsum('btd,de->bte', h, w_in)
    qa, ka, va, za, xb, zb, bb, ab = _split(proj, AB_SIZES)
    qa = qa.reshape(Bn, T, A_HEADS, A_HEAD_DIM)
    ka = ka.reshape(Bn, T, A_KV_HEADS, A_HEAD_DIM)
    va = va.reshape(Bn, T, A_KV_HEADS, A_HEAD_DIM)
    sink = sink.reshape(A_KV_HEADS, A_GROUP)
    if k_buf is None:
        oa = _swa_prompt(qa, ka, va, sink)
        n_keep = min(WINDOW, T)
        new_k, new_v = ka[:, T - n_keep:], va[:, T - n_keep:]
        conv_buf = jnp.zeros((Bn, CONV_W - 1, B_CONV_CH), xb.dtype)
        s0 = jnp.zeros((Bn, B_HEADS, B_DK, B_DV), F32)
    else:
        oa, new_k, new_v = _swa_sample(qa, ka, va, k_buf, v_buf, sink)
    xc = jnp.concatenate([conv_buf.astype(xb.dtype), xb], axis=1)
    new_conv = xc[:, xc.shape[1] - (CONV_W - 1):]
    c = _causal_conv_silu(xc, conv_w.astype(xb.dtype))
    qb, kb, vb = _split(c, (B_KEY, B_KEY, B_VAL))
    qb = _l2norm(qb.reshape(Bn, T, B_HEADS, B_DK)) * (B_DK ** -0.5)
    kb = _l2norm(kb.reshape(Bn, T, B_HEADS, B_DK))
    vb = vb.reshape(Bn, T, B_HEADS, B_DV)
    beta = jax.nn.sigmoid(bb.astype(F32))
    g = -jnp.exp(a_log.astype(F32)) * jax.nn.softplus(ab.astype(F32) + dt_bias.astype(F32))
    ob, s_new = _gated_delta_chunked(qb, kb, vb, beta, g, s0, lead)
    ob = _rmsnorm(ob.astype(x.dtype), onorm) * jax.nn.silu(zb.reshape(Bn, T, B_HEADS, B_DV))
    oa = oa.reshape(Bn, T, A_WIDTH) * jax.nn.silu(za)
    mix = jnp.concatenate([oa, ob.reshape(Bn, T, B_VAL)], axis=-1)
    y = x + jnp.einsum('bte,ed->btd', mix, w_out)
    return y, (new_k, new_v, new_conv, s_new.astype(x.dtype))


def _c_layer(x, norm_g, w_in, w_gk_up, b_gk, onorm, w_out, s0, lead):
    Bn, T, _ = x.shape
    h = _rmsnorm(x, norm_g)
    proj = jnp.einsum('btd,de->bte', h, w_in)
    qc, kc, vc, zc, gk_low = _split(proj, C_SIZES)
    log_a = jax.nn.log_sigmoid(jnp.einsum('btr,rk->btk', gk_low.astype(F32), w_gk_up.astype(F32)) + b_gk.astype(F32)) / C_GATE_NORM
    q = qc.reshape(Bn, T, C_HEADS, C_DK) * (C_DK ** -0.5)
    k = kc.reshape(Bn, T, C_HEADS, C_DK)
    v = vc.reshape(Bn, T, C_HEADS, C_DV)
    if s0 is None:
        s0 = jnp.zeros((Bn, C_HEADS, C_DK, C_DV), F32)
    o, s_new = _gla_chunked(q, k, v, log_a.reshape(Bn, T, C_HEADS, C_DK), s0, lead)
    o = _rmsnorm(o.astype(x.dtype), onorm) * jax.nn.silu(zc.reshape(Bn, T, C_HEADS, C_DV))
    y = x + jnp.einsum('bte,ed->btd', o.reshape(Bn, T, C_VAL), w_out)
    return y, s_new.astype(x.dtype)


def setup_inputs(seed: int = 0) -> dict:
    key = jax.random.key(seed)
    ks = jax.random.split(key, 24)

    def nrm(k, shape, scale):
        return jax.random.normal(k, shape, F32) * scale

    dt = jnp.exp(jax.random.uniform(ks[13], (N_AB_LAYERS, B_HEADS), F32, math.log(1e-3), math.log(1e-1)))
    return {
        'x_prompt': nrm(ks[0], (BATCH, SEQ, D_MODEL), 1.0),
        'x_sample': nrm(ks[1], (DEC_BATCH, DEC_SEQ, D_MODEL), 1.0),
        'cache_swa_k': nrm(ks[2], (N_AB_LAYERS, DEC_BATCH, WIN_BUF, A_KV_HEADS, A_HEAD_DIM), 1.0),
        'cache_swa_v': nrm(ks[3], (N_AB_LAYERS, DEC_BATCH, WIN_BUF, A_KV_HEADS, A_HEAD_DIM), 1.0),
        'state_dn_conv': nrm(ks[4], (N_AB_LAYERS, DEC_BATCH, CONV_W - 1, B_CONV_CH), 1.0),
        'state_dn': nrm(ks[5], (N_AB_LAYERS, DEC_BATCH, B_HEADS, B_DK, B_DV), 0.1),
        'state_gla': nrm(ks[6], (N_C_LAYERS, DEC_BATCH, C_HEADS, C_DK, C_DV), 1.0),
        'meta_tokens': nrm(ks[7], (N_META, D_MODEL), 1.0),
        'norm_ab': 1.0 + nrm(ks[8], (N_AB_LAYERS, D_MODEL), 0.05),
        'w_in_ab': nrm(ks[9], (N_AB_LAYERS, D_MODEL, AB_IN), D_MODEL ** -0.5),
        'sink_a': nrm(ks[10], (N_AB_LAYERS, A_HEADS), 0.5),
        'conv_b': nrm(ks[11], (N_AB_LAYERS, CONV_W, B_CONV_CH), CONV_W ** -0.5),
        'a_log_b': jnp.log(jax.random.uniform(ks[12], (N_AB_LAYERS, B_HEADS), F32, 1.0, 16.0)),
        'dt_bias_b': dt + jnp.log(-jnp.expm1(-dt)),
        'onorm_b': 1.0 + nrm(ks[14], (N_AB_LAYERS, B_DV), 0.05),
        'w_out_ab': nrm(ks[15], (N_AB_LAYERS, AB_MIX, D_MODEL), AB_MIX ** -0.5),
        'norm_c': 1.0 + nrm(ks[16], (N_C_LAYERS, D_MODEL), 0.05),
        'w_in_c': nrm(ks[17], (N_C_LAYERS, D_MODEL, C_IN), D_MODEL ** -0.5),
        'w_gk_up': nrm(ks[18], (N_C_LAYERS, C_GATE_RANK, C_KEY), C_GATE_RANK ** -0.5),
        'b_gk': nrm(ks[19], (N_C_LAYERS, C_KEY), 0.1),
        'onorm_c': 1.0 + nrm(ks[20], (N_C_LAYERS, C_DV), 0.05),
        'w_out_c': nrm(ks[21], (N_C_LAYERS, C_VAL, D_MODEL), C_VAL ** -0.5),
        'final_norm': 1.0 + nrm(ks[22], (D_MODEL,), 0.05),
    }


def reference(x_prompt, x_sample, cache_swa_k, cache_swa_v, state_dn_conv, state_dn, state_gla,
              meta_tokens, norm_ab, w_in_ab, sink_a, conv_b, a_log_b, dt_bias_b, onorm_b, w_out_ab,
              norm_c, w_in_c, w_gk_up, b_gk, onorm_c, w_out_c, final_norm):
    Bp = x_prompt.shape[0]
    meta = jnp.broadcast_to(meta_tokens.astype(x_prompt.dtype)[None], (Bp, N_META, D_MODEL))
    hp = jnp.concatenate([meta, x_prompt], axis=1)
    hs = x_sample
    pk, pv, pconv, pdn, pgla = [], [], [], [], []
    sk, sv, sconv, sdn, sgla = [], [], [], [], []
    for layer in range(DEPTH):
        i = layer // 2
        if layer % 2 == 0:
            hp, (k1, v1, c1, d1) = _ab_layer(hp, norm_ab[i], w_in_ab[i], sink_a[i], conv_b[i], a_log_b[i], dt_bias_b[i],
                                            onorm_b[i], w_out_ab[i], None, None, None, None, LEAD_PROMPT)
            hs, (k2, v2, c2, d2) = _ab_layer(hs, norm_ab[i], w_in_ab[i], sink_a[i], conv_b[i], a_log_b[i], dt_bias_b[i],
                                            onorm_b[i], w_out_ab[i], cache_swa_k[i], cache_swa_v[i],
                                            state_dn_conv[i], state_dn[i], 0)
            pk.append(k1); pv.append(v1); pconv.append(c1); pdn.append(d1)
            sk.append(k2); sv.append(v2); sconv.append(c2); sdn.append(d2)
        else:
            hp, g1 = _c_layer(hp, norm_c[i], w_in_c[i], w_gk_up[i], b_gk[i], onorm_c[i], w_out_c[i], None, LEAD_PROMPT)
            hs, g2 = _c_layer(hs, norm_c[i], w_in_c[i], w_gk_up[i], b_gk[i], onorm_c[i], w_out_c[i], state_gla[i], 0)
            pgla.append(g1)
            sgla.append(g2)
    y_prompt = _rmsnorm(hp, final_norm)[:, N_META:]
    y_sample = _rmsnorm(hs, final_norm)
    return (y_prompt, y_sample,
            jnp.stack(pk), jnp.stack(pv), jnp.stack(pconv), jnp.stack(pdn), jnp.stack(pgla),
            jnp.stack(sk), jnp.stack(sv), jnp.stack(sconv), jnp.stack(sdn), jnp.stack(sgla))
```

```python
import os
import numpy as np
from contextlib import ExitStack
import concourse.bass as bass
import concourse.mybir as mybir
from concourse.bass_utils import run_bass_kernel_spmd

F32 = mybir.dt.float32
BF16 = mybir.dt.bfloat16
AF = mybir.ActivationFunctionType
ALU = mybir.AluOpType
AX = mybir.AxisListType

PE, ACT, DVE, POOL, SP = "tensor", "scalar", "vector", "gpsimd", "sync"
COMPUTE = (PE, ACT, DVE, POOL)
SEM_LIM = 30000
N_DMA_SEMS = 16

D = 1024
NT_FULL = 65
SB_ = 16
NEG = -30000.0
HP = [0, 4, 1, 5, 2, 6, 3, 7]


class Prog:
    def __init__(self):
        self.ops = []
        self.cur = self.ops

    def begin(self):
        self.cur = []
        return self.cur

    def end(self):
        self.cur = self.ops

    def interleave(self, streams):
        its = [list(x) for x in streams if x]
        pos = [0] * len(its)
        left = sum(len(x) for x in its)
        while left:
            for k, x in enumerate(its):
                if pos[k] < len(x):
                    self.ops.append(x[pos[k]]); pos[k] += 1; left -= 1

    def op(self, eng, fn, reads=(), writes=()):
        import sys
        f = sys._getframe(1)
        self.cur.append(dict(eng=eng, fn=fn, reads=tuple(reads), writes=tuple(writes), dma=False, line=f.f_lineno))

    def dma(self, eng, fn, reads=(), writes=()):
        self.cur.append(dict(eng=eng, fn=fn, reads=tuple(reads), writes=tuple(writes), dma=True))

    def emit(self, nc, stack):
        import os
        ops = self.ops
        mx = int(os.environ.get('KMAXOPS', '0'))
        if mx:
            ops = ops[:mx]
        n = len(ops)
        print('n_ops', n)
        WHOLE = (0, 1 << 30, 0, 1 << 30)

        def norm(k):
            if isinstance(k, str):
                return (k,) + WHOLE
            return k

        def overlap(r1, r2):
            return r1[0] < r2[1] and r2[0] < r1[1] and r1[2] < r2[3] and r2[2] < r1[3]

        def covers(big, small):
            return big[0] <= small[0] and big[1] >= small[1] and big[2] <= small[2] and big[3] >= small[3]

        def analyze(ops, dedup):
            n = len(ops)
            recs = {}
            full = [None] * n
            deps = [None] * n
            needed = [False] * n
            for i, o in enumerate(ops):
                acc = []
                for k in o["reads"]:
                    k = norm(k)
                    if k[0].startswith("p_"):
                        acc.append((k[0], WHOLE, True))
                    else:
                        acc.append((k[0], k[1:], False))
                for k in o["writes"]:
                    k = norm(k)
                    acc.append((k[0], WHOLE if k[0].startswith("p_") else k[1:], True))
                d = set()
                for name, rect, isw in acc:
                    for r in recs.get(name, ()):
                        if (isw or r[2]) and overlap(rect, r[0]):
                            d.add(r[1])
                d.discard(i)
                full[i] = d
                best, dl = {}, []
                for j in d:
                    oj = ops[j]
                    if oj["dma"]:
                        dl.append(j)
                    else:
                        e = oj["eng"]
                        if e == PE and o["eng"] == PE and not o["dma"]:
                            continue
                        if e not in best or best[e] < j:
                            best[e] = j
                dl.extend(best.values())
                deps[i] = dl
                for j in dl:
                    needed[j] = True
                for name, rect, isw in acc:
                    lst = recs.setdefault(name, [])
                    if isw:
                        lst[:] = [r for r in lst if not covers(rect, r[0])]
                    elif dedup and not o["dma"]:
                        lst[:] = [r for r in lst if not (not r[2] and r[0] == rect and not ops[r[1]]["dma"]
                                                         and ops[r[1]]["eng"] == o["eng"])]
                    lst.append([rect, i, isw])
            return full, deps, needed

        if os.environ.get("KSCHED", "1") == "1":
            import heapq
            full, _, _ = analyze(ops, False)
            dur = [0.0] * n
            for i, o in enumerate(ops):
                ext = 512
                if o["writes"] and not isinstance(o["writes"][0], str):
                    w0 = o["writes"][0]
                    ext = (w0[4] - w0[3]) // 4
                e = o["eng"]
                if o["dma"]:
                    dur[i] = 2.5
                elif e == PE:
                    dur[i] = (0.11 + 0.0005 * ext) * o.get('cmul', 1.0)
                elif e == ACT:
                    dur[i] = 0.30 + 0.00085 * ext
                elif e == DVE:
                    dur[i] = 0.25 + 0.0011 * ext
                else:
                    dur[i] = 0.35 + 0.002 * ext
            succ = [[] for _ in range(n)]
            indeg = [0] * n
            for i in range(n):
                for j in full[i]:
                    succ[j].append(i)
                indeg[i] = len(full[i])
            cpl = [0.0] * n
            for i in range(n - 1, -1, -1):
                m = 0.0
                for k in succ[i]:
                    if cpl[k] > m:
                        m = cpl[k]
                cpl[i] = dur[i] + m
            engs = (PE, ACT, DVE, POOL, SP)
            fut = {e: [] for e in engs}
            av = {e: [] for e in engs}
            efree = {e: 0.0 for e in engs}
            ready_t = [0.0] * n
            avail_t = [0.0] * n
            for i in range(n):
                if indeg[i] == 0:
                    heapq.heappush(fut[ops[i]["eng"]], (0.0, i))
            order = []
            done = 0
            while done < n:
                bs, be, bi = None, None, None
                for e in engs:
                    f_, a_ = fut[e], av[e]
                    while f_ and f_[0][0] <= efree[e]:
                        rt, i = heapq.heappop(f_)
                        heapq.heappush(a_, (-cpl[i], i))
                    if a_:
                        cs = efree[e]
                    elif f_:
                        cs = f_[0][0]
                    else:
                        continue
                    if bs is None or cs < bs:
                        bs, be = cs, e
                e = be
                if av[e]:
                    _, i = heapq.heappop(av[e])
                else:
                    _, i = heapq.heappop(fut[e])
                st_ = bs
                o = ops[i]
                if o["dma"]:
                    efree[e] = st_ + 0.15
                    avail_t[i] = st_ + dur[i]
                else:
                    efree[e] = st_ + dur[i]
                    avail_t[i] = st_ + dur[i]
                order.append((st_, done, i))
                done += 1
                for k in succ[i]:
                    lat = 0.1 if (ops[k]["eng"] == e and not o["dma"]) else 0.15
                    t_ = avail_t[i] + lat
                    if t_ > ready_t[k]:
                        ready_t[k] = t_
                    indeg[k] -= 1
                    if indeg[k] == 0:
                        heapq.heappush(fut[ops[k]["eng"]], (ready_t[k], k))
            order.sort()
            ops_o = ops
            ops = [ops[i] for _, _, i in order]
            print("sched makespan est (us):", max(avail_t), "crit path:", max(cpl))
            _ld = {}
            for i, o in enumerate(self.ops if not mx else ops):
                pass
            for (st_, _, i) in order:
                pass
            for e in engs:
                print("  load", e, sum((0.15 if ops_o[i]["dma"] else dur[i]) for i in range(n) if ops_o[i]["eng"] == e))
        _, deps, needed = analyze(ops, True)
        eng_sems = {e: [] for e in COMPUTE}
        eng_cnt = {e: 0 for e in COMPUTE}
        qs_ = (SP, POOL, ACT)
        dma_sems = {q: [stack.enter_context(nc.semaphore("dma_%s%d" % (q, i))) for i in range(N_DMA_SEMS)] for q in qs_}
        dma_tot = {q: [0] * N_DMA_SEMS for q in qs_}
        dma_last = {q: [None] * N_DMA_SEMS for q in qs_}
        dma_rr = {q: 0 for q in qs_}
        tag = [None] * n
        waited = {e: {} for e in (PE, ACT, DVE, POOL, SP)}
        handles = {PE: nc.tensor, ACT: nc.scalar, DVE: nc.vector, POOL: nc.gpsimd, SP: nc.sync}

        import os as _os
        n_warm = int(_os.environ.get("KWARM", "0"))
        warm_ap = getattr(self, "warm_ap", None)

        def do_wait(e, sv):
            sem, val = sv
            key = id(sem)
            w = waited[e]
            if w.get(key, 0) >= val:
                return
            w[key] = val
            if e == PE and n_warm and warm_ap is not None:
                for _ in range(n_warm):
                    nc.tensor.ldweights(warm_ap)
            handles[e].wait_ge(sem, val)

        for i, o in enumerate(ops):
            e = o["eng"]
            for j in deps[i]:
                do_wait(e, tag[j])
            if o["dma"]:
                s = dma_rr[e]
                dma_rr[e] = (s + 1) % N_DMA_SEMS
                if dma_last[e][s] is not None:
                    do_wait(e, tag[dma_last[e][s]])
                ins = o["fn"](handles[e])
                dma_tot[e][s] += 16
                ins.then_inc(dma_sems[e][s], 16)
                tag[i] = (dma_sems[e][s], dma_tot[e][s])
                dma_last[e][s] = i
            else:
                ins = o["fn"](handles[e])
                if needed[i]:
                    c = eng_cnt[e]
                    si = c // SEM_LIM
                    while len(eng_sems[e]) <= si:
                        eng_sems[e].append(stack.enter_context(nc.semaphore("%s_p%d" % (e, len(eng_sems[e])))))
                    ins.then_inc(eng_sems[e][si], 1)
                    eng_cnt[e] = c + 1
                    tag[i] = (eng_sems[e][si], c % SEM_LIM + 1)
        for q in qs_:
            for s in range(N_DMA_SEMS):
                if dma_last[q][s] is not None:
                    do_wait(SP, tag[dma_last[q][s]])
        return n


def build_nc(NT=NT_FULL):
    NTOK = NT * 128
    nc = bass.Bass("TRN2", target_bir_lowering=False)
    P = Prog()

    def din(name, shape):
        return nc.dram_tensor(name, list(shape), F32, kind="ExternalInput").ap()

    def dout(name, shape):
        return nc.dram_tensor(name, list(shape), F32, kind="ExternalOutput").ap()

    xp = din("xp", [NTOK, D]); valid = din("valid", [NTOK, 1]); xs = din("xs", [SB_, D])
    ck = din("ck", [SB_, 128, 128]); cv = din("cv", [SB_, 128, 128]); cconv = din("cconv", [SB_, 3, 1536])
    sdn = din("sdn", [SB_, 4, 128, 128]); sgla = din("sgla", [SB_, 4, 128, 256])
    w0t_d = din("w0t", [D, 1288]); w0f_d = din("w0f", [D, 2176]);
    w0o_d = din("w0o", [D, D])
    w1t_d = din("w1t", [D, 2560]); w1f_d = din("w1f", [D, 1040]); w1o_d = din("w1o", [D, D])
    wgk_d = din("wgk", [17, 512])
    g0_d = din("g0", [D, 1]); g1_d = din("g1", [D, 1]); fn_d = din("fn", [1, D])
    sink_d = din("sink", [1, 8]); convw_d = din("convw", [4, 1536]); alog_d = din("alog", [1, 4]); dtb_d = din("dtb", [1, 4])
    onb_d = din("onb", [1, 128]); onc_d = din("onc", [1, 256])
    cmask_d = din("cmask", [4, 128, 128])

    yp = dout("yp", [NTOK, D]); ys = dout("ys", [SB_, D])
    pk = dout("pk", [128, 128]); pv = dout("pv", [128, 128]); pconv = dout("pconv", [3, 1536])
    pdn = dout("pdn", [4, 128, 128]); pgla = dout("pgla", [4, 128, 256])
    sk = dout("sk", [SB_, 128, 128]); sv = dout("sv", [SB_, 128, 128]); sconv = dout("sconv", [SB_, 3, 1536])
    sdn_o = dout("sdn_o", [SB_, 4, 128, 128]); sgla_o = dout("sgla_o", [SB_, 4, 128, 256])

    with ExitStack() as st:
        def sb(name, shape, dt=F32):
            return st.enter_context(nc.sbuf_tensor("s_" + name, list(shape), dt))

        def psb(name, shape, dt=F32):
            return st.enter_context(nc.psum_tensor("p_" + name, list(shape), dt))

        def names(*aps):
            out = []
            for a in aps:
                if a is None or isinstance(a, (int, float)):
                    continue
                try:
                    apl = a.ap
                    ps, pc = apl[0]
                    off = a.offset
                    if ps <= 0:
                        raise ValueError
                    plo = off // ps
                    flo = off % ps
                    ext = 1
                    for st_, cn in apl[1:]:
                        ext += (cn - 1) * abs(st_)
                    if flo + ext > ps:
                        raise ValueError
                    esz = 2 if a.dtype == BF16 else 4
                    out.append((a.name, plo, plo + pc, flo * esz, (flo + ext) * esz))
                except Exception:
                    out.append(a.name)
            return out

        def mm(out, lhsT, rhs, start=True, stop=True):
            P.op(PE, lambda e: e.matmul(out, lhsT=lhsT, rhs=rhs, start=start, stop=stop),
                 reads=names(lhsT, rhs), writes=names(out))
            if lhsT.dtype == F32:
                P.cur[-1]["cmul"] = 4.0

        def tr(out, in_, ident):
            P.op(PE, lambda e: e.transpose(out=out, in_=in_, identity=ident), reads=names(in_, ident), writes=names(out))

        def act(out, in_, func, bias=None, scale=None, accum=None, eng=ACT):
            kw = {}
            if bias is not None:
                kw["bias"] = bias
            if scale is not None:
                kw["scale"] = scale
            if accum is not None:
                kw["accum_out"] = accum
            P.op(eng, lambda e: e.activation(out=out, in_=in_, func=func, **kw),
                 reads=names(in_, bias, scale), writes=names(out, accum))

        def tt(out, in0, in1, op, eng=DVE):
            P.op(eng, lambda e: e.tensor_tensor(out=out, in0=in0, in1=in1, op=op), reads=names(in0, in1), writes=names(out))

        def ts(out, in0, s1, op0, s2=None, op1=None, eng=DVE, accum=None):
            kw = {}
            if op1 is not None:
                kw["op1"] = op1
            if accum is not None:
                kw["accum_out"] = accum
            P.op(eng, lambda e: e.tensor_scalar(out=out, in0=in0, scalar1=s1, scalar2=s2, op0=op0, **kw),
                 reads=names(in0, s1, s2), writes=names(out, accum))

        def stt(out, in0, scalar, in1, op0, op1, eng=DVE):
            P.op(eng, lambda e: e.scalar_tensor_tensor(out=out, in0=in0, scalar=scalar, in1=in1, op0=op0, op1=op1),
                 reads=names(in0, scalar, in1), writes=names(out))

        def cp(out, in_, eng=DVE):
            if eng == ACT:
                P.op(ACT, lambda e: e.copy(out=out, in_=in_), reads=names(in_), writes=names(out))
            else:
                P.op(eng, lambda e: e.tensor_copy(out=out, in_=in_), reads=names(in_), writes=names(out))

        def memset(ap, v, eng=POOL):
            P.op(eng, lambda e: e.memset(ap, v), writes=names(ap))

        def asel(out, in_, pattern, cmp, fill, base, cm):
            P.op(POOL, lambda e: e.affine_select(out=out, in_=in_, pattern=pattern, compare_op=cmp, fill=fill,
                                                 base=base, channel_multiplier=cm), reads=names(in_), writes=names(out))

        def rmax(out, in_):
            P.op(DVE, lambda e: e.reduce_max(out=out, in_=in_, axis=AX.X), reads=names(in_), writes=names(out))

        def recip(out, in_):
            P.op(DVE, lambda e: e.reciprocal(out=out, in_=in_), reads=names(in_), writes=names(out))

        def dma(out, in_, eng=SP):
            P.dma(eng, lambda e: e.dma_start(out=out, in_=in_, allow_slow_non_contiguous=True), reads=names(in_), writes=names(out))

        def rstd_from_ss(out, ss, inv_n, eps=1e-6):
            act(out, ss, AF.Ln, bias=epsc[0:out.shape[0], :], scale=inv_n)
            act(out, out, AF.Exp, scale=-0.5)

        epsc = sb("epsc", [128, 1]); memset(epsc[:], 1e-6)
        onec = sb("onec", [128, 1]); memset(onec[:], 1.0)
        ones_f = sb("ones_f", [128, 128]); memset(ones_f[:], 1.0)
        zeros_f = sb("zeros_f", [128, 256]); memset(zeros_f[:], 0.0)
        ones_b = sb("ones_b", [128, 128], BF16); cp(ones_b[:], ones_f[:])
        ident_f = sb("ident_f", [128, 128])
        asel(ident_f[:], ones_f[:], [[-1, 128]], ALU.is_equal, 0.0, 0, 1)
        ident_b = sb("ident_b", [128, 128], BF16); cp(ident_b[:], ident_f[:])
        triu_f = sb("triu_f", [128, 128])
        asel(triu_f[:], ones_f[:], [[1, 128]], ALU.is_ge, 0.0, 0, -1)
        nm_ls = sb("nm_ls", [128, 128])
        asel(nm_ls[:], zeros_f[:, 0:128], [[-1, 128]], ALU.is_gt, NEG, 0, 1)
        nm_ui = sb("nm_ui", [128, 128])
        asel(nm_ui[:], zeros_f[:, 0:128], [[1, 128]], ALU.is_ge, NEG, 0, -1)
        band = sb("band", [128, 256])
        asel(band[:], zeros_f[:], [[1, 256]], ALU.is_ge, NEG, -1, -1)
        asel(band[:], band[:], [[-1, 256]], ALU.is_ge, NEG, 128, 1)
        band0 = sb("band0", [128, 256])
        cp(band0[:], band[:], eng=POOL)
        memset(band0[:, 0:128], NEG)

        sink_bc = sb("sink_bc", [128, 8]); dma(sink_bc[:], sink_d.partition_broadcast(128))
        alog_bc = sb("alog_bc", [128, 4]); dma(alog_bc[:], alog_d.partition_broadcast(128))
        dtb_bc = sb("dtb_bc", [128, 4]); dma(dtb_bc[:], dtb_d.partition_broadcast(128))
        onb_bc = sb("onb_bc", [128, 128]); dma(onb_bc[:], onb_d.partition_broadcast(128))
        onc_bc = sb("onc_bc", [128, 256]); dma(onc_bc[:], onc_d.partition_broadcast(128))
        fn_bc = sb("fn_bc", [128, D]); dma(fn_bc[:], fn_d.partition_broadcast(128))
        negA = sb("negA", [128, 4])
        act(negA[:], alog_bc[:], AF.Exp)
        ts(negA[:], negA[:], -1.0, ALU.mult)
        convw = sb("convw", [128, 4, 12])
        for i in range(4):
            P.dma(SP, lambda e, i=i: e.dma_start(out=convw[:, i, :], in_=convw_d[i, :].rearrange("(c p) -> p c", p=128),
                                                 allow_slow_non_contiguous=True), reads=[], writes=names(convw))
        g0 = sb("g0", [128, 8]); dma(g0[:], g0_d.rearrange("(k p) o -> p (k o)", p=128))
        g1 = sb("g1", [128, 8]); dma(g1[:], g1_d.rearrange("(k p) o -> p (k o)", p=128))
        wgk = sb("wgk", [17, 512], BF16)
        wgk_f = sb("wgk_f", [17, 512]); dma(wgk_f[:], wgk_d[:, :]); cp(wgk[:], wgk_f[:])

        xt = [sb("xt%d" % i, [128, D]) for i in range(2)]
        x1t = sb("x1t", [128, D])
        stage = [x1t[:, :].rearrange("p (k n) -> p k n", k=8), xt[1][:, :].rearrange("p (k n) -> p k n", k=8)]
        stg_i = [0]

        def load_w(dst, src, ncols, g):
            for c0 in range(0, ncols, 128):
                cw = min(128, ncols - c0)
                s = stage[stg_i[0] % 2]; stg_i[0] += 1
                dma(s[:, :, 0:cw], src[:, c0:c0 + cw].rearrange("(k p) n -> p k n", p=128))
                for k in range(8):
                    e_ = (DVE, ACT, DVE, POOL)[(k + stg_i[0]) % 4]
                    if g is None:
                        cp(dst[:, k, c0:c0 + cw], s[:, k, 0:cw], eng=e_)
                    elif e_ == ACT:
                        act(dst[:, k, c0:c0 + cw], s[:, k, 0:cw], AF.Copy, scale=g[:, k:k + 1])
                    else:
                        ts(dst[:, k, c0:c0 + cw], s[:, k, 0:cw], g[:, k:k + 1], ALU.mult, eng=e_)

        Wall = sb("Wall", [128, 8, 4624], BF16)
        W0t, W0f, W0o = Wall[:, :, 0:1288], Wall[:, :, 1288:3464], Wall[:, :, 3464:4488]
        W1t, W1f, W1o = Wall[:, :, 0:2560], Wall[:, :, 2560:3600], Wall[:, :, 3600:4624]
        load_w(W0t, w0t_d, 1288, g0); load_w(W0f, w0f_d, 2176, g0); load_w(W0o, w0o_d, D, None)
        x1d = nc.dram_tensor("x1d", [NTOK, D], F32).ap()

        pTb = [psb("pTb%d" % i, [128, 8, 128], BF16) for i in range(2)]
        pA = [psb("pA%d" % i, [128, 512]) for i in range(6)]

        ss = sb("ss", [128, 1]); rstd = sb("rstd", [128, 1])
        hb = sb("hb", [128, D], BF16)
        junk = hb
        hT = sb("hT", [128, 8, 128], BF16)
        mix = sb("mix", [128, D], BF16)
        mixT = sb("mixT", [128, 8, 128], BF16)
        vld = sb("vld", [128, 1])

        def norm_and_transpose(xin, rows):
            act(junk[0:rows, :], xin, AF.Square, accum=ss[0:rows, :])
            rstd_from_ss(rstd[0:rows, :], ss[0:rows, :], 1.0 / D)
            ts(hb[0:rows, :], xin, rstd[0:rows, :], ALU.mult)
            for k in range(8):
                tr(pTb[0][:, k, 0:rows], hb[0:rows, k * 128:(k + 1) * 128], ident_b[0:rows, 0:rows])
            cp(hT[:, :, 0:rows], pTb[0][:, :, 0:rows], eng=ACT)

        def out_proj(Wo, xres, xout, rows, banks=None):
            banks = banks or (pA[0], pA[1])
            for k in range(8):
                tr(pTb[1][:, k, 0:rows], mix[0:rows, k * 128:(k + 1) * 128], ident_b[0:rows, 0:rows])
            cp(mixT[:, :, 0:rows], pTb[1][:, :, 0:rows], eng=ACT)
            for n in range(2):
                for k in range(8):
                    mm(banks[n][0:rows, :], mixT[:, k, 0:rows], Wo[:, k, n * 512:(n + 1) * 512], start=(k == 0), stop=(k == 7))
                tt(xout[:, n * 512:(n + 1) * 512], banks[n][0:rows, :], xres[:, n * 512:(n + 1) * 512], ALU.add)

        sza2 = [sb("sza%d" % i, [128, 512], BF16) for i in range(2)]; szb2 = [sb("szb%d" % i, [128, 512], BF16) for i in range(2)]
        kvg2 = [sb("kvg%d" % i, [128, 264]) for i in range(2)]
        vtok = [sb("vtok%d" % i, [128, 128], BF16) for i in range(3)]
        kTa = [sb("kTa%d" % i, [128, 128], BF16) for i in range(3)]
        qTa2 = [sb("qTa%d" % i, [128, 4, 128], BF16) for i in range(2)]
        xbT2 = [sb("xbT%d" % i, [128, 12, 131]) for i in range(2)]
        sza, szb, kvg, qTa, xbT = sza2[0], szb2[0], kvg2[0], qTa2[0], xbT2[0]
        cacc = sb("cacc", [128, 12, 128])
        sq = sb("sq", [128, 8, 128], BF16)
        rn = sb("rn", [128, 8, 128])
        qTn = sb("qTn", [128, 4, 128], BF16); kTn = sb("kTn", [128, 4, 128], BF16)
        gval = sb("gval", [128, 4]); beta = sb("beta", [128, 4]); gtmp = sb("gtmp", [128, 4])
        gc = sb("gc", [128, 4]); ngc = sb("ngc", [128, 4]); egl = sb("egl", [128, 4]); edk = sb("edk", [128, 4])
        bdk = sb("bdk", [128, 4])
        gh1 = sb("gh1", [128, 4, 128])
        W4 = lambda nm_, dt_=BF16: sb(nm_, [128, 4, 128], dt_)
        Dm4 = W4("Dm4", F32); DTm4 = W4("DTm4", F32); egcb4 = W4("egcb4", F32)
        Nfull4 = W4("Nfull4"); NnW = [W4("NnW0"), W4("NnW1")]; NtW = [W4("NtW0"), W4("NtW1")]; RrW = [W4("RrW0"), W4("RrW1")]
        Lm4 = W4("Lm4"); Xm4 = W4("Xm4"); Tm4 = W4("Tm4"); kbd4 = W4("kbd4"); kdd4 = W4("kdd4"); vbeta4 = W4("vbeta4")
        nwT4 = W4("nwT4"); vnew4 = W4("vnew4"); qdT4 = W4("qdT4"); QKm4 = W4("QKm4")
        dtmp = Dm4[:, 0, :]; Tm = Tm4[:, 0, :]; Xm = Xm4[:, 0, :]; Nfull = Nfull4[:, 0, :]; vbeta = vbeta4[:, 0, :]
        dma(Dm4[:], cmask_d.rearrange("m p x -> p m x"))
        cmask_b = sb("cmask_b", [128, 4, 128], BF16); cp(cmask_b[:], Dm4[:])
        Sdn = sb("Sdn", [128, 4, 128]); Sdnb = sb("Sdnb", [128, 4, 128], BF16)
        memset(Sdn[:], 0.0); memset(Sdnb[:], 0.0)
        osb = sb("osb", [128, 512])
        oss = sb("oss", [128, 4]); orn = sb("orn", [128, 4])
        oss8 = sb("oss8", [16, 8]); orn8 = sb("orn8", [16, 8]); kq = sb("kq", [16, 4])
        otmp = sb("otmp", [128, 256])
        s_sb = sb("s_sb", [128, 256]); p_sb = sb("p_sb", [128, 256], BF16); pT_sb = sb("pT_sb", [128, 2, 128], BF16)
        mrow = sb("mrow", [128, 1]); nmrow = sb("nmrow", [128, 1]); rsum = sb("rsum", [128, 1]); esk = sb("esk", [128, 1])
        rden = sb("rden", [128, 1])
        kd1 = sb("kd1", [128, 512], BF16); v1 = sb("v1", [128, D], BF16); sz1 = sb("sz1", [128, D], BF16)
        gk_aug = sb("gk_aug", [32, 128], BF16); memset(gk_aug[:], 1.0)
        la = sb("la", [128, 512]); sp1 = sb("sp1", [128, 512])
        bc_sb = sb("bc_sb", [128, 512]); edec = sb("edec", [128, 512])
        ebT = sb("ebT", [128, 4, 128]); einvT = sb("einvT", [128, 4, 128])
        qdT1 = sb("qdT1", [128, 4, 128], BF16); kinvT1 = sb("kinvT1", [128, 4, 128], BF16)
        QKm1 = sb("QKm1", [128, 128], BF16)
        Sg = sb("Sg", [128, 4, 256]); Sgb = sb("Sgb", [128, 4, 256], BF16)
        memset(Sg[:], 0.0); memset(Sgb[:], 0.0)
        o1 = cacc[:, 0:8, :].rearrange("p c t -> p (c t)")
        x2t = o1; yt = x1t

        def gated_rms(o_ap, width, nheads, onorm_bc, sz, mix_off, R=128):
            for h in range(nheads):
                act(otmp[0:R, 0:width], o_ap[0:R, h * width:(h + 1) * width], AF.Square, accum=oss[0:R, h:h + 1])
            rstd_from_ss(orn[0:R, 0:nheads], oss[0:R, 0:nheads], 1.0 / width)
            for h in range(nheads):
                stt(otmp[0:R, 0:width], o_ap[0:R, h * width:(h + 1) * width], orn[0:R, h:h + 1], onorm_bc[0:R, 0:width], ALU.mult, ALU.mult)
                tt(mix[0:R, mix_off + h * width:mix_off + (h + 1) * width], otmp[0:R, 0:width], sz[0:R, h * width:(h + 1) * width], ALU.mult)

        R = SB_
        kdd_s = sz1[0:R, 0:512]; dltf = xbT[0:R, :, :].rearrange("p c t -> p (c t)")[:, 0:512]
        xs1d = nc.dram_tensor("xs1d", [R, D], F32).ap()
        osc = nc.dram_tensor("osc", [8, R, 64], F32).ap()
        eyeb4 = sq[:, :, :].rearrange("p a b -> p (a b)").rearrange("p (b h t) -> p b h t", b=16, h=4)
        memset(sq[:], 0.0)
        for b in range(R):
            memset(eyeb4[:, b, :, b:b + 1], 1.0)
        sinkc = sb("sinkc", [8, 1])
        sink_hc = sink_d.rearrange("o (c half) -> half c o", half=2)
        dma(sinkc[0:4, :], sink_hc[0]); dma(sinkc[4:8, :], sink_hc[1])
        evm = sb("evm", [8, 2])
        memset(evm[:], 1.0)
        asel(evm[:, 0:1], evm[:, 0:1], [[0, 1]], ALU.is_ge, 0.0, 3, -1)
        asel(evm[:, 1:2], evm[:, 1:2], [[0, 1]], ALU.is_ge, 0.0, -4, 1)
        xsb = xt[0][0:R, :]
        dma(xsb, xs[:, :])
        norm_and_transpose(xsb, R)
        qs = la; xbs = cacc[0:R, :, :].rearrange("p c t -> p (c t)")
        for n, (c0, cw) in enumerate([(0, 512), (512, 512), (1024, 264)]):
            for k in range(8):
                mm(pA[n][0:R, 0:cw], hT[:, k, 0:R], W0t[:, k, c0:c0 + cw], start=(k == 0), stop=(k == 7))
        act(sza[0:R, :], pA[0][0:R, :], AF.Silu)
        act(szb[0:R, :], pA[1][0:R, :], AF.Silu)
        cp(kvg[0:R, :], pA[2][0:R, 0:264])
        for n, c0 in enumerate([0, 640, 1152, 1664]):
            for k in range(8):
                mm(pA[n][0:R, :], hT[:, k, 0:R], W0f[:, k, c0:c0 + 512], start=(k == 0), stop=(k == 7))
            if n == 0:
                act(qs[0:R, :], pA[0][0:R, :], AF.Copy, scale=0.125)
            else:
                cp(xbs[:, (n - 1) * 512:n * 512], pA[n][0:R, :])
        dma(sk[:, 0:127, :], ck[:, 1:128, :], eng=POOL); dma(sv[:, 0:127, :], cv[:, 1:128, :], eng=POOL)
        dma(sk[:, 127, :], kvg[0:R, 0:128], eng=POOL); dma(sv[:, 127, :], kvg[0:R, 128:256], eng=POOL)
        dma(sconv[:, 0:2, :], cconv[:, 1:3, :], eng=POOL)
        dma(sconv[:, 2, :], xbs, eng=POOL)
        cres = [rn[0:R, 0:4, :].rearrange("p c t -> p (c t)"), rn[0:R, 4:8, :].rearrange("p c t -> p (c t)"), osb[0:R, :]]
        tA, tW = sp1[0:R, :], bc_sb[0:R, :]
        for j in range(3):
            for i in range(4):
                dma(tW, convw_d[i, j * 512:(j + 1) * 512].partition_broadcast(R))
                if i < 3:
                    dma(tA, cconv[:, i, j * 512:(j + 1) * 512])
                    src = tA
                else:
                    src = xbs[:, j * 512:(j + 1) * 512]
                if i == 0:
                    tt(cres[j], src, tW, ALU.mult)
                else:
                    tt(edec[0:R, :], src, tW, ALU.mult)
                    tt(cres[j], cres[j], edec[0:R, :], ALU.add)
            act(cres[j], cres[j], AF.Silu)
        qk3 = rn[0:R, :, :]
        for c in range(8):
            act(otmp[0:R, 0:128], qk3[:, c, :], AF.Square, accum=oss8[0:R, c:c + 1])
        rstd_from_ss(orn8[0:R, :], oss8[0:R, :], 1.0)
        for c in range(8):
            ts(qk3[:, c, :], qk3[:, c, :], orn8[0:R, c:c + 1], ALU.mult, (128.0 ** -0.5 if c < 4 else 1.0), ALU.mult)
        tt(edec[0:R, :], rn[0:R, 0:4, :].rearrange("p c t -> p (c t)"), rn[0:R, 4:8, :].rearrange("p c t -> p (c t)"), ALU.mult)
        P.op(DVE, lambda e: e.reduce_sum(out=kq[0:R, :], in_=edec[0:R, :].rearrange("p (h t) -> p h t", h=4), axis=AX.X),
             reads=names(edec), writes=names(kq))
        for c in range(8):
            P.op(PE, lambda e, c=c: e.transpose(out=pA[5][:, c * 16:(c + 1) * 16], in_=qk3[:, c, :], identity=ident_f[0:R, 0:R]),
                 reads=names(rn, ident_f), writes=names(pA[5]))
        cp(qTn[:, :, 0:R], pA[5][:, 0:64].rearrange("p (c t) -> p c t", c=4))
        cp(kTn[:, :, 0:R], pA[5][:, 64:128].rearrange("p (c t) -> p c t", c=4))
        kTm = mixT[:, :, :].rearrange("p a b -> p (a b)").rearrange("p (b h t) -> p b h t", b=16, h=4)
        qTm = hT[:, :, :].rearrange("p a b -> p (a b)").rearrange("p (b h t) -> p b h t", b=16, h=4)
        for b in range(R):
            tt(kTm[:, b, :, :], kTn[:, :, 0:R], eyeb4[:, b, :, :], ALU.mult)
            tt(qTm[:, b, :, :], qTn[:, :, 0:R], eyeb4[:, b, :, :], ALU.mult, eng=POOL)
        cp(kdd_s, rn[0:R, 4:8, :].rearrange("p c t -> p (c t)"))
        act(beta[0:R, :], kvg[0:R, 256:260], AF.Sigmoid)
        tt(gtmp[0:R, :], kvg[0:R, 260:264], dtb_bc[0:R, :], ALU.add)
        act(gtmp[0:R, :], gtmp[0:R, :], AF.Exp)
        act(gtmp[0:R, :], gtmp[0:R, :], AF.Ln, bias=onec[0:R, :])
        tt(gval[0:R, :], gtmp[0:R, :], negA[0:R, :], ALU.mult)
        act(egl[0:R, :], gval[0:R, :], AF.Exp)
        egd = s_sb[0:R, 0:64]
        for b in range(R):
            ts(egd[:, b * 4:(b + 1) * 4], egl[0:R, :], ident_f[0:R, b:b + 1], ALU.mult)
        mm(pA[5][:, 128:192], ones_f[0:R, :], egd)
        eg_bc = dtmp[:, 0:64]
        cp(eg_bc, pA[5][:, 128:192])
        Sring = [Sdn, gh1]; Sbring = [Sdnb, qTa]
        for b in range(R):
            Sb_, Sbb = Sring[b % 2], Sbring[b % 2]
            dma(Sb_[:], sdn[b].rearrange("h k v -> k h v"))
            cp(Sbb[:], Sb_[:], eng=ACT)
            for h in range(4):
                mm(pA[h][0:R, 0:128], kTm[:, b, h, :], Sbb[:, h, :], start=(b == 0), stop=(b == R - 1))
        dlt = v1[0:R, 512:1024]
        vS = osb[0:R, :]
        for h in range(4):
            stt(otmp[0:R, 0:128], pA[h][0:R, 0:128], egl[0:R, h:h + 1], vS[:, h * 128:(h + 1) * 128], ALU.mult, ALU.subtract)
            ts(dltf[:, h * 128:(h + 1) * 128], otmp[0:R, 0:128], beta[0:R, h:h + 1], ALU.mult, -1.0, ALU.mult)
        cp(dlt, dltf)
        kmr = [kd1[0:R, :], v1[0:R, 0:512]]
        for b in range(R):
            Sb_, Sbb = Sring[b % 2], Sbring[b % 2]
            dma(Sb_[:], sdn[b].rearrange("h k v -> k h v"))
            cp(Sbb[:], Sb_[:], eng=ACT)
            km = kmr[b % 2]
            ts(km, kdd_s, ident_f[0:R, b:b + 1], ALU.mult)
            for h in range(4):
                mm(pA[h][0:R, 0:128], qTm[:, b, h, :], Sbb[:, h, :], start=(b == 0), stop=(b == R - 1))
                up = pA[4 + h % 2]
                mm(up[:, 0:128], km[:, h * 128:(h + 1) * 128], dlt[:, h * 128:(h + 1) * 128])
                stt(Sb_[:, h, :], Sb_[:, h, :], eg_bc[:, b * 4 + h:b * 4 + h + 1], up[:, 0:128], ALU.mult, ALU.add)
            dma(sdn_o[b].rearrange("h k v -> k h v"), Sb_[:], eng=POOL)
        ogs = sp1[0:R, :]
        for h in range(4):
            ts(otmp[0:R, 0:128], dltf[:, h * 128:(h + 1) * 128], kq[0:R, h:h + 1], ALU.mult)
            stt(ogs[:, h * 128:(h + 1) * 128], pA[h][0:R, 0:128], egl[0:R, h:h + 1], otmp[0:R, 0:128], ALU.mult, ALU.add)
        gated_rms(ogs, 128, 4, onb_bc, szb, 512, R=R)
        for c in range(4):
            P.op(PE, lambda e, c=c: e.transpose(out=pA[5][:, c * 16:(c + 1) * 16], in_=qs[0:R, c * 128:(c + 1) * 128], identity=ident_f[0:R, 0:R]),
                 reads=names(la, ident_f), writes=names(pA[5]))
        qblk = Nfull[:, :].rearrange("p (b q) -> p b q", b=16)
        memset(Nfull[:], 0.0)
        for c in range(4):
            for half in range(2):
                hs = slice(half * 64, half * 64 + 64)
                cp(qblk[hs, :, half * 4 + c], pA[5][hs, c * 16:(c + 1) * 16])
        scs = [xt[1][0:8, :].rearrange("p (b t) -> p b t", b=8), rn[0:8, :, :]]
        kst = [cacc[:, 0, :], cacc[:, 1, :]]
        for b in range(R):
            dma(kst[b % 2], sk[b])
            P.op(PE, lambda e, b=b: e.transpose(out=pA[4][:, (b % 2) * 128:(b % 2 + 1) * 128], in_=kst[b % 2], identity=ident_f[:]),
                 reads=names(cacc, ident_f), writes=names(pA[4]))
            cp(Tm[:], pA[4][:, (b % 2) * 128:(b % 2 + 1) * 128])
            mm(pA[3][0:8, 0:128], qblk[:, b, :], Tm[:])
            cp(scs[b // 8][:, b % 8, :], pA[3][0:8, 0:128])
        mx8 = sb("mx8", [8, 16]); nmx8 = sb("nmx8", [8, 16]); rs8 = sb("rs8", [8, 16]); es8 = sb("es8", [8, 16])
        for g in range(2):
            P.op(DVE, lambda e, g=g: e.reduce_max(out=mx8[:, g * 8:(g + 1) * 8], in_=scs[g], axis=AX.X),
                 reads=names(scs[g]), writes=names(mx8))
        ts(nmx8[:], mx8[:], sinkc[:], ALU.max, -1.0, ALU.mult)
        pbs = p_sb[0:8, 0:128]
        ohb = x1t[0:8, :].rearrange("p (b d) -> p b d", b=16)
        for b in range(R):
            act(pbs, scs[b // 8][:, b % 8, :], AF.Exp, bias=nmx8[:, b:b + 1], accum=rs8[:, b:b + 1])
            tr(pTb[1][:, 0, 0:8], pbs, ident_b[0:8, 0:8])
            cp(pT_sb[:, 0, 0:8], pTb[1][:, 0, 0:8], eng=ACT)
            dma(kst[b % 2], sv[b])
            cp(Xm[:], kst[b % 2], eng=POOL)
            mm(pA[3][0:8, 128:256], pT_sb[:, 0, 0:8], Xm[:])
            ts(otmp[0:8, 0:64], pA[3][0:8, 192:256], evm[:, 1:2], ALU.mult)
            stt(ohb[:, b, :], pA[3][0:8, 128:192], evm[:, 0:1], otmp[0:8, 0:64], ALU.mult, ALU.add)
        for b in range(R):
            act(es8[:, b:b + 1], sinkc[:], AF.Exp, bias=nmx8[:, b:b + 1])
        tt(rs8[:], rs8[:], es8[:], ALU.add)
        recip(rs8[:], rs8[:])
        for b in range(R):
            ts(ohb[:, b, :], ohb[:, b, :], rs8[:, b:b + 1], ALU.mult)
        dma(osc[:, :, :], ohb[:], eng=POOL)
        oas = edec[0:R, :]
        oas4 = oas.rearrange("p (c half d) -> p c half d", c=4, half=2)
        for half in range(2):
            dma(oas4[:, :, half, :], osc[half * 4:(half + 1) * 4].rearrange("c b d -> b c d"))
        tt(mix[0:R, 0:512], oas, sza[0:R, :], ALU.mult)
        xs1 = xt[1][0:R, :]
        out_proj(W0o, xsb, xs1, R)
        dma(xs1d[:, :], xs1, eng=POOL)
        memset(xbT2[0][:], 0.0); memset(xbT2[1][:], 0.0); memset(Sdn[:], 0.0); memset(Sdnb[:], 0.0)
        memset(kTa[2][:], 0.0); memset(vtok[2][:], 0.0)

        cstp = [la, sp1, bc_sb]
        bkA = [pA[0], pA[5]]

        def stageA(it):
                xc = xt[it % 2]
                sza, szb, kvg, qTa, xbT = sza2[it % 2], szb2[it % 2], kvg2[it % 2], qTa2[it % 2], xbT2[it % 2]
                vcur, vprev = vtok[it % 3], vtok[(it + 2) % 3]
                kcur, kprev = kTa[it % 3], kTa[(it + 2) % 3]
                dma(xc[:], xp[it * 128:(it + 1) * 128, :])
                norm_and_transpose(xc[:], 128)
                for n, (c0, cw) in enumerate([(0, 512), (512, 512), (1024, 264)]):
                    for k in range(8):
                        mm(bkA[n % 2][:, 0:cw], hT[:, k, :], W0t[:, k, c0:c0 + cw], start=(k == 0), stop=(k == 7))
                    if n == 0:
                        act(sza[:], bkA[0][:], AF.Silu)
                    elif n == 1:
                        act(szb[:], bkA[1][:], AF.Silu)
                    else:
                        cp(kvg[:], bkA[0][:, 0:264])
                cp(vcur[:], kvg[:, 128:256], eng=POOL)
                for c in range(17):
                    bank = bkA[(c + 1) % 2]; sl_ = slice(((c // 2) % 4) * 128, ((c // 2) % 4 + 1) * 128)
                    for k in range(8):
                        mm(bank[:, sl_], W0f[:, k, c * 128:(c + 1) * 128], hT[:, k, :],
                           start=(k == 0), stop=(k == 7))
                    if c < 4:
                        act(qTa[:, c, :], bank[:, sl_], AF.Copy, scale=0.125)
                    elif c == 4:
                        cp(kcur[:], bank[:, sl_])
                    else:
                        cp(xbT[:, c - 5, 3:131], bank[:, sl_], eng=(ACT if c % 2 else DVE))
                if it == NT - 2:
                    dma(pk[0:112, :], kvg[16:128, 0:128], eng=POOL); dma(pv[0:112, :], kvg[16:128, 128:256], eng=POOL)
                if it == NT - 1:
                    dma(pk[112:128, :], kvg[0:16, 0:128], eng=POOL); dma(pv[112:128, :], kvg[0:16, 128:256], eng=POOL)
                    for c in range(12):
                        P.op(PE, lambda e, c=c: e.transpose(out=pA[0][0:3, (c % 4) * 128:(c % 4 + 1) * 128], in_=xbT[:, c, 16:19], identity=ident_f[:, :]),
                             reads=names(xbT, ident_f), writes=names(pA[0]))
                        if c % 4 == 3:
                            cp(cstp[(c - 3) // 4][0:3, :], pA[0][0:3, :])
                    for j_ in range(3):
                        dma(pconv[:, j_ * 512:(j_ + 1) * 512], cstp[j_][0:3, :], eng=POOL)


        def stage_swa(it):
                xc = xt[it % 2]
                sza, szb, kvg, qTa, xbT = sza2[it % 2], szb2[it % 2], kvg2[it % 2], qTa2[it % 2], xbT2[it % 2]
                vcur, vprev = vtok[it % 3], vtok[(it + 2) % 3]
                kcur, kprev = kTa[it % 3], kTa[(it + 2) % 3]
                for p in range(8):
                    c, half = p // 2, p % 2
                    pr = slice(half * 64, half * 64 + 64)
                    sc = pA[1]
                    mm(sc[:, 0:128], qTa[pr, c, :], kprev[pr, :])
                    mm(sc[:, 128:256], qTa[pr, c, :], kcur[pr, :])
                    tt(s_sb[:], sc[:, 0:256], (band0 if it == 0 else band)[:], ALU.add)
                    rmax(mrow[:], s_sb[:])
                    ts(nmrow[:], mrow[:], sink_bc[:, p:p + 1], ALU.max, -1.0, ALU.mult)
                    act(p_sb[:], s_sb[:], AF.Exp, bias=nmrow[:], accum=rsum[:])
                    act(esk[:], sink_bc[:, p:p + 1], AF.Exp, bias=nmrow[:])
                    tt(rden[:], rsum[:], esk[:], ALU.add)
                    recip(rden[:], rden[:])
                    tr(pTb[1][:, 4, :], p_sb[:, 0:128], ident_b[:])
                    tr(pTb[1][:, 5, :], p_sb[:, 128:256], ident_b[:])
                    cp(pT_sb[:], pTb[1][:, 4:6, :], eng=ACT)
                    ob = pA[1]
                    mm(ob[:, 256:320], pT_sb[:, 0, :], vprev[:, half * 64:half * 64 + 64], start=True, stop=False)
                    mm(ob[:, 256:320], pT_sb[:, 1, :], vcur[:, half * 64:half * 64 + 64], start=False, stop=True)
                    stt(mix[:, p * 64:(p + 1) * 64], ob[:, 256:320], rden[:], sza[:, p * 64:(p + 1) * 64], ALU.mult, ALU.mult)


        def stage_pre(it):
                xc = xt[it % 2]
                sza, szb, kvg, qTa, xbT = sza2[it % 2], szb2[it % 2], kvg2[it % 2], qTa2[it % 2], xbT2[it % 2]
                vcur, vprev = vtok[it % 3], vtok[(it + 2) % 3]
                kcur, kprev = kTa[it % 3], kTa[(it + 2) % 3]
                dma(vld[:], valid[it * 128:(it + 1) * 128, :])
                for c in range(12):
                    act(cacc[:, c, :], xbT[:, c, 0:128], AF.Copy, scale=convw[:, 0, c:c + 1])
                    for i in range(1, 4):
                        stt(cacc[:, c, :], xbT[:, c, i:i + 128], convw[:, i, c:c + 1], cacc[:, c, :], ALU.mult, ALU.add)
                cp(xbT2[(it + 1) % 2][:, :, 0:3], xbT[:, :, 128:131], eng=POOL)
                act(cacc[:], cacc[:], AF.Silu)
                act(sq[:], cacc[:, 0:8, :], AF.Square)
                for c in range(8):
                    mm(pA[3 + c // 4][:, (c % 4) * 128:(c % 4 + 1) * 128], ones_b[:], sq[:, c, :])
                for hh in range(2):
                    act(rn[:, hh * 4:(hh + 1) * 4, :], pA[3 + hh][:], AF.Ln, bias=epsc[:])
                    act(rn[:, hh * 4:(hh + 1) * 4, :], rn[:, hh * 4:(hh + 1) * 4, :], AF.Exp, scale=-0.5)
                stt(qTn[:], cacc[:, 0:4, :], 128.0 ** -0.5, rn[:, 0:4, :], ALU.mult, ALU.mult)
                tt(kTn[:], cacc[:, 4:8, :], rn[:, 4:8, :], ALU.mult)
                act(beta[:], kvg[:, 256:260], AF.Sigmoid)
                ts(beta[:], beta[:], vld[:], ALU.mult)
                tt(gtmp[:], kvg[:, 260:264], dtb_bc[:], ALU.add)
                act(gtmp[:], gtmp[:], AF.Exp)
                act(gtmp[:], gtmp[:], AF.Ln, bias=onec[:])
                tt(gval[:], gtmp[:], negA[:], ALU.mult)
                ts(gval[:], gval[:], vld[:], ALU.mult)
                mm(pA[2][:, 0:4], triu_f[:], gval[:])
                mm(pA[2][:, 4:8], ones_f[:], gval[:])
                cp(gc[:], pA[2][:, 0:4])
                ts(ngc[:], pA[2][:, 0:4], -1.0, ALU.mult)
                act(egl[:], pA[2][:, 4:8], AF.Exp)
                tt(edk[:], pA[2][:, 4:8], gc[:], ALU.subtract)
                act(edk[:], edk[:], AF.Exp)
                act(bdk[:], gc[:], AF.Exp)
                tt(bdk[:], bdk[:], beta[:], ALU.mult)
                for h in range(4):
                    ts(gh1[:, h, :], ones_f[:], gval[:, h:h + 1], ALU.mult, eng=POOL)

        def G3(bank):
            return bank[:, :].rearrange("p (h t) -> p h t", h=4)

        def bc_mid(m2):
            return m2.unsqueeze(1).to_broadcast([128, 4, 128])

        def bc_in(v2):
            return v2.unsqueeze(2).to_broadcast([128, 4, 128])

        def mm4(bank, lf, rf, **kw):
            for h in range(4):
                mm(bank[:, h * 128:(h + 1) * 128], lf(h), rf(h), **kw)

        def stage_gdn(it):
            Gb, Vb, Kb, Qb = pA[2], pA[3], pA[3], pA[4]
            tb = pTb[1]
            mm4(Gb, lambda h: gh1[:, h, :], lambda h: triu_f[:])
            stt(Dm4[:], G3(Gb), -1.0, bc_in(gc[:, :]), ALU.mult, ALU.add)
            tt(Dm4[:], Dm4[:], bc_mid(nm_ls[:, :]), ALU.add)
            act(Dm4[:], Dm4[:], AF.Exp)
            tt(DTm4[:], G3(Gb), bc_in(ngc[:, :]), ALU.add)
            tt(DTm4[:], DTm4[:], bc_mid(nm_ui[:, :]), ALU.add)
            act(DTm4[:], DTm4[:], AF.Exp)
            act(egcb4[:], G3(Gb), AF.Exp)
            for h in range(4):
                tr(tb[:, h, :], kTn[:, h, :], ident_b[:])
            tt(kbd4[:], tb[:, 0:4, :], bc_in(bdk[:, :]), ALU.mult)
            tt(kdd4[:], tb[:, 0:4, :], bc_in(edk[:, :]), ALU.mult)
            for h in range(4):
                P.op(PE, lambda e, h=h: e.transpose(out=Vb[:, h * 128:(h + 1) * 128], in_=cacc[:, 8 + h, :], identity=ident_f[:]),
                     reads=names(cacc[:, 8 + h, :], ident_f), writes=names(Vb))
            tt(vbeta4[:], G3(Vb), bc_in(beta[:, :]), ALU.mult)
            mm4(Kb, lambda h: kTn[:, h, :], lambda h: kTn[:, h, :])
            mm4(Qb, lambda h: kTn[:, h, :], lambda h: qTn[:, h, :])
            tt(Dm4[:], Dm4[:], bc_in(beta[:, :]), ALU.mult)
            tt(Nfull4[:], G3(Kb), Dm4[:], ALU.mult)
            tt(NnW[0][:], Nfull4[:], bc_mid(cmask_b[:, 0, :]), ALU.mult)
            tt(QKm4[:], G3(Qb), DTm4[:], ALU.mult)
            tt(qdT4[:], qTn[:], egcb4[:], ALU.mult)
            for h in range(4):
                tr(tb[:, h, :], NnW[0][:, h, :], ident_b[:])
            cp(NtW[0][:], tb[:, 0:4, :], eng=ACT)
            stt(RrW[0][:], NtW[0][:], -1.0, bc_mid(ident_b[:, :]), ALU.mult, ALU.add)
            X, Y, Z = pA[2], pA[3], pA[4]
            ci = 0
            for lvl in range(3):
                mm4(X, lambda h: NtW[ci][:, h, :], lambda h: NnW[ci][:, h, :])
                if lvl < 2:
                    mm4(Y, lambda h: NnW[ci][:, h, :], lambda h: NtW[ci][:, h, :])
                cp(NnW[1 - ci][:], G3(X))
                if lvl < 2:
                    cp(NtW[1 - ci][:], G3(Y), eng=ACT)
                ci = 1 - ci
                mm4(Z, lambda h: NnW[ci][:, h, :], lambda h: RrW[lvl % 2][:, h, :])
                tt(RrW[1 - lvl % 2][:], G3(Z), RrW[lvl % 2][:], ALU.add)
            ri = 1
            for lv in range(3):
                tt(Lm4[:], Nfull4[:], bc_mid(cmask_b[:, 1 + lv, :]), ALU.mult)
                for h in range(4):
                    tr(tb[:, h, :], RrW[ri][:, h, :], ident_b[:])
                cp(Tm4[:], tb[:, 0:4, :], eng=ACT)
                mm4(X, lambda h: Lm4[:, h, :], lambda h: RrW[ri][:, h, :])
                cp(Xm4[:], G3(X))
                mm4(Y, lambda h: Tm4[:, h, :], lambda h: Xm4[:, h, :])
                tt(RrW[1 - ri][:], RrW[ri][:], G3(Y), ALU.subtract)
                ri = 1 - ri
            Tt = RrW[ri]
            mm4(X, lambda h: kbd4[:, h, :], lambda h: Tt[:, h, :])
            ts(nwT4[:], G3(X), -1.0, ALU.mult)
            for h in range(4):
                mm(Y[:, h * 128:(h + 1) * 128], Tt[:, h, :], vbeta4[:, h, :], start=True, stop=False)
                mm(Y[:, h * 128:(h + 1) * 128], nwT4[:, h, :], Sdnb[:, h, :], start=False, stop=True)
            cp(vnew4[:], G3(Y))
            for h in range(4):
                mm(Z[:, h * 128:(h + 1) * 128], qdT4[:, h, :], Sdnb[:, h, :], start=True, stop=False)
                mm(Z[:, h * 128:(h + 1) * 128], QKm4[:, h, :], vnew4[:, h, :], start=False, stop=True)
            cp(osb[:, :], Z[:, :], eng=ACT)
            Wb = pA[2]
            mm4(Wb, lambda h: kdd4[:, h, :], lambda h: vnew4[:, h, :])
            tt(Sdn[:], Sdn[:], bc_in(egl[:, :]), ALU.mult)
            tt(Sdn[:], Sdn[:], G3(Wb), ALU.add)
            cp(Sdnb[:], Sdn[:], eng=ACT)

        def stage_post(it):
                xc = xt[it % 2]
                sza, szb, kvg, qTa, xbT = sza2[it % 2], szb2[it % 2], kvg2[it % 2], qTa2[it % 2], xbT2[it % 2]
                vcur, vprev = vtok[it % 3], vtok[(it + 2) % 3]
                kcur, kprev = kTa[it % 3], kTa[(it + 2) % 3]
                gated_rms(osb, 128, 4, onb_bc, szb, 512)
                out_proj(W0o, xc, x1t, 128, banks=(pA[1], pA[2]))
                dma(x1d[it * 128:(it + 1) * 128, :], x1t[:], eng=POOL)


        def stream(fn, *a):
            lst = P.begin(); fn(*a); P.end()
            return lst

        P.ops.extend(stream(stageA, 0))
        for it in range(NT):
            nxt = stream(stageA, it + 1) if it + 1 < NT else []
            n1, n2 = len(nxt) // 5, (4 * len(nxt)) // 5
            P.interleave([nxt[:n1], stream(stage_pre, it)])
            _skip = os.environ.get("KSKIP", "")
            _swa = [] if "swa" in _skip else stream(stage_swa, it)
            _gdn = [] if "gdn" in _skip else stream(stage_gdn, it)
            P.interleave([nxt[n1:n2], _swa, _gdn])
            P.interleave([nxt[n2:], stream(stage_post, it)])

        load_w(W1t, w1t_d, 2560, g1); load_w(W1f, w1f_d, 1040, g1); load_w(W1o, w1o_d, D, None)

        memset(sq[:], 0.0)
        for b in range(R):
            memset(eyeb4[:, b, :, b:b + 1], 1.0)
        xs1b = xt[0][0:R, :]
        dma(xs1b, xs1d[:, :])
        norm_and_transpose(xs1b, R)
        for k in range(8):
            mm(pA[5][0:16, 0:R], W1f[:, k, 1024:1040], hT[:, k, 0:R], start=(k == 0), stop=(k == 7))
        cp(gk_aug[0:16, 0:R], pA[5][0:16, 0:R])
        mm(pA[4][0:R, :], gk_aug[0:17, 0:R], wgk[:, :])
        act(sp1[0:R, :], pA[4][0:R, :], AF.Exp, scale=-1.0)
        act(sp1[0:R, :], sp1[0:R, :], AF.Ln, bias=onec[0:R, :])
        act(la[0:R, :], sp1[0:R, :], AF.Exp, scale=-1.0 / 16.0)
        for k in range(8):
            mm(pA[0][0:R, :], hT[:, k, 0:R], W1f[:, k, 0:512], start=(k == 0), stop=(k == 7))
        stt(bc_sb[0:R, :], pA[0][0:R, :], 128.0 ** -0.5, la[0:R, :], ALU.mult, ALU.mult)
        for k in range(8):
            mm(pA[1][0:R, :], hT[:, k, 0:R], W1t[:, k, 0:512], start=(k == 0), stop=(k == 7))
        cp(edec[0:R, :], pA[1][0:R, :])
        ts(sp1[0:R, :], pA[0][0:R, :], 128.0 ** -0.5, ALU.mult)
        tt(sp1[0:R, :], sp1[0:R, :], edec[0:R, :], ALU.mult)
        P.op(DVE, lambda e: e.reduce_sum(out=kq[0:R, :], in_=sp1[0:R, :].rearrange("p (h t) -> p h t", h=4), axis=AX.X),
             reads=names(sp1), writes=names(kq))
        kb1 = kd1[0:R, :]
        cp(kb1, edec[0:R, :])
        vS1 = o1[0:R, :]
        for n in range(4):
            bank = pA[2 + n % 2]
            for k in range(8):
                mm(bank[0:R, :], hT[:, k, 0:R], W1t[:, k, 512 + n * 512:1024 + n * 512], start=(k == 0), stop=(k == 7))
            if n < 2:
                cp(vS1[:, n * 512:(n + 1) * 512], bank[0:R, :])
            else:
                act(sz1[0:R, (n - 2) * 512:(n - 1) * 512], bank[0:R, :], AF.Silu)
        vb1 = mix[0:R, :]
        cp(vb1, vS1)
        for h in range(4):
            P.op(PE, lambda e, h=h: e.transpose(out=pA[5][:, h * 16:(h + 1) * 16], in_=bc_sb[0:R, h * 128:(h + 1) * 128], identity=ident_f[0:R, 0:R]),
                 reads=names(bc_sb, ident_f), writes=names(pA[5]))
            P.op(PE, lambda e, h=h: e.transpose(out=pA[5][:, 64 + h * 16:64 + (h + 1) * 16], in_=la[0:R, h * 128:(h + 1) * 128], identity=ident_f[0:R, 0:R]),
                 reads=names(la, ident_f), writes=names(pA[5]))
        cp(qTn[:, :, 0:R], pA[5][:, 0:64].rearrange("p (c t) -> p c t", c=4))
        aT = dtmp[:, 0:64]
        cp(aT, pA[5][:, 64:128])
        qTm1 = hT[:, :, :].rearrange("p a b -> p (a b)").rearrange("p (b h t) -> p b h t", b=16, h=4)
        for b in range(R):
            tt(qTm1[:, b, :, :], qTn[:, :, 0:R], eyeb4[:, b, :, :], ALU.mult)
        Sr1 = [Sg[:], rn[:, :, :].rearrange("p (h a) t -> p h (a t)", h=4)]
        Sb1 = [Sgb[:], v1[:, :].rearrange("p (h v) -> p h v", h=4)]
        kmr1 = [QKm1[0:R, :], vbeta[0:R, :]]
        for b in range(R):
            Sb_, Sbb = Sr1[b % 2], Sb1[b % 2]
            dma(Sb_, sgla[b].rearrange("h k v -> k h v"))
            cp(Sbb, Sb_, eng=ACT)
            for h in range(4):
                mm(pA[h][0:R, 0:256], qTm1[:, b, h, :], Sbb[:, h, :], start=(b == 0), stop=(b == R - 1))
                km = kmr1[(b * 4 + h) % 2]
                ts(km, kb1[:, h * 128:(h + 1) * 128], ident_f[0:R, b:b + 1], ALU.mult)
                up = pA[4 + h % 2]
                mm(up[:, 0:256], km, vb1[:, h * 256:(h + 1) * 256])
                stt(Sb_[:, h, :], Sb_[:, h, :], aT[:, h * 16 + b:h * 16 + b + 1], up[:, 0:256], ALU.mult, ALU.add)
            dma(sgla_o[b].rearrange("h k v -> k h v"), Sb_, eng=POOL)
        og1 = x1t[0:R, :]
        for h in range(4):
            ts(otmp[0:R, :], vS1[:, h * 256:(h + 1) * 256], kq[0:R, h:h + 1], ALU.mult)
            tt(og1[:, h * 256:(h + 1) * 256], pA[h][0:R, 0:256], otmp[0:R, :], ALU.add)
        gated_rms(og1, 256, 4, onc_bc, sz1, 0, R=R)
        xs2 = xt[1][0:R, :]
        out_proj(W1o, xs1b, xs2, R)
        act(junk[0:R, :], xs2, AF.Square, accum=ss[0:R, :])
        rstd_from_ss(rstd[0:R, :], ss[0:R, :], 1.0 / D)
        stt(og1, xs2, rstd[0:R, :], fn_bc[0:R, :], ALU.mult, ALU.mult)
        dma(ys[:, :], og1, eng=POOL)
        memset(Sg[:], 0.0); memset(Sgb[:], 0.0)

        r1b = xbT2[0][:, :, :].rearrange("p c t -> p (c t)").bitcast(BF16)
        r2b = xbT2[1][:, :, :].rearrange("p c t -> p (c t)").bitcast(BF16)
        f512 = lambda t_: t_[:, :, :].rearrange("p h t -> p (h t)")
        L1S = [dict(hb=hb, hT=hT, la=la, sp1=sp1, bc_sb=bc_sb, edec=edec, ebT=ebT, einvT=einvT, qdT1=qdT1, kinvT1=kinvT1,
                    kd1=kd1, v1=v1, sz1=sz1, gk_aug=gk_aug),
               dict(hb=r1b[:, 0:1024], hT=r1b[:, 1024:2048].rearrange("p (k t) -> p k t", k=8),
                    la=f512(Dm4), sp1=f512(DTm4), bc_sb=f512(egcb4), edec=f512(rn[:, 0:4, :]), ebT=rn[:, 4:8, :], einvT=gh1[:, :, :],
                    qdT1=NnW[0][:, :, :], kinvT1=NnW[1][:, :, :], kd1=f512(NtW[0]), v1=r2b[:, 0:1024], sz1=r2b[:, 1024:2048],
                    gk_aug=RrW[0][0:32, 0, :])]
        memset(RrW[0][0:32, 0, :], 1.0)
        for it in range(NT):
            _S = L1S[it % 2]
            hb, hT, la, sp1, bc_sb, edec, ebT, einvT = _S["hb"], _S["hT"], _S["la"], _S["sp1"], _S["bc_sb"], _S["edec"], _S["ebT"], _S["einvT"]
            qdT1, kinvT1, kd1, v1, sz1, gk_aug = _S["qdT1"], _S["kinvT1"], _S["kd1"], _S["v1"], _S["sz1"], _S["gk_aug"]
            junk = hb
            x1t = xt[it % 2]
            dma(x1t[:], x1d[it * 128:(it + 1) * 128, :])
            dma(vld[:], valid[it * 128:(it + 1) * 128, :])
            norm_and_transpose(x1t[:], 128)
            for k in range(8):
                mm(pA[2][0:16, 0:128], W1f[:, k, 1024:1040], hT[:, k, :], start=(k == 0), stop=(k == 7))
            cp(gk_aug[0:16, :], pA[2][0:16, 0:128])
            mm(pA[1][:], gk_aug[0:17, :], wgk[:, :])
            act(sp1[:], pA[1][:], AF.Exp, scale=-1.0)
            act(sp1[:], sp1[:], AF.Ln, bias=onec[:])
            ts(la[:], sp1[:], vld[:], ALU.mult, -1.0 / 16.0, ALU.mult)
            mm(pA[0][:], triu_f[:], la[:])
            mm(pA[1][:], ones_f[:], la[:])
            cp(bc_sb[:], pA[0][:], eng=ACT)
            tt(edec[:], pA[1][:], bc_sb[:], ALU.subtract)
            act(edec[:], edec[:], AF.Exp)
            for h in range(4):
                mm(pA[2][:, h * 128:(h + 1) * 128], la[:, h * 128:(h + 1) * 128], triu_f[:])
            act(ebT[:], pA[2][:], AF.Exp)
            act(einvT[:], pA[2][:], AF.Exp, scale=-1.0)
            for c in range(8):
                bank = pA[c // 4]
                for k in range(8):
                    mm(bank[:, (c % 4) * 128:(c % 4 + 1) * 128], W1f[:, k, c * 128:(c + 1) * 128], hT[:, k, :],
                       start=(k == 0), stop=(k == 7))
            stt(qdT1[:], pA[0][:], 128.0 ** -0.5, ebT[:], ALU.mult, ALU.mult)
            tt(kinvT1[:], pA[1][:], einvT[:], ALU.mult)
            kraw = cacc[:, 8:12, :]
            cp(kraw, pA[1][:, :].rearrange("p (h t) -> p h t", h=4), eng=ACT)
            for h in range(4):
                P.op(PE, lambda e, h=h, kraw=kraw: e.transpose(out=pA[2][:, h * 128:(h + 1) * 128], in_=kraw[:, h, :], identity=ident_f[:]),
                     reads=names(kraw[:, h, :], ident_f), writes=names(pA[2]))
            tt(kd1[:], pA[2][:], edec[:], ALU.mult)
            for n in range(1, 5):
                bank = pA[n % 3]
                for k in range(8):
                    mm(bank[:], hT[:, k, :], W1t[:, k, n * 512:(n + 1) * 512], start=(k == 0), stop=(k == 7))
                if n < 3:
                    cp(v1[:, (n - 1) * 512:n * 512], bank[:], eng=ACT)
                else:
                    act(sz1[:, (n - 3) * 512:(n - 2) * 512], bank[:], AF.Silu)
            for h in range(4):
                wk = pA[3 + h % 2]
                mm(wk[:, 0:128], kinvT1[:, h, :], qdT1[:, h, :])
                tt(QKm1[:], wk[:, 0:128], triu_f[:], ALU.mult)
                ob = pA[5]
                mm(ob[:, 0:256], QKm1[:], v1[:, h * 256:(h + 1) * 256], start=True, stop=False)
                mm(ob[:, 0:256], qdT1[:, h, :], Sgb[:, h, :], start=False, stop=True)
                cp(o1[:, h * 256:(h + 1) * 256], ob[:, 0:256], eng=ACT)
                mm(wk[:, 256:512], kd1[:, h * 128:(h + 1) * 128], v1[:, h * 256:(h + 1) * 256])
                stt(Sg[:, h, :], Sg[:, h, :], ebT[:, h, 127:128], wk[:, 256:512], ALU.mult, ALU.add)
                cp(Sgb[:, h, :], Sg[:, h, :], eng=ACT)
            gated_rms(o1, 256, 4, onc_bc, sz1, 0)
            out_proj(W1o, x1t, x2t, 128, banks=(pA[3], pA[4]))
            act(junk[:], x2t[:], AF.Square, accum=ss[:])
            rstd_from_ss(rstd[:], ss[:], 1.0 / D)
            stt(yt[:], x2t[:], rstd[:], fn_bc[:], ALU.mult, ALU.mult)
            dma(yp[it * 128:(it + 1) * 128, :], yt[:], eng=POOL)

        _S = L1S[0]
        hb, hT, la, sp1, bc_sb, edec, ebT, einvT = _S["hb"], _S["hT"], _S["la"], _S["sp1"], _S["bc_sb"], _S["edec"], _S["ebT"], _S["einvT"]
        qdT1, kinvT1, kd1, v1, sz1, gk_aug = _S["qdT1"], _S["kinvT1"], _S["kd1"], _S["v1"], _S["sz1"], _S["gk_aug"]
        junk = hb
        dma(pdn.rearrange("h k v -> k h v"), Sdn[:], eng=POOL)
        dma(pgla.rearrange("h k v -> k h v"), Sg[:], eng=POOL)

        global _P
        _P = P
        P.warm_ap = ident_b[:]
        n = P.emit(nc, st)
    return nc


_NC = None


def _get_nc():
    global _NC
    if _NC is None:
        _NC = build_nc()
    return _NC


def kernel(x_prompt, x_sample, cache_swa_k, cache_swa_v, state_dn_conv, state_dn, state_gla,
           meta_tokens, norm_ab, w_in_ab, sink_a, conv_b, a_log_b, dt_bias_b, onorm_b, w_out_ab,
           norm_c, w_in_c, w_gk_up, b_gk, onorm_c, w_out_c, final_norm):
    f = lambda a: np.ascontiguousarray(np.asarray(a, dtype=np.float32))
    x_prompt, x_sample = f(x_prompt), f(x_sample)
    w0, w1 = f(w_in_ab)[0], f(w_in_c)[0]
    o = np.cumsum([0, 512, 128, 128, 512, 1536, 512, 4, 4])
    qa, ka, va, za, xb, zb, bb, ab = [w0[:, o[i]:o[i + 1]] for i in range(8)]
    perm = np.concatenate([np.arange(h * 64, h * 64 + 64) for h in HP])
    qa_p, za_p = qa[:, perm], za[:, perm]
    w0t = np.concatenate([za_p, zb, ka, va, bb, ab], axis=1)
    w0f = np.concatenate([qa_p, ka, xb], axis=1)
    w0o = f(w_out_ab)[0].copy()
    w0o[0:512] = w0o[0:512][perm]
    o1 = np.cumsum([0, 512, 512, 1024, 1024, 16])
    qc, kc, vc, zc, gkl = [w1[:, o1[i]:o1[i + 1]] for i in range(5)]
    w1t = np.concatenate([kc, vc, zc], axis=1)
    w1f = np.concatenate([qc, kc, gkl], axis=1)
    wgk = np.concatenate([f(w_gk_up)[0], f(b_gk)[0][None, :]], axis=0)
    sink_p = f(sink_a)[0][HP][None, :]
    common = dict(
        w0t=f(w0t), w0f=f(w0f), w0o=f(w0o), w1t=f(w1t), w1f=f(w1f), w1o=f(w_out_c)[0], wgk=f(wgk),
        g0=f(norm_ab)[0][:, None], g1=f(norm_c)[0][:, None], fn=f(final_norm)[None, :],
        sink=f(sink_p), convw=f(conv_b)[0], alog=f(a_log_b), dtb=f(dt_bias_b), onb=f(onorm_b), onc=f(onorm_c),
    )
    ii = np.arange(128)
    same = lambda b: ((ii[:, None] // b) == (ii[None, :] // b)).astype(np.float32)
    cmask = np.stack([same(16), same(32) - same(16), same(64) - same(32), same(128) - same(64)])
    common["cmask"] = cmask
    meta = f(meta_tokens)
    NT = NT_FULL; NTOK = NT * 128
    valid = np.zeros((NTOK, 1), np.float32); valid[:8208] = 1.0
    in_maps = []
    for c in range(8):
        b = c // 4
        xpc = np.zeros((NTOK, D), np.float32)
        xpc[:16] = meta; xpc[16:8208] = x_prompt[b]
        sl = slice(c * SB_, (c + 1) * SB_)
        m = dict(common)
        m.update(xp=xpc, valid=valid, xs=f(x_sample[sl, 0, :]),
                 ck=f(np.asarray(cache_swa_k)[0, sl].reshape(SB_, 128, 128)),
                 cv=f(np.asarray(cache_swa_v)[0, sl].reshape(SB_, 128, 128)),
                 cconv=f(np.asarray(state_dn_conv)[0, sl]), sdn=f(np.asarray(state_dn)[0, sl]),
                 sgla=f(np.asarray(state_gla)[0, sl]))
        in_maps.append(m)
    nc = _get_nc()
    res = run_bass_kernel_spmd(nc, in_maps, core_ids=list(range(8))).results
    R = lambda k, cs: [np.asarray(res[c][k]) for c in cs]
    y_prompt = np.stack([r[16:8208] for r in R("yp", [0, 4])])
    y_sample = np.concatenate(R("ys", range(8)))[:, None, :]
    pk = np.stack(R("pk", [0, 4])).reshape(1, 2, 128, 2, 64)
    pv = np.stack(R("pv", [0, 4])).reshape(1, 2, 128, 2, 64)
    pconv = np.stack(R("pconv", [0, 4]))[None]
    pdn = np.stack(R("pdn", [0, 4]))[None]
    pgla = np.stack(R("pgla", [0, 4]))[None]
    sk = np.concatenate(R("sk", range(8))).reshape(1, 128, 128, 2, 64)
    sv = np.concatenate(R("sv", range(8))).reshape(1, 128, 128, 2, 64)
    sconv = np.concatenate(R("sconv", range(8)))[None]
    sdn_o = np.concatenate(R("sdn_o", range(8)))[None]
    sgla_o = np.concatenate(R("sgla_o", range(8)))[None]
    outs = (y_prompt, y_sample, pk, pv, pconv, pdn, pgla, sk, sv, sconv, sdn_o, sgla_o)
    return tuple(np.ascontiguousarray(a, dtype=np.float32) for a in outs)
```

```python
import os
import numpy as np
from contextlib import ExitStack
import concourse.bass as bass
import concourse.mybir as mybir
from concourse.bass_utils import run_bass_kernel_spmd

F32 = mybir.dt.float32
BF16 = mybir.dt.bfloat16
AF = mybir.ActivationFunctionType
ALU = mybir.AluOpType
AX = mybir.AxisListType

PE, ACT, DVE, POOL, SP = "tensor", "scalar", "vector", "gpsimd", "sync"
COMPUTE = (PE, ACT, DVE, POOL)
SEM_LIM = 30000
N_DMA_SEMS = 16

D = 1024
NT_FULL = 65
SB_ = 16
NEG = -30000.0
HP = [0, 4, 1, 5, 2, 6, 3, 7]


class Prog:
    def __init__(self):
        self.ops = []
        self.cur = self.ops

    def begin(self):
        self.cur = []
        return self.cur

    def end(self):
        self.cur = self.ops

    def interleave(self, streams):
        its = [list(x) for x in streams if x]
        pos = [0] * len(its)
        left = sum(len(x) for x in its)
        while left:
            for k, x in enumerate(its):
                if pos[k] < len(x):
                    self.ops.append(x[pos[k]]); pos[k] += 1; left -= 1

    def op(self, eng, fn, reads=(), writes=()):
        import sys
        f = sys._getframe(1)
        self.cur.append(dict(eng=eng, fn=fn, reads=tuple(reads), writes=tuple(writes), dma=False, line=f.f_lineno))

    def dma(self, eng, fn, reads=(), writes=()):
        self.cur.append(dict(eng=eng, fn=fn, reads=tuple(reads), writes=tuple(writes), dma=True))

    def emit(self, nc, stack):
        import os
        ops = self.ops
        mx = int(os.environ.get('KMAXOPS', '0'))
        if mx:
            ops = ops[:mx]
        n = len(ops)
        print('n_ops', n)
        WHOLE = (0, 1 << 30, 0, 1 << 30)

        def norm(k):
            if isinstance(k, str):
                return (k,) + WHOLE
            return k

        def overlap(r1, r2):
            return r1[0] < r2[1] and r2[0] < r1[1] and r1[2] < r2[3] and r2[2] < r1[3]

        def covers(big, small):
            return big[0] <= small[0] and big[1] >= small[1] and big[2] <= small[2] and big[3] >= small[3]

        def analyze(ops, dedup):
            n = len(ops)
            recs = {}
            full = [None] * n
            deps = [None] * n
            needed = [False] * n
            for i, o in enumerate(ops):
                acc = []
                for k in o["reads"]:
                    k = norm(k)
                    if k[0].startswith("p_"):
                        acc.append((k[0], WHOLE, True))
                    else:
                        acc.append((k[0], k[1:], False))
                for k in o["writes"]:
                    k = norm(k)
                    acc.append((k[0], WHOLE if k[0].startswith("p_") else k[1:], True))
                d = set()
                for name, rect, isw in acc:
                    for r in recs.get(name, ()):
                        if (isw or r[2]) and overlap(rect, r[0]):
                            d.add(r[1])
                d.discard(i)
                full[i] = d
                best, dl = {}, []
                for j in d:
                    oj = ops[j]
                    if oj["dma"]:
                        dl.append(j)
                    else:
                        e = oj["eng"]
                        if e == PE and o["eng"] == PE and not o["dma"]:
                            continue
                        if e not in best or best[e] < j:
                            best[e] = j
                dl.extend(best.values())
                deps[i] = dl
                for j in dl:
                    needed[j] = True
                for name, rect, isw in acc:
                    lst = recs.setdefault(name, [])
                    if isw:
                        lst[:] = [r for r in lst if not covers(rect, r[0])]
                    elif dedup and not o["dma"]:
                        lst[:] = [r for r in lst if not (not r[2] and r[0] == rect and not ops[r[1]]["dma"]
                                                         and ops[r[1]]["eng"] == o["eng"])]
                    lst.append([rect, i, isw])
            return full, deps, needed

        if os.environ.get("KSCHED", "1") == "1":
            import heapq
            full, _, _ = analyze(ops, False)
            dur = [0.0] * n
            for i, o in enumerate(ops):
                ext = 512
                if o["writes"] and not isinstance(o["writes"][0], str):
                    w0 = o["writes"][0]
                    ext = (w0[4] - w0[3]) // 4
                e = o["eng"]
                if o["dma"]:
                    dur[i] = 2.5
                elif e == PE:
                    dur[i] = 0.11 + 0.0005 * ext
                elif e == ACT:
                    dur[i] = 0.30 + 0.00085 * ext
                elif e == DVE:
                    dur[i] = 0.25 + 0.0011 * ext
                else:
                    dur[i] = 0.35 + 0.002 * ext
            succ = [[] for _ in range(n)]
            indeg = [0] * n
            for i in range(n):
                for j in full[i]:
                    succ[j].append(i)
                indeg[i] = len(full[i])
            cpl = [0.0] * n
            for i in range(n - 1, -1, -1):
                m = 0.0
                for k in succ[i]:
                    if cpl[k] > m:
                        m = cpl[k]
                cpl[i] = dur[i] + m
            engs = (PE, ACT, DVE, POOL, SP)
            fut = {e: [] for e in engs}
            av = {e: [] for e in engs}
            efree = {e: 0.0 for e in engs}
            ready_t = [0.0] * n
            avail_t = [0.0] * n
            for i in range(n):
                if indeg[i] == 0:
                    heapq.heappush(fut[ops[i]["eng"]], (0.0, i))
            order = []
            done = 0
            while done < n:
                bs, be, bi = None, None, None
                for e in engs:
                    f_, a_ = fut[e], av[e]
                    while f_ and f_[0][0] <= efree[e]:
                        rt, i = heapq.heappop(f_)
                        heapq.heappush(a_, (-cpl[i], i))
                    if a_:
                        cs = efree[e]
                    elif f_:
                        cs = f_[0][0]
                    else:
                        continue
                    if bs is None or cs < bs:
                        bs, be = cs, e
                e = be
                if av[e]:
                    _, i = heapq.heappop(av[e])
                else:
                    _, i = heapq.heappop(fut[e])
                st_ = bs
                o = ops[i]
                if o["dma"]:
                    efree[e] = st_ + 0.15
                    avail_t[i] = st_ + dur[i]
                else:
                    efree[e] = st_ + dur[i]
                    avail_t[i] = st_ + dur[i]
                order.append((st_, done, i))
                done += 1
                for k in succ[i]:
                    lat = 0.1 if (ops[k]["eng"] == e and not o["dma"]) else 0.15
                    t_ = avail_t[i] + lat
                    if t_ > ready_t[k]:
                        ready_t[k] = t_
                    indeg[k] -= 1
                    if indeg[k] == 0:
                        heapq.heappush(fut[ops[k]["eng"]], (ready_t[k], k))
            order.sort()
            ops_o = ops
            ops = [ops[i] for _, _, i in order]
            print("sched makespan est (us):", max(avail_t), "crit path:", max(cpl))
            _ld = {}
            for i, o in enumerate(self.ops if not mx else ops):
                pass
            for (st_, _, i) in order:
                pass
            for e in engs:
                print("  load", e, sum((0.15 if ops_o[i]["dma"] else dur[i]) for i in range(n) if ops_o[i]["eng"] == e))
        _, deps, needed = analyze(ops, True)
        eng_sems = {e: [] for e in COMPUTE}
        eng_cnt = {e: 0 for e in COMPUTE}
        qs_ = (SP, POOL, ACT)
        dma_sems = {q: [stack.enter_context(nc.semaphore("dma_%s%d" % (q, i))) for i in range(N_DMA_SEMS)] for q in qs_}
        dma_tot = {q: [0] * N_DMA_SEMS for q in qs_}
        dma_last = {q: [None] * N_DMA_SEMS for q in qs_}
        dma_rr = {q: 0 for q in qs_}
        tag = [None] * n
        waited = {e: {} for e in (PE, ACT, DVE, POOL, SP)}
        handles = {PE: nc.tensor, ACT: nc.scalar, DVE: nc.vector, POOL: nc.gpsimd, SP: nc.sync}

        import os as _os
        n_warm = int(_os.environ.get("KWARM", "0"))
        warm_ap = getattr(self, "warm_ap", None)

        def do_wait(e, sv):
            sem, val = sv
            key = id(sem)
            w = waited[e]
            if w.get(key, 0) >= val:
                return
            w[key] = val
            if e == PE and n_warm and warm_ap is not None:
                for _ in range(n_warm):
                    nc.tensor.ldweights(warm_ap)
            handles[e].wait_ge(sem, val)

        for i, o in enumerate(ops):
            e = o["eng"]
            for j in deps[i]:
                do_wait(e, tag[j])
            if o["dma"]:
                s = dma_rr[e]
                dma_rr[e] = (s + 1) % N_DMA_SEMS
                if dma_last[e][s] is not None:
                    do_wait(e, tag[dma_last[e][s]])
                ins = o["fn"](handles[e])
                dma_tot[e][s] += 16
                ins.then_inc(dma_sems[e][s], 16)
                tag[i] = (dma_sems[e][s], dma_tot[e][s])
                dma_last[e][s] = i
            else:
                ins = o["fn"](handles[e])
                if needed[i]:
                    c = eng_cnt[e]
                    si = c // SEM_LIM
                    while len(eng_sems[e]) <= si:
                        eng_sems[e].append(stack.enter_context(nc.semaphore("%s_p%d" % (e, len(eng_sems[e])))))
                    ins.then_inc(eng_sems[e][si], 1)
                    eng_cnt[e] = c + 1
                    tag[i] = (eng_sems[e][si], c % SEM_LIM + 1)
        for q in qs_:
            for s in range(N_DMA_SEMS):
                if dma_last[q][s] is not None:
                    do_wait(SP, tag[dma_last[q][s]])
        return n


def build_nc(NT=NT_FULL):
    NTOK = NT * 128
    nc = bass.Bass("TRN2", target_bir_lowering=False)
    P = Prog()

    def din(name, shape):
        return nc.dram_tensor(name, list(shape), F32, kind="ExternalInput").ap()

    def dout(name, shape):
        return nc.dram_tensor(name, list(shape), F32, kind="ExternalOutput").ap()

    xp = din("xp", [NTOK, D]); valid = din("valid", [NTOK, 1]); xs = din("xs", [SB_, D])
    ck = din("ck", [SB_, 128, 128]); cv = din("cv", [SB_, 128, 128]); cconv = din("cconv", [SB_, 3, 1536])
    sdn = din("sdn", [SB_, 4, 128, 128]); sgla = din("sgla", [SB_, 4, 128, 256])
    w0t_d = din("w0t", [D, 1288]); w0f_d = din("w0f", [D, 2176]);
    w0o_d = din("w0o", [D, D])
    w1t_d = din("w1t", [D, 2560]); w1f_d = din("w1f", [D, 1040]); w1o_d = din("w1o", [D, D])
    wgk_d = din("wgk", [17, 512])
    g0_d = din("g0", [D, 1]); g1_d = din("g1", [D, 1]); fn_d = din("fn", [1, D])
    sink_d = din("sink", [1, 8]); convw_d = din("convw", [4, 1536]); alog_d = din("alog", [1, 4]); dtb_d = din("dtb", [1, 4])
    onb_d = din("onb", [1, 128]); onc_d = din("onc", [1, 256])
    cmask_d = din("cmask", [4, 128, 128])

    yp = dout("yp", [NTOK, D]); ys = dout("ys", [SB_, D])
    pk = dout("pk", [128, 128]); pv = dout("pv", [128, 128]); pconv = dout("pconv", [3, 1536])
    pdn = dout("pdn", [4, 128, 128]); pgla = dout("pgla", [4, 128, 256])
    sk = dout("sk", [SB_, 128, 128]); sv = dout("sv", [SB_, 128, 128]); sconv = dout("sconv", [SB_, 3, 1536])
    sdn_o = dout("sdn_o", [SB_, 4, 128, 128]); sgla_o = dout("sgla_o", [SB_, 4, 128, 256])

    with ExitStack() as st:
        def sb(name, shape, dt=F32):
            return st.enter_context(nc.sbuf_tensor("s_" + name, list(shape), dt))

        def psb(name, shape, dt=F32):
            return st.enter_context(nc.psum_tensor("p_" + name, list(shape), dt))

        def names(*aps):
            out = []
            for a in aps:
                if a is None or isinstance(a, (int, float)):
                    continue
                try:
                    apl = a.ap
                    ps, pc = apl[0]
                    off = a.offset
                    if ps <= 0:
                        raise ValueError
                    plo = off // ps
                    flo = off % ps
                    ext = 1
                    for st_, cn in apl[1:]:
                        ext += (cn - 1) * abs(st_)
                    if flo + ext > ps:
                        raise ValueError
                    esz = 2 if a.dtype == BF16 else 4
                    out.append((a.name, plo, plo + pc, flo * esz, (flo + ext) * esz))
                except Exception:
                    out.append(a.name)
            return out

        def mm(out, lhsT, rhs, start=True, stop=True):
            P.op(PE, lambda e: e.matmul(out, lhsT=lhsT, rhs=rhs, start=start, stop=stop),
                 reads=names(lhsT, rhs), writes=names(out))

        def tr(out, in_, ident):
            P.op(PE, lambda e: e.transpose(out=out, in_=in_, identity=ident), reads=names(in_, ident), writes=names(out))

        def act(out, in_, func, bias=None, scale=None, accum=None, eng=ACT):
            kw = {}
            if bias is not None:
                kw["bias"] = bias
            if scale is not None:
                kw["scale"] = scale
            if accum is not None:
                kw["accum_out"] = accum
            P.op(eng, lambda e: e.activation(out=out, in_=in_, func=func, **kw),
                 reads=names(in_, bias, scale), writes=names(out, accum))

        def tt(out, in0, in1, op, eng=DVE):
            P.op(eng, lambda e: e.tensor_tensor(out=out, in0=in0, in1=in1, op=op), reads=names(in0, in1), writes=names(out))

        def ts(out, in0, s1, op0, s2=None, op1=None, eng=DVE, accum=None):
            kw = {}
            if op1 is not None:
                kw["op1"] = op1
            if accum is not None:
                kw["accum_out"] = accum
            P.op(eng, lambda e: e.tensor_scalar(out=out, in0=in0, scalar1=s1, scalar2=s2, op0=op0, **kw),
                 reads=names(in0, s1, s2), writes=names(out, accum))

        def stt(out, in0, scalar, in1, op0, op1, eng=DVE):
            P.op(eng, lambda e: e.scalar_tensor_tensor(out=out, in0=in0, scalar=scalar, in1=in1, op0=op0, op1=op1),
                 reads=names(in0, scalar, in1), writes=names(out))

        def cp(out, in_, eng=DVE):
            if eng == ACT:
                P.op(ACT, lambda e: e.copy(out=out, in_=in_), reads=names(in_), writes=names(out))
            else:
                P.op(eng, lambda e: e.tensor_copy(out=out, in_=in_), reads=names(in_), writes=names(out))

        def memset(ap, v, eng=POOL):
            P.op(eng, lambda e: e.memset(ap, v), writes=names(ap))

        def asel(out, in_, pattern, cmp, fill, base, cm):
            P.op(POOL, lambda e: e.affine_select(out=out, in_=in_, pattern=pattern, compare_op=cmp, fill=fill,
                                                 base=base, channel_multiplier=cm), reads=names(in_), writes=names(out))

        def rmax(out, in_):
            P.op(DVE, lambda e: e.reduce_max(out=out, in_=in_, axis=AX.X), reads=names(in_), writes=names(out))

        def recip(out, in_):
            P.op(DVE, lambda e: e.reciprocal(out=out, in_=in_), reads=names(in_), writes=names(out))

        def dma(out, in_, eng=SP):
            P.dma(eng, lambda e: e.dma_start(out=out, in_=in_, allow_slow_non_contiguous=True), reads=names(in_), writes=names(out))

        def rstd_from_ss(out, ss, inv_n, eps=1e-6):
            act(out, ss, AF.Ln, bias=epsc[0:out.shape[0], :], scale=inv_n)
            act(out, out, AF.Exp, scale=-0.5)

        epsc = sb("epsc", [128, 1]); memset(epsc[:], 1e-6)
        onec = sb("onec", [128, 1]); memset(onec[:], 1.0)
        ones_f = sb("ones_f", [128, 128]); memset(ones_f[:], 1.0)
        zeros_f = sb("zeros_f", [128, 256]); memset(zeros_f[:], 0.0)
        ones_b = sb("ones_b", [128, 128], BF16); cp(ones_b[:], ones_f[:])
        ident_f = sb("ident_f", [128, 128])
        asel(ident_f[:], ones_f[:], [[-1, 128]], ALU.is_equal, 0.0, 0, 1)
        ident_b = sb("ident_b", [128, 128], BF16); cp(ident_b[:], ident_f[:])
        triu_f = sb("triu_f", [128, 128])
        asel(triu_f[:], ones_f[:], [[1, 128]], ALU.is_ge, 0.0, 0, -1)
        nm_ls = sb("nm_ls", [128, 128])
        asel(nm_ls[:], zeros_f[:, 0:128], [[-1, 128]], ALU.is_gt, NEG, 0, 1)
        nm_ui = sb("nm_ui", [128, 128])
        asel(nm_ui[:], zeros_f[:, 0:128], [[1, 128]], ALU.is_ge, NEG, 0, -1)
        band = sb("band", [128, 256])
        asel(band[:], zeros_f[:], [[1, 256]], ALU.is_ge, NEG, -1, -1)
        asel(band[:], band[:], [[-1, 256]], ALU.is_ge, NEG, 128, 1)
        band0 = sb("band0", [128, 256])
        cp(band0[:], band[:], eng=POOL)
        memset(band0[:, 0:128], NEG)

        sink_bc = sb("sink_bc", [128, 8]); dma(sink_bc[:], sink_d.partition_broadcast(128))
        alog_bc = sb("alog_bc", [128, 4]); dma(alog_bc[:], alog_d.partition_broadcast(128))
        dtb_bc = sb("dtb_bc", [128, 4]); dma(dtb_bc[:], dtb_d.partition_broadcast(128))
        onb_bc = sb("onb_bc", [128, 128]); dma(onb_bc[:], onb_d.partition_broadcast(128))
        onc_bc = sb("onc_bc", [128, 256]); dma(onc_bc[:], onc_d.partition_broadcast(128))
        fn_bc = sb("fn_bc", [128, D]); dma(fn_bc[:], fn_d.partition_broadcast(128))
        negA = sb("negA", [128, 4])
        act(negA[:], alog_bc[:], AF.Exp)
        ts(negA[:], negA[:], -1.0, ALU.mult)
        convw = sb("convw", [128, 4, 12])
        for i in range(4):
            P.dma(SP, lambda e, i=i: e.dma_start(out=convw[:, i, :], in_=convw_d[i, :].rearrange("(c p) -> p c", p=128),
                                                 allow_slow_non_contiguous=True), reads=[], writes=names(convw))
        g0 = sb("g0", [128, 8]); dma(g0[:], g0_d.rearrange("(k p) o -> p (k o)", p=128))
        g1 = sb("g1", [128, 8]); dma(g1[:], g1_d.rearrange("(k p) o -> p (k o)", p=128))
        wgk = sb("wgk", [17, 512], BF16)
        wgk_f = sb("wgk_f", [17, 512]); dma(wgk_f[:], wgk_d[:, :]); cp(wgk[:], wgk_f[:])

        xt = [sb("xt%d" % i, [128, D]) for i in range(2)]
        x1t = sb("x1t", [128, D])
        stage = [x1t[:, :].rearrange("p (k n) -> p k n", k=8), xt[1][:, :].rearrange("p (k n) -> p k n", k=8)]
        stg_i = [0]

        def load_w(dst, src, ncols, g):
            for c0 in range(0, ncols, 128):
                cw = min(128, ncols - c0)
                s = stage[stg_i[0] % 2]; stg_i[0] += 1
                dma(s[:, :, 0:cw], src[:, c0:c0 + cw].rearrange("(k p) n -> p k n", p=128))
                for k in range(8):
                    e_ = (DVE, ACT, DVE, POOL)[(k + stg_i[0]) % 4]
                    if g is None:
                        cp(dst[:, k, c0:c0 + cw], s[:, k, 0:cw], eng=e_)
                    elif e_ == ACT:
                        act(dst[:, k, c0:c0 + cw], s[:, k, 0:cw], AF.Copy, scale=g[:, k:k + 1])
                    else:
                        ts(dst[:, k, c0:c0 + cw], s[:, k, 0:cw], g[:, k:k + 1], ALU.mult, eng=e_)

        Wall = sb("Wall", [128, 8, 4624], BF16)
        W0t, W0f, W0o = Wall[:, :, 0:1288], Wall[:, :, 1288:3464], Wall[:, :, 3464:4488]
        W1t, W1f, W1o = Wall[:, :, 0:2560], Wall[:, :, 2560:3600], Wall[:, :, 3600:4624]
        load_w(W0t, w0t_d, 1288, g0); load_w(W0f, w0f_d, 2176, g0); load_w(W0o, w0o_d, D, None)
        x1d = nc.dram_tensor("x1d", [NTOK, D], F32).ap()

        pTb = [psb("pTb%d" % i, [128, 8, 128], BF16) for i in range(2)]
        pA = [psb("pA%d" % i, [128, 512]) for i in range(6)]

        ss = sb("ss", [128, 1]); rstd = sb("rstd", [128, 1])
        hb = sb("hb", [128, D], BF16)
        junk = hb
        hT = sb("hT", [128, 8, 128], BF16)
        mix = sb("mix", [128, D], BF16)
        mixT = sb("mixT", [128, 8, 128], BF16)
        vld = sb("vld", [128, 1])

        def norm_and_transpose(xin, rows):
            act(junk[0:rows, :], xin, AF.Square, accum=ss[0:rows, :])
            rstd_from_ss(rstd[0:rows, :], ss[0:rows, :], 1.0 / D)
            ts(hb[0:rows, :], xin, rstd[0:rows, :], ALU.mult)
            for k in range(8):
                tr(pTb[0][:, k, 0:rows], hb[0:rows, k * 128:(k + 1) * 128], ident_b[0:rows, 0:rows])
            cp(hT[:, :, 0:rows], pTb[0][:, :, 0:rows], eng=ACT)

        def out_proj(Wo, xres, xout, rows, banks=None):
            banks = banks or (pA[0], pA[1])
            for k in range(8):
                tr(pTb[1][:, k, 0:rows], mix[0:rows, k * 128:(k + 1) * 128], ident_b[0:rows, 0:rows])
            cp(mixT[:, :, 0:rows], pTb[1][:, :, 0:rows], eng=ACT)
            for n in range(2):
                for k in range(8):
                    mm(banks[n][0:rows, :], mixT[:, k, 0:rows], Wo[:, k, n * 512:(n + 1) * 512], start=(k == 0), stop=(k == 7))
                tt(xout[:, n * 512:(n + 1) * 512], banks[n][0:rows, :], xres[:, n * 512:(n + 1) * 512], ALU.add)

        sza2 = [sb("sza%d" % i, [128, 512], BF16) for i in range(2)]; szb2 = [sb("szb%d" % i, [128, 512], BF16) for i in range(2)]
        kvg2 = [sb("kvg%d" % i, [128, 264]) for i in range(2)]
        vtok = [sb("vtok%d" % i, [128, 128], BF16) for i in range(3)]
        kTa = [sb("kTa%d" % i, [128, 128], BF16) for i in range(3)]
        qTa2 = [sb("qTa%d" % i, [128, 4, 128], BF16) for i in range(2)]
        xbT2 = [sb("xbT%d" % i, [128, 12, 131]) for i in range(2)]
        sza, szb, kvg, qTa, xbT = sza2[0], szb2[0], kvg2[0], qTa2[0], xbT2[0]
        cacc = sb("cacc", [128, 12, 128])
        sq = sb("sq", [128, 8, 128], BF16)
        rn = sb("rn", [128, 8, 128])
        qTn = sb("qTn", [128, 4, 128], BF16); kTn = sb("kTn", [128, 4, 128], BF16)
        gval = sb("gval", [128, 4]); beta = sb("beta", [128, 4]); gtmp = sb("gtmp", [128, 4])
        gc = sb("gc", [128, 4]); ngc = sb("ngc", [128, 4]); egl = sb("egl", [128, 4]); edk = sb("edk", [128, 4])
        bdk = sb("bdk", [128, 4])
        gh1 = sb("gh1", [128, 4, 128])
        W4 = lambda nm_, dt_=BF16: sb(nm_, [128, 4, 128], dt_)
        Dm4 = W4("Dm4", F32); DTm4 = W4("DTm4", F32); egcb4 = W4("egcb4", F32)
        Nfull4 = W4("Nfull4"); NnW = [W4("NnW0"), W4("NnW1")]; NtW = [W4("NtW0"), W4("NtW1")]; RrW = [W4("RrW0"), W4("RrW1")]
        Lm4 = W4("Lm4"); Xm4 = W4("Xm4"); Tm4 = W4("Tm4"); kbd4 = W4("kbd4"); kdd4 = W4("kdd4"); vbeta4 = W4("vbeta4")
        nwT4 = W4("nwT4"); vnew4 = W4("vnew4"); qdT4 = W4("qdT4"); QKm4 = W4("QKm4")
        dtmp = Dm4[:, 0, :]; Tm = Tm4[:, 0, :]; Xm = Xm4[:, 0, :]; Nfull = Nfull4[:, 0, :]; vbeta = vbeta4[:, 0, :]
        dma(Dm4[:], cmask_d.rearrange("m p x -> p m x"))
        cmask_b = sb("cmask_b", [128, 4, 128], BF16); cp(cmask_b[:], Dm4[:])
        Sdn = sb("Sdn", [128, 4, 128]); Sdnb = sb("Sdnb", [128, 4, 128], BF16)
        memset(Sdn[:], 0.0); memset(Sdnb[:], 0.0)
        osb = sb("osb", [128, 512])
        oss = sb("oss", [128, 4]); orn = sb("orn", [128, 4])
        oss8 = sb("oss8", [16, 8]); orn8 = sb("orn8", [16, 8]); kq = sb("kq", [16, 4])
        otmp = sb("otmp", [128, 256])
        s_sb = sb("s_sb", [128, 256]); p_sb = sb("p_sb", [128, 256], BF16); pT_sb = sb("pT_sb", [128, 2, 128], BF16)
        mrow = sb("mrow", [128, 1]); nmrow = sb("nmrow", [128, 1]); rsum = sb("rsum", [128, 1]); esk = sb("esk", [128, 1])
        rden = sb("rden", [128, 1])
        kd1 = sb("kd1", [128, 512], BF16); v1 = sb("v1", [128, D], BF16); sz1 = sb("sz1", [128, D], BF16)
        gk_aug = sb("gk_aug", [32, 128], BF16); memset(gk_aug[:], 1.0)
        la = sb("la", [128, 512]); sp1 = sb("sp1", [128, 512])
        bc_sb = sb("bc_sb", [128, 512]); edec = sb("edec", [128, 512])
        ebT = sb("ebT", [128, 4, 128]); einvT = sb("einvT", [128, 4, 128])
        qdT1 = sb("qdT1", [128, 4, 128], BF16); kinvT1 = sb("kinvT1", [128, 4, 128], BF16)
        QKm1 = sb("QKm1", [128, 128], BF16)
        Sg = sb("Sg", [128, 4, 256]); Sgb = sb("Sgb", [128, 4, 256], BF16)
        memset(Sg[:], 0.0); memset(Sgb[:], 0.0)
        o1 = cacc[:, 0:8, :].rearrange("p c t -> p (c t)")
        x2t = o1; yt = x1t

        def gated_rms(o_ap, width, nheads, onorm_bc, sz, mix_off, R=128):
            for h in range(nheads):
                act(otmp[0:R, 0:width], o_ap[0:R, h * width:(h + 1) * width], AF.Square, accum=oss[0:R, h:h + 1])
            rstd_from_ss(orn[0:R, 0:nheads], oss[0:R, 0:nheads], 1.0 / width)
            for h in range(nheads):
                stt(otmp[0:R, 0:width], o_ap[0:R, h * width:(h + 1) * width], orn[0:R, h:h + 1], onorm_bc[0:R, 0:width], ALU.mult, ALU.mult)
                tt(mix[0:R, mix_off + h * width:mix_off + (h + 1) * width], otmp[0:R, 0:width], sz[0:R, h * width:(h + 1) * width], ALU.mult)

        R = SB_
        kdd_s = sz1[0:R, 0:512]; dltf = xbT[0:R, :, :].rearrange("p c t -> p (c t)")[:, 0:512]
        xs1d = nc.dram_tensor("xs1d", [R, D], F32).ap()
        osc = nc.dram_tensor("osc", [8, R, 64], F32).ap()
        eyeb4 = sq[:, :, :].rearrange("p a b -> p (a b)").rearrange("p (b h t) -> p b h t", b=16, h=4)
        memset(sq[:], 0.0)
        for b in range(R):
            memset(eyeb4[:, b, :, b:b + 1], 1.0)
        sinkc = sb("sinkc", [8, 1])
        sink_hc = sink_d.rearrange("o (c half) -> half c o", half=2)
        dma(sinkc[0:4, :], sink_hc[0]); dma(sinkc[4:8, :], sink_hc[1])
        evm = sb("evm", [8, 2])
        memset(evm[:], 1.0)
        asel(evm[:, 0:1], evm[:, 0:1], [[0, 1]], ALU.is_ge, 0.0, 3, -1)
        asel(evm[:, 1:2], evm[:, 1:2], [[0, 1]], ALU.is_ge, 0.0, -4, 1)
        xsb = xt[0][0:R, :]
        dma(xsb, xs[:, :])
        norm_and_transpose(xsb, R)
        qs = la; xbs = cacc[0:R, :, :].rearrange("p c t -> p (c t)")
        for n, (c0, cw) in enumerate([(0, 512), (512, 512), (1024, 264)]):
            for k in range(8):
                mm(pA[n][0:R, 0:cw], hT[:, k, 0:R], W0t[:, k, c0:c0 + cw], start=(k == 0), stop=(k == 7))
        act(sza[0:R, :], pA[0][0:R, :], AF.Silu)
        act(szb[0:R, :], pA[1][0:R, :], AF.Silu)
        cp(kvg[0:R, :], pA[2][0:R, 0:264])
        for n, c0 in enumerate([0, 640, 1152, 1664]):
            for k in range(8):
                mm(pA[n][0:R, :], hT[:, k, 0:R], W0f[:, k, c0:c0 + 512], start=(k == 0), stop=(k == 7))
            if n == 0:
                act(qs[0:R, :], pA[0][0:R, :], AF.Copy, scale=0.125)
            else:
                cp(xbs[:, (n - 1) * 512:n * 512], pA[n][0:R, :])
        dma(sk[:, 0:127, :], ck[:, 1:128, :], eng=POOL); dma(sv[:, 0:127, :], cv[:, 1:128, :], eng=POOL)
        dma(sk[:, 127, :], kvg[0:R, 0:128], eng=POOL); dma(sv[:, 127, :], kvg[0:R, 128:256], eng=POOL)
        dma(sconv[:, 0:2, :], cconv[:, 1:3, :], eng=POOL)
        dma(sconv[:, 2, :], xbs, eng=POOL)
        cres = [rn[0:R, 0:4, :].rearrange("p c t -> p (c t)"), rn[0:R, 4:8, :].rearrange("p c t -> p (c t)"), osb[0:R, :]]
        tA, tW = sp1[0:R, :], bc_sb[0:R, :]
        for j in range(3):
            for i in range(4):
                dma(tW, convw_d[i, j * 512:(j + 1) * 512].partition_broadcast(R))
                if i < 3:
                    dma(tA, cconv[:, i, j * 512:(j + 1) * 512])
                    src = tA
                else:
                    src = xbs[:, j * 512:(j + 1) * 512]
                if i == 0:
                    tt(cres[j], src, tW, ALU.mult)
                else:
                    tt(edec[0:R, :], src, tW, ALU.mult)
                    tt(cres[j], cres[j], edec[0:R, :], ALU.add)
            act(cres[j], cres[j], AF.Silu)
        qk3 = rn[0:R, :, :]
        for c in range(8):
            act(otmp[0:R, 0:128], qk3[:, c, :], AF.Square, accum=oss8[0:R, c:c + 1])
        rstd_from_ss(orn8[0:R, :], oss8[0:R, :], 1.0)
        for c in range(8):
            ts(qk3[:, c, :], qk3[:, c, :], orn8[0:R, c:c + 1], ALU.mult, (128.0 ** -0.5 if c < 4 else 1.0), ALU.mult)
        tt(edec[0:R, :], rn[0:R, 0:4, :].rearrange("p c t -> p (c t)"), rn[0:R, 4:8, :].rearrange("p c t -> p (c t)"), ALU.mult)
        P.op(DVE, lambda e: e.reduce_sum(out=kq[0:R, :], in_=edec[0:R, :].rearrange("p (h t) -> p h t", h=4), axis=AX.X),
             reads=names(edec), writes=names(kq))
        for c in range(8):
            P.op(PE, lambda e, c=c: e.transpose(out=pA[5][:, c * 16:(c + 1) * 16], in_=qk3[:, c, :], identity=ident_f[0:R, 0:R]),
                 reads=names(rn, ident_f), writes=names(pA[5]))
        cp(qTn[:, :, 0:R], pA[5][:, 0:64].rearrange("p (c t) -> p c t", c=4))
        cp(kTn[:, :, 0:R], pA[5][:, 64:128].rearrange("p (c t) -> p c t", c=4))
        kTm = mixT[:, :, :].rearrange("p a b -> p (a b)").rearrange("p (b h t) -> p b h t", b=16, h=4)
        qTm = hT[:, :, :].rearrange("p a b -> p (a b)").rearrange("p (b h t) -> p b h t", b=16, h=4)
        for b in range(R):
            tt(kTm[:, b, :, :], kTn[:, :, 0:R], eyeb4[:, b, :, :], ALU.mult)
            tt(qTm[:, b, :, :], qTn[:, :, 0:R], eyeb4[:, b, :, :], ALU.mult, eng=POOL)
        cp(kdd_s, rn[0:R, 4:8, :].rearrange("p c t -> p (c t)"))
        act(beta[0:R, :], kvg[0:R, 256:260], AF.Sigmoid)
        tt(gtmp[0:R, :], kvg[0:R, 260:264], dtb_bc[0:R, :], ALU.add)
        act(gtmp[0:R, :], gtmp[0:R, :], AF.Exp)
        act(gtmp[0:R, :], gtmp[0:R, :], AF.Ln, bias=onec[0:R, :])
        tt(gval[0:R, :], gtmp[0:R, :], negA[0:R, :], ALU.mult)
        act(egl[0:R, :], gval[0:R, :], AF.Exp)
        egd = s_sb[0:R, 0:64]
        for b in range(R):
            ts(egd[:, b * 4:(b + 1) * 4], egl[0:R, :], ident_f[0:R, b:b + 1], ALU.mult)
        mm(pA[5][:, 128:192], ones_f[0:R, :], egd)
        eg_bc = dtmp[:, 0:64]
        cp(eg_bc, pA[5][:, 128:192])
        Sring = [Sdn, gh1]; Sbring = [Sdnb, qTa]
        for b in range(R):
            Sb_, Sbb = Sring[b % 2], Sbring[b % 2]
            dma(Sb_[:], sdn[b].rearrange("h k v -> k h v"))
            cp(Sbb[:], Sb_[:], eng=ACT)
            for h in range(4):
                mm(pA[h][0:R, 0:128], kTm[:, b, h, :], Sbb[:, h, :], start=(b == 0), stop=(b == R - 1))
        dlt = v1[0:R, 512:1024]
        vS = osb[0:R, :]
        for h in range(4):
            stt(otmp[0:R, 0:128], pA[h][0:R, 0:128], egl[0:R, h:h + 1], vS[:, h * 128:(h + 1) * 128], ALU.mult, ALU.subtract)
            ts(dltf[:, h * 128:(h + 1) * 128], otmp[0:R, 0:128], beta[0:R, h:h + 1], ALU.mult, -1.0, ALU.mult)
        cp(dlt, dltf)
        kmr = [kd1[0:R, :], v1[0:R, 0:512]]
        for b in range(R):
            Sb_, Sbb = Sring[b % 2], Sbring[b % 2]
            dma(Sb_[:], sdn[b].rearrange("h k v -> k h v"))
            cp(Sbb[:], Sb_[:], eng=ACT)
            km = kmr[b % 2]
            ts(km, kdd_s, ident_f[0:R, b:b + 1], ALU.mult)
            for h in range(4):
                mm(pA[h][0:R, 0:128], qTm[:, b, h, :], Sbb[:, h, :], start=(b == 0), stop=(b == R - 1))
                up = pA[4 + h % 2]
                mm(up[:, 0:128], km[:, h * 128:(h + 1) * 128], dlt[:, h * 128:(h + 1) * 128])
                stt(Sb_[:, h, :], Sb_[:, h, :], eg_bc[:, b * 4 + h:b * 4 + h + 1], up[:, 0:128], ALU.mult, ALU.add)
            dma(sdn_o[b].rearrange("h k v -> k h v"), Sb_[:], eng=POOL)
        ogs = sp1[0:R, :]
        for h in range(4):
            ts(otmp[0:R, 0:128], dltf[:, h * 128:(h + 1) * 128], kq[0:R, h:h + 1], ALU.mult)
            stt(ogs[:, h * 128:(h + 1) * 128], pA[h][0:R, 0:128], egl[0:R, h:h + 1], otmp[0:R, 0:128], ALU.mult, ALU.add)
        gated_rms(ogs, 128, 4, onb_bc, szb, 512, R=R)
        for c in range(4):
            P.op(PE, lambda e, c=c: e.transpose(out=pA[5][:, c * 16:(c + 1) * 16], in_=qs[0:R, c * 128:(c + 1) * 128], identity=ident_f[0:R, 0:R]),
                 reads=names(la, ident_f), writes=names(pA[5]))
        qblk = Nfull[:, :].rearrange("p (b q) -> p b q", b=16)
        memset(Nfull[:], 0.0)
        for c in range(4):
            for half in range(2):
                hs = slice(half * 64, half * 64 + 64)
                cp(qblk[hs, :, half * 4 + c], pA[5][hs, c * 16:(c + 1) * 16])
        scs = [xt[1][0:8, :].rearrange("p (b t) -> p b t", b=8), rn[0:8, :, :]]
        kst = [cacc[:, 0, :], cacc[:, 1, :]]
        for b in range(R):
            dma(kst[b % 2], sk[b])
            P.op(PE, lambda e, b=b: e.transpose(out=pA[4][:, (b % 2) * 128:(b % 2 + 1) * 128], in_=kst[b % 2], identity=ident_f[:]),
                 reads=names(cacc, ident_f), writes=names(pA[4]))
            cp(Tm[:], pA[4][:, (b % 2) * 128:(b % 2 + 1) * 128])
            mm(pA[3][0:8, 0:128], qblk[:, b, :], Tm[:])
            cp(scs[b // 8][:, b % 8, :], pA[3][0:8, 0:128])
        mx8 = sb("mx8", [8, 16]); nmx8 = sb("nmx8", [8, 16]); rs8 = sb("rs8", [8, 16]); es8 = sb("es8", [8, 16])
        for g in range(2):
            P.op(DVE, lambda e, g=g: e.reduce_max(out=mx8[:, g * 8:(g + 1) * 8], in_=scs[g], axis=AX.X),
                 reads=names(scs[g]), writes=names(mx8))
        ts(nmx8[:], mx8[:], sinkc[:], ALU.max, -1.0, ALU.mult)
        pbs = p_sb[0:8, 0:128]
        ohb = x1t[0:8, :].rearrange("p (b d) -> p b d", b=16)
        for b in range(R):
            act(pbs, scs[b // 8][:, b % 8, :], AF.Exp, bias=nmx8[:, b:b + 1], accum=rs8[:, b:b + 1])
            tr(pTb[1][:, 0, 0:8], pbs, ident_b[0:8, 0:8])
            cp(pT_sb[:, 0, 0:8], pTb[1][:, 0, 0:8], eng=ACT)
            dma(kst[b % 2], sv[b])
            cp(Xm[:], kst[b % 2], eng=POOL)
            mm(pA[3][0:8, 128:256], pT_sb[:, 0, 0:8], Xm[:])
            ts(otmp[0:8, 0:64], pA[3][0:8, 192:256], evm[:, 1:2], ALU.mult)
            stt(ohb[:, b, :], pA[3][0:8, 128:192], evm[:, 0:1], otmp[0:8, 0:64], ALU.mult, ALU.add)
        for b in range(R):
            act(es8[:, b:b + 1], sinkc[:], AF.Exp, bias=nmx8[:, b:b + 1])
        tt(rs8[:], rs8[:], es8[:], ALU.add)
        recip(rs8[:], rs8[:])
        for b in range(R):
            ts(ohb[:, b, :], ohb[:, b, :], rs8[:, b:b + 1], ALU.mult)
        dma(osc[:, :, :], ohb[:], eng=POOL)
        oas = edec[0:R, :]
        oas4 = oas.rearrange("p (c half d) -> p c half d", c=4, half=2)
        for half in range(2):
            dma(oas4[:, :, half, :], osc[half * 4:(half + 1) * 4].rearrange("c b d -> b c d"))
        tt(mix[0:R, 0:512], oas, sza[0:R, :], ALU.mult)
        xs1 = xt[1][0:R, :]
        out_proj(W0o, xsb, xs1, R)
        dma(xs1d[:, :], xs1, eng=POOL)
        memset(xbT2[0][:], 0.0); memset(xbT2[1][:], 0.0); memset(Sdn[:], 0.0); memset(Sdnb[:], 0.0)
        memset(kTa[2][:], 0.0); memset(vtok[2][:], 0.0)

        cstp = [la, sp1, bc_sb]
        bkA = [pA[0], pA[5]]

        def stageA(it):
                xc = xt[it % 2]
                sza, szb, kvg, qTa, xbT = sza2[it % 2], szb2[it % 2], kvg2[it % 2], qTa2[it % 2], xbT2[it % 2]
                vcur, vprev = vtok[it % 3], vtok[(it + 2) % 3]
                kcur, kprev = kTa[it % 3], kTa[(it + 2) % 3]
                dma(xc[:], xp[it * 128:(it + 1) * 128, :])
                norm_and_transpose(xc[:], 128)
                for n, (c0, cw) in enumerate([(0, 512), (512, 512), (1024, 264)]):
                    for k in range(8):
                        mm(bkA[n % 2][:, 0:cw], hT[:, k, :], W0t[:, k, c0:c0 + cw], start=(k == 0), stop=(k == 7))
                    if n == 0:
                        act(sza[:], bkA[0][:], AF.Silu)
                    elif n == 1:
                        act(szb[:], bkA[1][:], AF.Silu)
                    else:
                        cp(kvg[:], bkA[0][:, 0:264])
                cp(vcur[:], kvg[:, 128:256], eng=POOL)
                for c in range(17):
                    bank = bkA[(c + 1) % 2]; sl_ = slice(((c // 2) % 4) * 128, ((c // 2) % 4 + 1) * 128)
                    for k in range(8):
                        mm(bank[:, sl_], W0f[:, k, c * 128:(c + 1) * 128], hT[:, k, :],
                           start=(k == 0), stop=(k == 7))
                    if c < 4:
                        act(qTa[:, c, :], bank[:, sl_], AF.Copy, scale=0.125)
                    elif c == 4:
                        cp(kcur[:], bank[:, sl_])
                    else:
                        cp(xbT[:, c - 5, 3:131], bank[:, sl_], eng=(ACT if c % 2 else DVE))
                if it == NT - 2:
                    dma(pk[0:112, :], kvg[16:128, 0:128], eng=POOL); dma(pv[0:112, :], kvg[16:128, 128:256], eng=POOL)
                if it == NT - 1:
                    dma(pk[112:128, :], kvg[0:16, 0:128], eng=POOL); dma(pv[112:128, :], kvg[0:16, 128:256], eng=POOL)
                    for c in range(12):
                        P.op(PE, lambda e, c=c: e.transpose(out=pA[0][0:3, (c % 4) * 128:(c % 4 + 1) * 128], in_=xbT[:, c, 16:19], identity=ident_f[:, :]),
                             reads=names(xbT, ident_f), writes=names(pA[0]))
                        if c % 4 == 3:
                            cp(cstp[(c - 3) // 4][0:3, :], pA[0][0:3, :])
                    for j_ in range(3):
                        dma(pconv[:, j_ * 512:(j_ + 1) * 512], cstp[j_][0:3, :], eng=POOL)


        def stage_swa(it):
                xc = xt[it % 2]
                sza, szb, kvg, qTa, xbT = sza2[it % 2], szb2[it % 2], kvg2[it % 2], qTa2[it % 2], xbT2[it % 2]
                vcur, vprev = vtok[it % 3], vtok[(it + 2) % 3]
                kcur, kprev = kTa[it % 3], kTa[(it + 2) % 3]
                for p in range(8):
                    c, half = p // 2, p % 2
                    pr = slice(half * 64, half * 64 + 64)
                    sc = pA[1]
                    mm(sc[:, 0:128], qTa[pr, c, :], kprev[pr, :])
                    mm(sc[:, 128:256], qTa[pr, c, :], kcur[pr, :])
                    tt(s_sb[:], sc[:, 0:256], (band0 if it == 0 else band)[:], ALU.add)
                    rmax(mrow[:], s_sb[:])
                    ts(nmrow[:], mrow[:], sink_bc[:, p:p + 1], ALU.max, -1.0, ALU.mult)
                    act(p_sb[:], s_sb[:], AF.Exp, bias=nmrow[:], accum=rsum[:])
                    act(esk[:], sink_bc[:, p:p + 1], AF.Exp, bias=nmrow[:])
                    tt(rden[:], rsum[:], esk[:], ALU.add)
                    recip(rden[:], rden[:])
                    tr(pTb[1][:, 4, :], p_sb[:, 0:128], ident_b[:])
                    tr(pTb[1][:, 5, :], p_sb[:, 128:256], ident_b[:])
                    cp(pT_sb[:], pTb[1][:, 4:6, :], eng=ACT)
                    ob = pA[1]
                    mm(ob[:, 256:320], pT_sb[:, 0, :], vprev[:, half * 64:half * 64 + 64], start=True, stop=False)
                    mm(ob[:, 256:320], pT_sb[:, 1, :], vcur[:, half * 64:half * 64 + 64], start=False, stop=True)
                    stt(mix[:, p * 64:(p + 1) * 64], ob[:, 256:320], rden[:], sza[:, p * 64:(p + 1) * 64], ALU.mult, ALU.mult)


        def stage_pre(it):
                xc = xt[it % 2]
                sza, szb, kvg, qTa, xbT = sza2[it % 2], szb2[it % 2], kvg2[it % 2], qTa2[it % 2], xbT2[it % 2]
                vcur, vprev = vtok[it % 3], vtok[(it + 2) % 3]
                kcur, kprev = kTa[it % 3], kTa[(it + 2) % 3]
                dma(vld[:], valid[it * 128:(it + 1) * 128, :])
                for c in range(12):
                    act(cacc[:, c, :], xbT[:, c, 0:128], AF.Copy, scale=convw[:, 0, c:c + 1])
                    for i in range(1, 4):
                        stt(cacc[:, c, :], xbT[:, c, i:i + 128], convw[:, i, c:c + 1], cacc[:, c, :], ALU.mult, ALU.add)
                cp(xbT2[(it + 1) % 2][:, :, 0:3], xbT[:, :, 128:131], eng=POOL)
                act(cacc[:], cacc[:], AF.Silu)
                act(sq[:], cacc[:, 0:8, :], AF.Square)
                for c in range(8):
                    mm(pA[3 + c // 4][:, (c % 4) * 128:(c % 4 + 1) * 128], ones_b[:], sq[:, c, :])
                for hh in range(2):
                    act(rn[:, hh * 4:(hh + 1) * 4, :], pA[3 + hh][:], AF.Ln, bias=epsc[:])
                    act(rn[:, hh * 4:(hh + 1) * 4, :], rn[:, hh * 4:(hh + 1) * 4, :], AF.Exp, scale=-0.5)
                stt(qTn[:], cacc[:, 0:4, :], 128.0 ** -0.5, rn[:, 0:4, :], ALU.mult, ALU.mult)
                tt(kTn[:], cacc[:, 4:8, :], rn[:, 4:8, :], ALU.mult)
                act(beta[:], kvg[:, 256:260], AF.Sigmoid)
                ts(beta[:], beta[:], vld[:], ALU.mult)
                tt(gtmp[:], kvg[:, 260:264], dtb_bc[:], ALU.add)
                act(gtmp[:], gtmp[:], AF.Exp)
                act(gtmp[:], gtmp[:], AF.Ln, bias=onec[:])
                tt(gval[:], gtmp[:], negA[:], ALU.mult)
                ts(gval[:], gval[:], vld[:], ALU.mult)
                mm(pA[2][:, 0:4], triu_f[:], gval[:])
                mm(pA[2][:, 4:8], ones_f[:], gval[:])
                cp(gc[:], pA[2][:, 0:4])
                ts(ngc[:], pA[2][:, 0:4], -1.0, ALU.mult)
                act(egl[:], pA[2][:, 4:8], AF.Exp)
                tt(edk[:], pA[2][:, 4:8], gc[:], ALU.subtract)
                act(edk[:], edk[:], AF.Exp)
                act(bdk[:], gc[:], AF.Exp)
                tt(bdk[:], bdk[:], beta[:], ALU.mult)
                for h in range(4):
                    ts(gh1[:, h, :], ones_f[:], gval[:, h:h + 1], ALU.mult, eng=POOL)

        def G3(bank):
            return bank[:, :].rearrange("p (h t) -> p h t", h=4)

        def bc_mid(m2):
            return m2.unsqueeze(1).to_broadcast([128, 4, 128])

        def bc_in(v2):
            return v2.unsqueeze(2).to_broadcast([128, 4, 128])

        def mm4(bank, lf, rf, **kw):
            for h in range(4):
                mm(bank[:, h * 128:(h + 1) * 128], lf(h), rf(h), **kw)

        def stage_gdn(it):
            Gb, Vb, Kb, Qb = pA[2], pA[3], pA[3], pA[4]
            tb = pTb[1]
            mm4(Gb, lambda h: gh1[:, h, :], lambda h: triu_f[:])
            stt(Dm4[:], G3(Gb), -1.0, bc_in(gc[:, :]), ALU.mult, ALU.add)
            tt(Dm4[:], Dm4[:], bc_mid(nm_ls[:, :]), ALU.add)
            act(Dm4[:], Dm4[:], AF.Exp)
            tt(DTm4[:], G3(Gb), bc_in(ngc[:, :]), ALU.add)
            tt(DTm4[:], DTm4[:], bc_mid(nm_ui[:, :]), ALU.add)
            act(DTm4[:], DTm4[:], AF.Exp)
            act(egcb4[:], G3(Gb), AF.Exp)
            for h in range(4):
                tr(tb[:, h, :], kTn[:, h, :], ident_b[:])
            tt(kbd4[:], tb[:, 0:4, :], bc_in(bdk[:, :]), ALU.mult)
            tt(kdd4[:], tb[:, 0:4, :], bc_in(edk[:, :]), ALU.mult)
            for h in range(4):
                P.op(PE, lambda e, h=h: e.transpose(out=Vb[:, h * 128:(h + 1) * 128], in_=cacc[:, 8 + h, :], identity=ident_f[:]),
                     reads=names(cacc[:, 8 + h, :], ident_f), writes=names(Vb))
            tt(vbeta4[:], G3(Vb), bc_in(beta[:, :]), ALU.mult)
            mm4(Kb, lambda h: kTn[:, h, :], lambda h: kTn[:, h, :])
            mm4(Qb, lambda h: kTn[:, h, :], lambda h: qTn[:, h, :])
            tt(Dm4[:], Dm4[:], bc_in(beta[:, :]), ALU.mult)
            tt(Nfull4[:], G3(Kb), Dm4[:], ALU.mult)
            tt(NnW[0][:], Nfull4[:], bc_mid(cmask_b[:, 0, :]), ALU.mult)
            tt(QKm4[:], G3(Qb), DTm4[:], ALU.mult)
            tt(qdT4[:], qTn[:], egcb4[:], ALU.mult)
            for h in range(4):
                tr(tb[:, h, :], NnW[0][:, h, :], ident_b[:])
            cp(NtW[0][:], tb[:, 0:4, :], eng=ACT)
            stt(RrW[0][:], NtW[0][:], -1.0, bc_mid(ident_b[:, :]), ALU.mult, ALU.add)
            X, Y, Z = pA[2], pA[3], pA[4]
            ci = 0
            for lvl in range(3):
                mm4(X, lambda h: NtW[ci][:, h, :], lambda h: NnW[ci][:, h, :])
                if lvl < 2:
                    mm4(Y, lambda h: NnW[ci][:, h, :], lambda h: NtW[ci][:, h, :])
                cp(NnW[1 - ci][:], G3(X))
                if lvl < 2:
                    cp(NtW[1 - ci][:], G3(Y), eng=ACT)
                ci = 1 - ci
                mm4(Z, lambda h: NnW[ci][:, h, :], lambda h: RrW[lvl % 2][:, h, :])
                tt(RrW[1 - lvl % 2][:], G3(Z), RrW[lvl % 2][:], ALU.add)
            ri = 1
            for lv in range(3):
                tt(Lm4[:], Nfull4[:], bc_mid(cmask_b[:, 1 + lv, :]), ALU.mult)
                for h in range(4):
                    tr(tb[:, h, :], RrW[ri][:, h, :], ident_b[:])
                cp(Tm4[:], tb[:, 0:4, :], eng=ACT)
                mm4(X, lambda h: Lm4[:, h, :], lambda h: RrW[ri][:, h, :])
                cp(Xm4[:], G3(X))
                mm4(Y, lambda h: Tm4[:, h, :], lambda h: Xm4[:, h, :])
                tt(RrW[1 - ri][:], RrW[ri][:], G3(Y), ALU.subtract)
                ri = 1 - ri
            Tt = RrW[ri]
            mm4(X, lambda h: kbd4[:, h, :], lambda h: Tt[:, h, :])
            ts(nwT4[:], G3(X), -1.0, ALU.mult)
            for h in range(4):
                mm(Y[:, h * 128:(h + 1) * 128], Tt[:, h, :], vbeta4[:, h, :], start=True, stop=False)
                mm(Y[:, h * 128:(h + 1) * 128], nwT4[:, h, :], Sdnb[:, h, :], start=False, stop=True)
            cp(vnew4[:], G3(Y))
            for h in range(4):
                mm(Z[:, h * 128:(h + 1) * 128], qdT4[:, h, :], Sdnb[:, h, :], start=True, stop=False)
                mm(Z[:, h * 128:(h + 1) * 128], QKm4[:, h, :], vnew4[:, h, :], start=False, stop=True)
            cp(osb[:, :], Z[:, :], eng=ACT)
            Wb = pA[2]
            mm4(Wb, lambda h: kdd4[:, h, :], lambda h: vnew4[:, h, :])
            tt(Sdn[:], Sdn[:], bc_in(egl[:, :]), ALU.mult)
            tt(Sdn[:], Sdn[:], G3(Wb), ALU.add)
            cp(Sdnb[:], Sdn[:], eng=ACT)

        def stage_post(it):
                xc = xt[it % 2]
                sza, szb, kvg, qTa, xbT = sza2[it % 2], szb2[it % 2], kvg2[it % 2], qTa2[it % 2], xbT2[it % 2]
                vcur, vprev = vtok[it % 3], vtok[(it + 2) % 3]
                kcur, kprev = kTa[it % 3], kTa[(it + 2) % 3]
                gated_rms(osb, 128, 4, onb_bc, szb, 512)
                out_proj(W0o, xc, x1t, 128, banks=(pA[1], pA[2]))
                dma(x1d[it * 128:(it + 1) * 128, :], x1t[:], eng=SP)


        def stream(fn, *a):
            lst = P.begin(); fn(*a); P.end()
            return lst

        P.ops.extend(stream(stageA, 0))
        for it in range(NT):
            nxt = stream(stageA, it + 1) if it + 1 < NT else []
            n1, n2 = len(nxt) // 5, (4 * len(nxt)) // 5
            P.interleave([nxt[:n1], stream(stage_pre, it)])
            _skip = os.environ.get("KSKIP", "")
            _swa = [] if "swa" in _skip else stream(stage_swa, it)
            _gdn = [] if "gdn" in _skip else stream(stage_gdn, it)
            P.interleave([nxt[n1:n2], _swa, _gdn])
            P.interleave([nxt[n2:], stream(stage_post, it)])

        load_w(W1t, w1t_d, 2560, g1); load_w(W1f, w1f_d, 1040, g1); load_w(W1o, w1o_d, D, None)

        memset(sq[:], 0.0)
        for b in range(R):
            memset(eyeb4[:, b, :, b:b + 1], 1.0)
        xs1b = xt[0][0:R, :]
        dma(xs1b, xs1d[:, :])
        norm_and_transpose(xs1b, R)
        for k in range(8):
            mm(pA[5][0:16, 0:R], W1f[:, k, 1024:1040], hT[:, k, 0:R], start=(k == 0), stop=(k == 7))
        cp(gk_aug[0:16, 0:R], pA[5][0:16, 0:R])
        mm(pA[4][0:R, :], gk_aug[0:17, 0:R], wgk[:, :])
        act(sp1[0:R, :], pA[4][0:R, :], AF.Exp, scale=-1.0)
        act(sp1[0:R, :], sp1[0:R, :], AF.Ln, bias=onec[0:R, :])
        act(la[0:R, :], sp1[0:R, :], AF.Exp, scale=-1.0 / 16.0)
        for k in range(8):
            mm(pA[0][0:R, :], hT[:, k, 0:R], W1f[:, k, 0:512], start=(k == 0), stop=(k == 7))
        stt(bc_sb[0:R, :], pA[0][0:R, :], 128.0 ** -0.5, la[0:R, :], ALU.mult, ALU.mult)
        for k in range(8):
            mm(pA[1][0:R, :], hT[:, k, 0:R], W1t[:, k, 0:512], start=(k == 0), stop=(k == 7))
        cp(edec[0:R, :], pA[1][0:R, :])
        ts(sp1[0:R, :], pA[0][0:R, :], 128.0 ** -0.5, ALU.mult)
        tt(sp1[0:R, :], sp1[0:R, :], edec[0:R, :], ALU.mult)
        P.op(DVE, lambda e: e.reduce_sum(out=kq[0:R, :], in_=sp1[0:R, :].rearrange("p (h t) -> p h t", h=4), axis=AX.X),
             reads=names(sp1), writes=names(kq))
        kb1 = kd1[0:R, :]
        cp(kb1, edec[0:R, :])
        vS1 = o1[0:R, :]
        for n in range(4):
            bank = pA[2 + n % 2]
            for k in range(8):
                mm(bank[0:R, :], hT[:, k, 0:R], W1t[:, k, 512 + n * 512:1024 + n * 512], start=(k == 0), stop=(k == 7))
            if n < 2:
                cp(vS1[:, n * 512:(n + 1) * 512], bank[0:R, :])
            else:
                act(sz1[0:R, (n - 2) * 512:(n - 1) * 512], bank[0:R, :], AF.Silu)
        vb1 = mix[0:R, :]
        cp(vb1, vS1)
        for h in range(4):
            P.op(PE, lambda e, h=h: e.transpose(out=pA[5][:, h * 16:(h + 1) * 16], in_=bc_sb[0:R, h * 128:(h + 1) * 128], identity=ident_f[0:R, 0:R]),
                 reads=names(bc_sb, ident_f), writes=names(pA[5]))
            P.op(PE, lambda e, h=h: e.transpose(out=pA[5][:, 64 + h * 16:64 + (h + 1) * 16], in_=la[0:R, h * 128:(h + 1) * 128], identity=ident_f[0:R, 0:R]),
                 reads=names(la, ident_f), writes=names(pA[5]))
        cp(qTn[:, :, 0:R], pA[5][:, 0:64].rearrange("p (c t) -> p c t", c=4))
        aT = dtmp[:, 0:64]
        cp(aT, pA[5][:, 64:128])
        qTm1 = hT[:, :, :].rearrange("p a b -> p (a b)").rearrange("p (b h t) -> p b h t", b=16, h=4)
        for b in range(R):
            tt(qTm1[:, b, :, :], qTn[:, :, 0:R], eyeb4[:, b, :, :], ALU.mult)
        Sr1 = [Sg[:], rn[:, :, :].rearrange("p (h a) t -> p h (a t)", h=4)]
        Sb1 = [Sgb[:], v1[:, :].rearrange("p (h v) -> p h v", h=4)]
        kmr1 = [QKm1[0:R, :], vbeta[0:R, :]]
        for b in range(R):
            Sb_, Sbb = Sr1[b % 2], Sb1[b % 2]
            dma(Sb_, sgla[b].rearrange("h k v -> k h v"))
            cp(Sbb, Sb_, eng=ACT)
            for h in range(4):
                mm(pA[h][0:R, 0:256], qTm1[:, b, h, :], Sbb[:, h, :], start=(b == 0), stop=(b == R - 1))
                km = kmr1[(b * 4 + h) % 2]
                ts(km, kb1[:, h * 128:(h + 1) * 128], ident_f[0:R, b:b + 1], ALU.mult)
                up = pA[4 + h % 2]
                mm(up[:, 0:256], km, vb1[:, h * 256:(h + 1) * 256])
                stt(Sb_[:, h, :], Sb_[:, h, :], aT[:, h * 16 + b:h * 16 + b + 1], up[:, 0:256], ALU.mult, ALU.add)
            dma(sgla_o[b].rearrange("h k v -> k h v"), Sb_, eng=POOL)
        og1 = x1t[0:R, :]
        for h in range(4):
            ts(otmp[0:R, :], vS1[:, h * 256:(h + 1) * 256], kq[0:R, h:h + 1], ALU.mult)
            tt(og1[:, h * 256:(h + 1) * 256], pA[h][0:R, 0:256], otmp[0:R, :], ALU.add)
        gated_rms(og1, 256, 4, onc_bc, sz1, 0, R=R)
        xs2 = xt[1][0:R, :]
        out_proj(W1o, xs1b, xs2, R)
        act(junk[0:R, :], xs2, AF.Square, accum=ss[0:R, :])
        rstd_from_ss(rstd[0:R, :], ss[0:R, :], 1.0 / D)
        stt(og1, xs2, rstd[0:R, :], fn_bc[0:R, :], ALU.mult, ALU.mult)
        dma(ys[:, :], og1, eng=POOL)
        memset(Sg[:], 0.0); memset(Sgb[:], 0.0)

        r1b = xbT2[0][:, :, :].rearrange("p c t -> p (c t)").bitcast(BF16)
        r2b = xbT2[1][:, :, :].rearrange("p c t -> p (c t)").bitcast(BF16)
        f512 = lambda t_: t_[:, :, :].rearrange("p h t -> p (h t)")
        L1S = [dict(hb=hb, hT=hT, la=la, sp1=sp1, bc_sb=bc_sb, edec=edec, ebT=ebT, einvT=einvT, qdT1=qdT1, kinvT1=kinvT1,
                    kd1=kd1, v1=v1, sz1=sz1, gk_aug=gk_aug),
               dict(hb=r1b[:, 0:1024], hT=r1b[:, 1024:2048].rearrange("p (k t) -> p k t", k=8),
                    la=f512(Dm4), sp1=f512(DTm4), bc_sb=f512(egcb4), edec=f512(rn[:, 0:4, :]), ebT=rn[:, 4:8, :], einvT=gh1[:, :, :],
                    qdT1=NnW[0][:, :, :], kinvT1=NnW[1][:, :, :], kd1=f512(NtW[0]), v1=r2b[:, 0:1024], sz1=r2b[:, 1024:2048],
                    gk_aug=RrW[0][0:32, 0, :])]
        memset(RrW[0][0:32, 0, :], 1.0)
        for it in range(NT):
            _S = L1S[it % 2]
            hb, hT, la, sp1, bc_sb, edec, ebT, einvT = _S["hb"], _S["hT"], _S["la"], _S["sp1"], _S["bc_sb"], _S["edec"], _S["ebT"], _S["einvT"]
            qdT1, kinvT1, kd1, v1, sz1, gk_aug = _S["qdT1"], _S["kinvT1"], _S["kd1"], _S["v1"], _S["sz1"], _S["gk_aug"]
            junk = hb
            x1t = xt[it % 2]
            dma(x1t[:], x1d[it * 128:(it + 1) * 128, :])
            dma(vld[:], valid[it * 128:(it + 1) * 128, :])
            norm_and_transpose(x1t[:], 128)
            for k in range(8):
                mm(pA[2][0:16, 0:128], W1f[:, k, 1024:1040], hT[:, k, :], start=(k == 0), stop=(k == 7))
            cp(gk_aug[0:16, :], pA[2][0:16, 0:128])
            mm(pA[1][:], gk_aug[0:17, :], wgk[:, :])
            act(sp1[:], pA[1][:], AF.Exp, scale=-1.0)
            act(sp1[:], sp1[:], AF.Ln, bias=onec[:])
            ts(la[:], sp1[:], vld[:], ALU.mult, -1.0 / 16.0, ALU.mult)
            mm(pA[0][:], triu_f[:], la[:])
            mm(pA[1][:], ones_f[:], la[:])
            cp(bc_sb[:], pA[0][:], eng=ACT)
            tt(edec[:], pA[1][:], bc_sb[:], ALU.subtract)
            act(edec[:], edec[:], AF.Exp)
            for h in range(4):
                mm(pA[2][:, h * 128:(h + 1) * 128], la[:, h * 128:(h + 1) * 128], triu_f[:])
            act(ebT[:], pA[2][:], AF.Exp)
            act(einvT[:], pA[2][:], AF.Exp, scale=-1.0)
            for c in range(8):
                bank = pA[c // 4]
                for k in range(8):
                    mm(bank[:, (c % 4) * 128:(c % 4 + 1) * 128], W1f[:, k, c * 128:(c + 1) * 128], hT[:, k, :],
                       start=(k == 0), stop=(k == 7))
            stt(qdT1[:], pA[0][:], 128.0 ** -0.5, ebT[:], ALU.mult, ALU.mult)
            tt(kinvT1[:], pA[1][:], einvT[:], ALU.mult)
            kraw = cacc[:, 8:12, :]
            cp(kraw, pA[1][:, :].rearrange("p (h t) -> p h t", h=4), eng=ACT)
            for h in range(4):
                P.op(PE, lambda e, h=h, kraw=kraw: e.transpose(out=pA[2][:, h * 128:(h + 1) * 128], in_=kraw[:, h, :], identity=ident_f[:]),
                     reads=names(kraw[:, h, :], ident_f), writes=names(pA[2]))
            tt(kd1[:], pA[2][:], edec[:], ALU.mult)
            for n in range(1, 5):
                bank = pA[n % 3]
                for k in range(8):
                    mm(bank[:], hT[:, k, :], W1t[:, k, n * 512:(n + 1) * 512], start=(k == 0), stop=(k == 7))
                if n < 3:
                    cp(v1[:, (n - 1) * 512:n * 512], bank[:], eng=ACT)
                else:
                    act(sz1[:, (n - 3) * 512:(n - 2) * 512], bank[:], AF.Silu)
            for h in range(4):
                wk = pA[3 + h % 2]
                mm(wk[:, 0:128], kinvT1[:, h, :], qdT1[:, h, :])
                tt(QKm1[:], wk[:, 0:128], triu_f[:], ALU.mult)
                ob = pA[5]
                mm(ob[:, 0:256], QKm1[:], v1[:, h * 256:(h + 1) * 256], start=True, stop=False)
                mm(ob[:, 0:256], qdT1[:, h, :], Sgb[:, h, :], start=False, stop=True)
                cp(o1[:, h * 256:(h + 1) * 256], ob[:, 0:256], eng=ACT)
                mm(wk[:, 256:512], kd1[:, h * 128:(h + 1) * 128], v1[:, h * 256:(h + 1) * 256])
                stt(Sg[:, h, :], Sg[:, h, :], ebT[:, h, 127:128], wk[:, 256:512], ALU.mult, ALU.add)
                cp(Sgb[:, h, :], Sg[:, h, :], eng=ACT)
            gated_rms(o1, 256, 4, onc_bc, sz1, 0)
            out_proj(W1o, x1t, x2t, 128, banks=(pA[3], pA[4]))
            act(junk[:], x2t[:], AF.Square, accum=ss[:])
            rstd_from_ss(rstd[:], ss[:], 1.0 / D)
            stt(yt[:], x2t[:], rstd[:], fn_bc[:], ALU.mult, ALU.mult)
            dma(yp[it * 128:(it + 1) * 128, :], yt[:], eng=SP)

        _S = L1S[0]
        hb, hT, la, sp1, bc_sb, edec, ebT, einvT = _S["hb"], _S["hT"], _S["la"], _S["sp1"], _S["bc_sb"], _S["edec"], _S["ebT"], _S["einvT"]
        qdT1, kinvT1, kd1, v1, sz1, gk_aug = _S["qdT1"], _S["kinvT1"], _S["kd1"], _S["v1"], _S["sz1"], _S["gk_aug"]
        junk = hb
        dma(pdn.rearrange("h k v -> k h v"), Sdn[:], eng=POOL)
        dma(pgla.rearrange("h k v -> k h v"), Sg[:], eng=POOL)

        global _P
        _P = P
        P.warm_ap = ident_b[:]
        n = P.emit(nc, st)
    return nc


_NC = None


def _get_nc():
    global _NC
    if _NC is None:
        _NC = build_nc()
    return _NC


def kernel(x_prompt, x_sample, cache_swa_k, cache_swa_v, state_dn_conv, state_dn, state_gla,
           meta_tokens, norm_ab, w_in_ab, sink_a, conv_b, a_log_b, dt_bias_b, onorm_b, w_out_ab,
           norm_c, w_in_c, w_gk_up, b_gk, onorm_c, w_out_c, final_norm):
    f = lambda a: np.ascontiguousarray(np.asarray(a, dtype=np.float32))
    x_prompt, x_sample = f(x_prompt), f(x_sample)
    w0, w1 = f(w_in_ab)[0], f(w_in_c)[0]
    o = np.cumsum([0, 512, 128, 128, 512, 1536, 512, 4, 4])
    qa, ka, va, za, xb, zb, bb, ab = [w0[:, o[i]:o[i + 1]] for i in range(8)]
    perm = np.concatenate([np.arange(h * 64, h * 64 + 64) for h in HP])
    qa_p, za_p = qa[:, perm], za[:, perm]
    w0t = np.concatenate([za_p, zb, ka, va, bb, ab], axis=1)
    w0f = np.concatenate([qa_p, ka, xb], axis=1)
    w0o = f(w_out_ab)[0].copy()
    w0o[0:512] = w0o[0:512][perm]
    o1 = np.cumsum([0, 512, 512, 1024, 1024, 16])
    qc, kc, vc, zc, gkl = [w1[:, o1[i]:o1[i + 1]] for i in range(5)]
    w1t = np.concatenate([kc, vc, zc], axis=1)
    w1f = np.concatenate([qc, kc, gkl], axis=1)
    wgk = np.concatenate([f(w_gk_up)[0], f(b_gk)[0][None, :]], axis=0)
    sink_p = f(sink_a)[0][HP][None, :]
    common = dict(
        w0t=f(w0t), w0f=f(w0f), w0o=f(w0o), w1t=f(w1t), w1f=f(w1f), w1o=f(w_out_c)[0], wgk=f(wgk),
        g0=f(norm_ab)[0][:, None], g1=f(norm_c)[0][:, None], fn=f(final_norm)[None, :],
        sink=f(sink_p), convw=f(conv_b)[0], alog=f(a_log_b), dtb=f(dt_bias_b), onb=f(onorm_b), onc=f(onorm_c),
    )
    ii = np.arange(128)
    same = lambda b: ((ii[:, None] // b) == (ii[None, :] // b)).astype(np.float32)
    cmask = np.stack([same(16), same(32) - same(16), same(64) - same(32), same(128) - same(64)])
    common["cmask"] = cmask
    meta = f(meta_tokens)
    NT = NT_FULL; NTOK = NT * 128
    valid = np.zeros((NTOK, 1), np.float32); valid[:8208] = 1.0
    in_maps = []
    for c in range(8):
        b = c // 4
        xpc = np.zeros((NTOK, D), np.float32)
        xpc[:16] = meta; xpc[16:8208] = x_prompt[b]
        sl = slice(c * SB_, (c + 1) * SB_)
        m = dict(common)
        m.update(xp=xpc, valid=valid, xs=f(x_sample[sl, 0, :]),
                 ck=f(np.asarray(cache_swa_k)[0, sl].reshape(SB_, 128, 128)),
                 cv=f(np.asarray(cache_swa_v)[0, sl].reshape(SB_, 128, 128)),
                 cconv=f(np.asarray(state_dn_conv)[0, sl]), sdn=f(np.asarray(state_dn)[0, sl]),
                 sgla=f(np.asarray(state_gla)[0, sl]))
        in_maps.append(m)
    nc = _get_nc()
    res = run_bass_kernel_spmd(nc, in_maps, core_ids=list(range(8))).results
    R = lambda k, cs: [np.asarray(res[c][k]) for c in cs]
    y_prompt = np.stack([r[16:8208] for r in R("yp", [0, 4])])
    y_sample = np.concatenate(R("ys", range(8)))[:, None, :]
    pk = np.stack(R("pk", [0, 4])).reshape(1, 2, 128, 2, 64)
    pv = np.stack(R("pv", [0, 4])).reshape(1, 2, 128, 2, 64)
    pconv = np.stack(R("pconv", [0, 4]))[None]
    pdn = np.stack(R("pdn", [0, 4]))[None]
    pgla = np.stack(R("pgla", [0, 4]))[None]
    sk = np.concatenate(R("sk", range(8))).reshape(1, 128, 128, 2, 64)
    sv = np.concatenate(R("sv", range(8))).reshape(1, 128, 128, 2, 64)
    sconv = np.concatenate(R("sconv", range(8)))[None]
    sdn_o = np.concatenate(R("sdn_o", range(8)))[None]
    sgla_o = np.concatenate(R("sgla_o", range(8)))[None]
    outs = (y_prompt, y_sample, pk, pv, pconv, pdn, pgla, sk, sv, sconv, sdn_o, sgla_o)
    return tuple(np.ascontiguousarray(a, dtype=np.float32) for a in outs)
```

```python
import os
import numpy as np
from contextlib import ExitStack
import concourse.bass as bass
import concourse.mybir as mybir
from concourse.bass_utils import run_bass_kernel_spmd

F32 = mybir.dt.float32
BF16 = mybir.dt.bfloat16
AF = mybir.ActivationFunctionType
ALU = mybir.AluOpType
AX = mybir.AxisListType

PE, ACT, DVE, POOL, SP = "tensor", "scalar", "vector", "gpsimd", "sync"
COMPUTE = (PE, ACT, DVE, POOL)
SEM_LIM = 30000
N_DMA_SEMS = 16

D = 1024
NT_FULL = 65
SB_ = 16
NEG = -30000.0
HP = [0, 4, 1, 5, 2, 6, 3, 7]


class Prog:
    def __init__(self):
        self.ops = []
        self.cur = self.ops

    def begin(self):
        self.cur = []
        return self.cur

    def end(self):
        self.cur = self.ops

    def interleave(self, streams):
        its = [list(x) for x in streams if x]
        pos = [0] * len(its)
        left = sum(len(x) for x in its)
        while left:
            for k, x in enumerate(its):
                if pos[k] < len(x):
                    self.ops.append(x[pos[k]]); pos[k] += 1; left -= 1

    def op(self, eng, fn, reads=(), writes=()):
        import sys
        f = sys._getframe(1)
        self.cur.append(dict(eng=eng, fn=fn, reads=tuple(reads), writes=tuple(writes), dma=False, line=f.f_lineno))

    def dma(self, eng, fn, reads=(), writes=()):
        self.cur.append(dict(eng=eng, fn=fn, reads=tuple(reads), writes=tuple(writes), dma=True))

    def emit(self, nc, stack):
        import os
        ops = self.ops
        mx = int(os.environ.get('KMAXOPS', '0'))
        if mx:
            ops = ops[:mx]
        n = len(ops)
        print('n_ops', n)
        WHOLE = (0, 1 << 30, 0, 1 << 30)

        def norm(k):
            if isinstance(k, str):
                return (k,) + WHOLE
            return k

        def overlap(r1, r2):
            return r1[0] < r2[1] and r2[0] < r1[1] and r1[2] < r2[3] and r2[2] < r1[3]

        def covers(big, small):
            return big[0] <= small[0] and big[1] >= small[1] and big[2] <= small[2] and big[3] >= small[3]

        def analyze(ops, dedup):
            n = len(ops)
            recs = {}
            full = [None] * n
            deps = [None] * n
            needed = [False] * n
            for i, o in enumerate(ops):
                acc = []
                for k in o["reads"]:
                    k = norm(k)
                    if k[0].startswith("p_"):
                        acc.append((k[0], WHOLE, True))
                    else:
                        acc.append((k[0], k[1:], False))
                for k in o["writes"]:
                    k = norm(k)
                    acc.append((k[0], WHOLE if k[0].startswith("p_") else k[1:], True))
                d = set()
                for name, rect, isw in acc:
                    for r in recs.get(name, ()):
                        if (isw or r[2]) and overlap(rect, r[0]):
                            d.add(r[1])
                d.discard(i)
                full[i] = d
                best, dl = {}, []
                for j in d:
                    oj = ops[j]
                    if oj["dma"]:
                        dl.append(j)
                    else:
                        e = oj["eng"]
                        if e == PE and o["eng"] == PE and not o["dma"]:
                            continue
                        if e not in best or best[e] < j:
                            best[e] = j
                dl.extend(best.values())
                deps[i] = dl
                for j in dl:
                    needed[j] = True
                for name, rect, isw in acc:
                    lst = recs.setdefault(name, [])
                    if isw:
                        lst[:] = [r for r in lst if not covers(rect, r[0])]
                    elif dedup and not o["dma"]:
                        lst[:] = [r for r in lst if not (not r[2] and r[0] == rect and not ops[r[1]]["dma"]
                                                         and ops[r[1]]["eng"] == o["eng"])]
                    lst.append([rect, i, isw])
            return full, deps, needed

        if os.environ.get("KSCHED", "1") == "1":
            import heapq
            full, _, _ = analyze(ops, False)
            dur = [0.0] * n
            for i, o in enumerate(ops):
                ext = 512
                if o["writes"] and not isinstance(o["writes"][0], str):
                    w0 = o["writes"][0]
                    ext = (w0[4] - w0[3]) // 4
                e = o["eng"]
                if o["dma"]:
                    dur[i] = 2.5
                elif e == PE:
                    dur[i] = 0.11 + 0.0005 * ext
                elif e == ACT:
                    dur[i] = 0.30 + 0.00085 * ext
                elif e == DVE:
                    dur[i] = 0.25 + 0.0011 * ext
                else:
                    dur[i] = 0.35 + 0.002 * ext
            succ = [[] for _ in range(n)]
            indeg = [0] * n
            for i in range(n):
                for j in full[i]:
                    succ[j].append(i)
                indeg[i] = len(full[i])
            cpl = [0.0] * n
            for i in range(n - 1, -1, -1):
                m = 0.0
                for k in succ[i]:
                    if cpl[k] > m:
                        m = cpl[k]
                cpl[i] = dur[i] + m
            engs = (PE, ACT, DVE, POOL, SP)
            fut = {e: [] for e in engs}
            av = {e: [] for e in engs}
            efree = {e: 0.0 for e in engs}
            ready_t = [0.0] * n
            avail_t = [0.0] * n
            for i in range(n):
                if indeg[i] == 0:
                    heapq.heappush(fut[ops[i]["eng"]], (0.0, i))
            order = []
            done = 0
            while done < n:
                bs, be, bi = None, None, None
                for e in engs:
                    f_, a_ = fut[e], av[e]
                    while f_ and f_[0][0] <= efree[e]:
                        rt, i = heapq.heappop(f_)
                        heapq.heappush(a_, (-cpl[i], i))
                    if a_:
                        cs = efree[e]
                    elif f_:
                        cs = f_[0][0]
                    else:
                        continue
                    if bs is None or cs < bs:
                        bs, be = cs, e
                e = be
                if av[e]:
                    _, i = heapq.heappop(av[e])
                else:
                    _, i = heapq.heappop(fut[e])
                st_ = bs
                o = ops[i]
                if o["dma"]:
                    efree[e] = st_ + 0.15
                    avail_t[i] = st_ + dur[i]
                else:
                    efree[e] = st_ + dur[i]
                    avail_t[i] = st_ + dur[i]
                order.append((st_, done, i))
                done += 1
                for k in succ[i]:
                    lat = 0.1 if (ops[k]["eng"] == e and not o["dma"]) else 0.15
                    t_ = avail_t[i] + lat
                    if t_ > ready_t[k]:
                        ready_t[k] = t_
                    indeg[k] -= 1
                    if indeg[k] == 0:
                        heapq.heappush(fut[ops[k]["eng"]], (ready_t[k], k))
            order.sort()
            ops_o = ops
            ops = [ops[i] for _, _, i in order]
            print("sched makespan est (us):", max(avail_t), "crit path:", max(cpl))
            _ld = {}
            for i, o in enumerate(self.ops if not mx else ops):
                pass
            for (st_, _, i) in order:
                pass
            for e in engs:
                print("  load", e, sum((0.15 if ops_o[i]["dma"] else dur[i]) for i in range(n) if ops_o[i]["eng"] == e))
        _, deps, needed = analyze(ops, True)
        eng_sems = {e: [] for e in COMPUTE}
        eng_cnt = {e: 0 for e in COMPUTE}
        qs_ = (SP, POOL, ACT)
        dma_sems = {q: [stack.enter_context(nc.semaphore("dma_%s%d" % (q, i))) for i in range(N_DMA_SEMS)] for q in qs_}
        dma_tot = {q: [0] * N_DMA_SEMS for q in qs_}
        dma_last = {q: [None] * N_DMA_SEMS for q in qs_}
        dma_rr = {q: 0 for q in qs_}
        tag = [None] * n
        waited = {e: {} for e in (PE, ACT, DVE, POOL, SP)}
        handles = {PE: nc.tensor, ACT: nc.scalar, DVE: nc.vector, POOL: nc.gpsimd, SP: nc.sync}

        import os as _os
        n_warm = int(_os.environ.get("KWARM", "0"))
        warm_ap = getattr(self, "warm_ap", None)

        def do_wait(e, sv):
            sem, val = sv
            key = id(sem)
            w = waited[e]
            if w.get(key, 0) >= val:
                return
            w[key] = val
            if e == PE and n_warm and warm_ap is not None:
                for _ in range(n_warm):
                    nc.tensor.ldweights(warm_ap)
            handles[e].wait_ge(sem, val)

        for i, o in enumerate(ops):
            e = o["eng"]
            for j in deps[i]:
                do_wait(e, tag[j])
            if o["dma"]:
                s = dma_rr[e]
                dma_rr[e] = (s + 1) % N_DMA_SEMS
                if dma_last[e][s] is not None:
                    do_wait(e, tag[dma_last[e][s]])
                ins = o["fn"](handles[e])
                dma_tot[e][s] += 16
                ins.then_inc(dma_sems[e][s], 16)
                tag[i] = (dma_sems[e][s], dma_tot[e][s])
                dma_last[e][s] = i
            else:
                ins = o["fn"](handles[e])
                if needed[i]:
                    c = eng_cnt[e]
                    si = c // SEM_LIM
                    while len(eng_sems[e]) <= si:
                        eng_sems[e].append(stack.enter_context(nc.semaphore("%s_p%d" % (e, len(eng_sems[e])))))
                    ins.then_inc(eng_sems[e][si], 1)
                    eng_cnt[e] = c + 1
                    tag[i] = (eng_sems[e][si], c % SEM_LIM + 1)
        for q in qs_:
            for s in range(N_DMA_SEMS):
                if dma_last[q][s] is not None:
                    do_wait(SP, tag[dma_last[q][s]])
        return n


def build_nc(NT=NT_FULL):
    NTOK = NT * 128
    nc = bass.Bass("TRN2", target_bir_lowering=False)
    P = Prog()

    def din(name, shape):
        return nc.dram_tensor(name, list(shape), F32, kind="ExternalInput").ap()

    def dout(name, shape):
        return nc.dram_tensor(name, list(shape), F32, kind="ExternalOutput").ap()

    xp = din("xp", [NTOK, D]); valid = din("valid", [NTOK, 1]); xs = din("xs", [SB_, D])
    ck = din("ck", [SB_, 128, 128]); cv = din("cv", [SB_, 128, 128]); cconv = din("cconv", [SB_, 3, 1536])
    sdn = din("sdn", [SB_, 4, 128, 128]); sgla = din("sgla", [SB_, 4, 128, 256])
    w0t_d = din("w0t", [D, 1288]); w0f_d = din("w0f", [D, 2176]);
    w0o_d = din("w0o", [D, D])
    w1t_d = din("w1t", [D, 2560]); w1f_d = din("w1f", [D, 1040]); w1o_d = din("w1o", [D, D])
    wgk_d = din("wgk", [17, 512])
    g0_d = din("g0", [D, 1]); g1_d = din("g1", [D, 1]); fn_d = din("fn", [1, D])
    sink_d = din("sink", [1, 8]); convw_d = din("convw", [4, 1536]); alog_d = din("alog", [1, 4]); dtb_d = din("dtb", [1, 4])
    onb_d = din("onb", [1, 128]); onc_d = din("onc", [1, 256])
    cmask_d = din("cmask", [4, 128, 128])

    yp = dout("yp", [NTOK, D]); ys = dout("ys", [SB_, D])
    pk = dout("pk", [128, 128]); pv = dout("pv", [128, 128]); pconv = dout("pconv", [3, 1536])
    pdn = dout("pdn", [4, 128, 128]); pgla = dout("pgla", [4, 128, 256])
    sk = dout("sk", [SB_, 128, 128]); sv = dout("sv", [SB_, 128, 128]); sconv = dout("sconv", [SB_, 3, 1536])
    sdn_o = dout("sdn_o", [SB_, 4, 128, 128]); sgla_o = dout("sgla_o", [SB_, 4, 128, 256])

    with ExitStack() as st:
        def sb(name, shape, dt=F32):
            return st.enter_context(nc.sbuf_tensor("s_" + name, list(shape), dt))

        def psb(name, shape, dt=F32):
            return st.enter_context(nc.psum_tensor("p_" + name, list(shape), dt))

        def names(*aps):
            out = []
            for a in aps:
                if a is None or isinstance(a, (int, float)):
                    continue
                try:
                    apl = a.ap
                    ps, pc = apl[0]
                    off = a.offset
                    if ps <= 0:
                        raise ValueError
                    plo = off // ps
                    flo = off % ps
                    ext = 1
                    for st_, cn in apl[1:]:
                        ext += (cn - 1) * abs(st_)
                    if flo + ext > ps:
                        raise ValueError
                    esz = 2 if a.dtype == BF16 else 4
                    out.append((a.name, plo, plo + pc, flo * esz, (flo + ext) * esz))
                except Exception:
                    out.append(a.name)
            return out

        def mm(out, lhsT, rhs, start=True, stop=True):
            P.op(PE, lambda e: e.matmul(out, lhsT=lhsT, rhs=rhs, start=start, stop=stop),
                 reads=names(lhsT, rhs), writes=names(out))

        def tr(out, in_, ident):
            P.op(PE, lambda e: e.transpose(out=out, in_=in_, identity=ident), reads=names(in_, ident), writes=names(out))

        def act(out, in_, func, bias=None, scale=None, accum=None, eng=ACT):
            kw = {}
            if bias is not None:
                kw["bias"] = bias
            if scale is not None:
                kw["scale"] = scale
            if accum is not None:
                kw["accum_out"] = accum
            P.op(eng, lambda e: e.activation(out=out, in_=in_, func=func, **kw),
                 reads=names(in_, bias, scale), writes=names(out, accum))

        def tt(out, in0, in1, op, eng=DVE):
            P.op(eng, lambda e: e.tensor_tensor(out=out, in0=in0, in1=in1, op=op), reads=names(in0, in1), writes=names(out))

        def ts(out, in0, s1, op0, s2=None, op1=None, eng=DVE, accum=None):
            kw = {}
            if op1 is not None:
                kw["op1"] = op1
            if accum is not None:
                kw["accum_out"] = accum
            P.op(eng, lambda e: e.tensor_scalar(out=out, in0=in0, scalar1=s1, scalar2=s2, op0=op0, **kw),
                 reads=names(in0, s1, s2), writes=names(out, accum))

        def stt(out, in0, scalar, in1, op0, op1, eng=DVE):
            P.op(eng, lambda e: e.scalar_tensor_tensor(out=out, in0=in0, scalar=scalar, in1=in1, op0=op0, op1=op1),
                 reads=names(in0, scalar, in1), writes=names(out))

        def cp(out, in_, eng=DVE):
            if eng == ACT:
                P.op(ACT, lambda e: e.copy(out=out, in_=in_), reads=names(in_), writes=names(out))
            else:
                P.op(eng, lambda e: e.tensor_copy(out=out, in_=in_), reads=names(in_), writes=names(out))

        def memset(ap, v, eng=POOL):
            P.op(eng, lambda e: e.memset(ap, v), writes=names(ap))

        def asel(out, in_, pattern, cmp, fill, base, cm):
            P.op(POOL, lambda e: e.affine_select(out=out, in_=in_, pattern=pattern, compare_op=cmp, fill=fill,
                                                 base=base, channel_multiplier=cm), reads=names(in_), writes=names(out))

        def rmax(out, in_):
            P.op(DVE, lambda e: e.reduce_max(out=out, in_=in_, axis=AX.X), reads=names(in_), writes=names(out))

        def recip(out, in_):
            P.op(DVE, lambda e: e.reciprocal(out=out, in_=in_), reads=names(in_), writes=names(out))

        def dma(out, in_, eng=SP):
            P.dma(eng, lambda e: e.dma_start(out=out, in_=in_, allow_slow_non_contiguous=True), reads=names(in_), writes=names(out))

        def rstd_from_ss(out, ss, inv_n, eps=1e-6):
            act(out, ss, AF.Ln, bias=epsc[0:out.shape[0], :], scale=inv_n)
            act(out, out, AF.Exp, scale=-0.5)

        epsc = sb("epsc", [128, 1]); memset(epsc[:], 1e-6)
        onec = sb("onec", [128, 1]); memset(onec[:], 1.0)
        ones_f = sb("ones_f", [128, 128]); memset(ones_f[:], 1.0)
        zeros_f = sb("zeros_f", [128, 256]); memset(zeros_f[:], 0.0)
        ones_b = sb("ones_b", [128, 128], BF16); cp(ones_b[:], ones_f[:])
        ident_f = sb("ident_f", [128, 128])
        asel(ident_f[:], ones_f[:], [[-1, 128]], ALU.is_equal, 0.0, 0, 1)
        ident_b = sb("ident_b", [128, 128], BF16); cp(ident_b[:], ident_f[:])
        triu_f = sb("triu_f", [128, 128])
        asel(triu_f[:], ones_f[:], [[1, 128]], ALU.is_ge, 0.0, 0, -1)
        nm_ls = sb("nm_ls", [128, 128])
        asel(nm_ls[:], zeros_f[:, 0:128], [[-1, 128]], ALU.is_gt, NEG, 0, 1)
        nm_ui = sb("nm_ui", [128, 128])
        asel(nm_ui[:], zeros_f[:, 0:128], [[1, 128]], ALU.is_ge, NEG, 0, -1)
        band = sb("band", [128, 256])
        asel(band[:], zeros_f[:], [[1, 256]], ALU.is_ge, NEG, -1, -1)
        asel(band[:], band[:], [[-1, 256]], ALU.is_ge, NEG, 128, 1)
        band0 = sb("band0", [128, 256])
        cp(band0[:], band[:], eng=POOL)
        memset(band0[:, 0:128], NEG)

        sink_bc = sb("sink_bc", [128, 8]); dma(sink_bc[:], sink_d.partition_broadcast(128))
        alog_bc = sb("alog_bc", [128, 4]); dma(alog_bc[:], alog_d.partition_broadcast(128))
        dtb_bc = sb("dtb_bc", [128, 4]); dma(dtb_bc[:], dtb_d.partition_broadcast(128))
        onb_bc = sb("onb_bc", [128, 128]); dma(onb_bc[:], onb_d.partition_broadcast(128))
        onc_bc = sb("onc_bc", [128, 256]); dma(onc_bc[:], onc_d.partition_broadcast(128))
        fn_bc = sb("fn_bc", [128, D]); dma(fn_bc[:], fn_d.partition_broadcast(128))
        negA = sb("negA", [128, 4])
        act(negA[:], alog_bc[:], AF.Exp)
        ts(negA[:], negA[:], -1.0, ALU.mult)
        convw = sb("convw", [128, 4, 12])
        for i in range(4):
            P.dma(SP, lambda e, i=i: e.dma_start(out=convw[:, i, :], in_=convw_d[i, :].rearrange("(c p) -> p c", p=128),
                                                 allow_slow_non_contiguous=True), reads=[], writes=names(convw))
        g0 = sb("g0", [128, 8]); dma(g0[:], g0_d.rearrange("(k p) o -> p (k o)", p=128))
        g1 = sb("g1", [128, 8]); dma(g1[:], g1_d.rearrange("(k p) o -> p (k o)", p=128))
        wgk = sb("wgk", [17, 512], BF16)
        wgk_f = sb("wgk_f", [17, 512]); dma(wgk_f[:], wgk_d[:, :]); cp(wgk[:], wgk_f[:])

        xt = [sb("xt%d" % i, [128, D]) for i in range(2)]
        x1t = sb("x1t", [128, D])
        stage = [x1t[:, :].rearrange("p (k n) -> p k n", k=8), xt[1][:, :].rearrange("p (k n) -> p k n", k=8)]
        stg_i = [0]

        def load_w(dst, src, ncols, g):
            for c0 in range(0, ncols, 128):
                cw = min(128, ncols - c0)
                s = stage[stg_i[0] % 2]; stg_i[0] += 1
                dma(s[:, :, 0:cw], src[:, c0:c0 + cw].rearrange("(k p) n -> p k n", p=128))
                for k in range(8):
                    e_ = (DVE, ACT, DVE, POOL)[(k + stg_i[0]) % 4]
                    if g is None:
                        cp(dst[:, k, c0:c0 + cw], s[:, k, 0:cw], eng=e_)
                    elif e_ == ACT:
                        act(dst[:, k, c0:c0 + cw], s[:, k, 0:cw], AF.Copy, scale=g[:, k:k + 1])
                    else:
                        ts(dst[:, k, c0:c0 + cw], s[:, k, 0:cw], g[:, k:k + 1], ALU.mult, eng=e_)

        Wall = sb("Wall", [128, 8, 4624], BF16)
        W0t, W0f, W0o = Wall[:, :, 0:1288], Wall[:, :, 1288:3464], Wall[:, :, 3464:4488]
        W1t, W1f, W1o = Wall[:, :, 0:2560], Wall[:, :, 2560:3600], Wall[:, :, 3600:4624]
        load_w(W0t, w0t_d, 1288, g0); load_w(W0f, w0f_d, 2176, g0); load_w(W0o, w0o_d, D, None)
        x1d = nc.dram_tensor("x1d", [NTOK, D], F32).ap()

        pTb = [psb("pTb%d" % i, [128, 8, 128], BF16) for i in range(2)]
        pA = [psb("pA%d" % i, [128, 512]) for i in range(6)]

        ss = sb("ss", [128, 1]); rstd = sb("rstd", [128, 1])
        hb = sb("hb", [128, D], BF16)
        junk = hb
        hT = sb("hT", [128, 8, 128], BF16)
        mix = sb("mix", [128, D], BF16)
        mixT = sb("mixT", [128, 8, 128], BF16)
        vld = sb("vld", [128, 1])

        def norm_and_transpose(xin, rows):
            act(junk[0:rows, :], xin, AF.Square, accum=ss[0:rows, :])
            rstd_from_ss(rstd[0:rows, :], ss[0:rows, :], 1.0 / D)
            ts(hb[0:rows, :], xin, rstd[0:rows, :], ALU.mult)
            for k in range(8):
                tr(pTb[0][:, k, 0:rows], hb[0:rows, k * 128:(k + 1) * 128], ident_b[0:rows, 0:rows])
            cp(hT[:, :, 0:rows], pTb[0][:, :, 0:rows], eng=ACT)

        def out_proj(Wo, xres, xout, rows, banks=None):
            banks = banks or (pA[0], pA[1])
            for k in range(8):
                tr(pTb[1][:, k, 0:rows], mix[0:rows, k * 128:(k + 1) * 128], ident_b[0:rows, 0:rows])
            cp(mixT[:, :, 0:rows], pTb[1][:, :, 0:rows], eng=ACT)
            for n in range(2):
                for k in range(8):
                    mm(banks[n][0:rows, :], mixT[:, k, 0:rows], Wo[:, k, n * 512:(n + 1) * 512], start=(k == 0), stop=(k == 7))
                tt(xout[:, n * 512:(n + 1) * 512], banks[n][0:rows, :], xres[:, n * 512:(n + 1) * 512], ALU.add)

        sza2 = [sb("sza%d" % i, [128, 512], BF16) for i in range(2)]; szb2 = [sb("szb%d" % i, [128, 512], BF16) for i in range(2)]
        kvg2 = [sb("kvg%d" % i, [128, 264]) for i in range(2)]
        vtok = [sb("vtok%d" % i, [128, 128], BF16) for i in range(3)]
        kTa = [sb("kTa%d" % i, [128, 128], BF16) for i in range(3)]
        qTa2 = [sb("qTa%d" % i, [128, 4, 128], BF16) for i in range(2)]
        xbT2 = [sb("xbT%d" % i, [128, 12, 131]) for i in range(2)]
        sza, szb, kvg, qTa, xbT = sza2[0], szb2[0], kvg2[0], qTa2[0], xbT2[0]
        cacc = sb("cacc", [128, 12, 128])
        sq = sb("sq", [128, 8, 128], BF16)
        rn = sb("rn", [128, 8, 128])
        qTn = sb("qTn", [128, 4, 128], BF16); kTn = sb("kTn", [128, 4, 128], BF16)
        gval = sb("gval", [128, 4]); beta = sb("beta", [128, 4]); gtmp = sb("gtmp", [128, 4])
        gc = sb("gc", [128, 4]); ngc = sb("ngc", [128, 4]); egl = sb("egl", [128, 4]); edk = sb("edk", [128, 4])
        bdk = sb("bdk", [128, 4])
        gh1 = sb("gh1", [128, 4, 128])
        W4 = lambda nm_, dt_=BF16: sb(nm_, [128, 4, 128], dt_)
        Dm4 = W4("Dm4", F32); DTm4 = W4("DTm4", F32); egcb4 = W4("egcb4", F32)
        Nfull4 = W4("Nfull4"); NnW = [W4("NnW0"), W4("NnW1")]; NtW = [W4("NtW0"), W4("NtW1")]; RrW = [W4("RrW0"), W4("RrW1")]
        Lm4 = W4("Lm4"); Xm4 = W4("Xm4"); Tm4 = W4("Tm4"); kbd4 = W4("kbd4"); kdd4 = W4("kdd4"); vbeta4 = W4("vbeta4")
        nwT4 = W4("nwT4"); vnew4 = W4("vnew4"); qdT4 = W4("qdT4"); QKm4 = W4("QKm4")
        dtmp = Dm4[:, 0, :]; Tm = Tm4[:, 0, :]; Xm = Xm4[:, 0, :]; Nfull = Nfull4[:, 0, :]; vbeta = vbeta4[:, 0, :]
        dma(Dm4[:], cmask_d.rearrange("m p x -> p m x"))
        cmask_b = sb("cmask_b", [128, 4, 128], BF16); cp(cmask_b[:], Dm4[:])
        Sdn = sb("Sdn", [128, 4, 128]); Sdnb = sb("Sdnb", [128, 4, 128], BF16)
        memset(Sdn[:], 0.0); memset(Sdnb[:], 0.0)
        osb = sb("osb", [128, 512])
        oss = sb("oss", [128, 4]); orn = sb("orn", [128, 4])
        oss8 = sb("oss8", [16, 8]); orn8 = sb("orn8", [16, 8]); kq = sb("kq", [16, 4])
        otmp = sb("otmp", [128, 256])
        s_sb = sb("s_sb", [128, 256]); p_sb = sb("p_sb", [128, 256], BF16); pT_sb = sb("pT_sb", [128, 2, 128], BF16)
        mrow = sb("mrow", [128, 1]); nmrow = sb("nmrow", [128, 1]); rsum = sb("rsum", [128, 1]); esk = sb("esk", [128, 1])
        rden = sb("rden", [128, 1])
        kd1 = sb("kd1", [128, 512], BF16); v1 = sb("v1", [128, D], BF16); sz1 = sb("sz1", [128, D], BF16)
        gk_aug = sb("gk_aug", [32, 128], BF16); memset(gk_aug[:], 1.0)
        la = sb("la", [128, 512]); sp1 = sb("sp1", [128, 512])
        bc_sb = sb("bc_sb", [128, 512]); edec = sb("edec", [128, 512])
        ebT = sb("ebT", [128, 4, 128]); einvT = sb("einvT", [128, 4, 128])
        qdT1 = sb("qdT1", [128, 4, 128], BF16); kinvT1 = sb("kinvT1", [128, 4, 128], BF16)
        QKm1 = sb("QKm1", [128, 128], BF16)
        Sg = sb("Sg", [128, 4, 256]); Sgb = sb("Sgb", [128, 4, 256], BF16)
        memset(Sg[:], 0.0); memset(Sgb[:], 0.0)
        o1 = cacc[:, 0:8, :].rearrange("p c t -> p (c t)")
        x2t = o1; yt = x1t

        def gated_rms(o_ap, width, nheads, onorm_bc, sz, mix_off, R=128):
            for h in range(nheads):
                act(otmp[0:R, 0:width], o_ap[0:R, h * width:(h + 1) * width], AF.Square, accum=oss[0:R, h:h + 1])
            rstd_from_ss(orn[0:R, 0:nheads], oss[0:R, 0:nheads], 1.0 / width)
            for h in range(nheads):
                stt(otmp[0:R, 0:width], o_ap[0:R, h * width:(h + 1) * width], orn[0:R, h:h + 1], onorm_bc[0:R, 0:width], ALU.mult, ALU.mult)
                tt(mix[0:R, mix_off + h * width:mix_off + (h + 1) * width], otmp[0:R, 0:width], sz[0:R, h * width:(h + 1) * width], ALU.mult)

        R = SB_
        kdd_s = sz1[0:R, 0:512]; dltf = xbT[0:R, :, :].rearrange("p c t -> p (c t)")[:, 0:512]
        xs1d = nc.dram_tensor("xs1d", [R, D], F32).ap()
        osc = nc.dram_tensor("osc", [8, R, 64], F32).ap()
        eyeb4 = sq[:, :, :].rearrange("p a b -> p (a b)").rearrange("p (b h t) -> p b h t", b=16, h=4)
        memset(sq[:], 0.0)
        for b in range(R):
            memset(eyeb4[:, b, :, b:b + 1], 1.0)
        sinkc = sb("sinkc", [8, 1])
        sink_hc = sink_d.rearrange("o (c half) -> half c o", half=2)
        dma(sinkc[0:4, :], sink_hc[0]); dma(sinkc[4:8, :], sink_hc[1])
        evm = sb("evm", [8, 2])
        memset(evm[:], 1.0)
        asel(evm[:, 0:1], evm[:, 0:1], [[0, 1]], ALU.is_ge, 0.0, 3, -1)
        asel(evm[:, 1:2], evm[:, 1:2], [[0, 1]], ALU.is_ge, 0.0, -4, 1)
        xsb = xt[0][0:R, :]
        dma(xsb, xs[:, :])
        norm_and_transpose(xsb, R)
        qs = la; xbs = cacc[0:R, :, :].rearrange("p c t -> p (c t)")
        for n, (c0, cw) in enumerate([(0, 512), (512, 512), (1024, 264)]):
            for k in range(8):
                mm(pA[n][0:R, 0:cw], hT[:, k, 0:R], W0t[:, k, c0:c0 + cw], start=(k == 0), stop=(k == 7))
        act(sza[0:R, :], pA[0][0:R, :], AF.Silu)
        act(szb[0:R, :], pA[1][0:R, :], AF.Silu)
        cp(kvg[0:R, :], pA[2][0:R, 0:264])
        for n, c0 in enumerate([0, 640, 1152, 1664]):
            for k in range(8):
                mm(pA[n][0:R, :], hT[:, k, 0:R], W0f[:, k, c0:c0 + 512], start=(k == 0), stop=(k == 7))
            if n == 0:
                act(qs[0:R, :], pA[0][0:R, :], AF.Copy, scale=0.125)
            else:
                cp(xbs[:, (n - 1) * 512:n * 512], pA[n][0:R, :])
        dma(sk[:, 0:127, :], ck[:, 1:128, :], eng=POOL); dma(sv[:, 0:127, :], cv[:, 1:128, :], eng=POOL)
        dma(sk[:, 127, :], kvg[0:R, 0:128], eng=POOL); dma(sv[:, 127, :], kvg[0:R, 128:256], eng=POOL)
        dma(sconv[:, 0:2, :], cconv[:, 1:3, :], eng=POOL)
        dma(sconv[:, 2, :], xbs, eng=POOL)
        cres = [rn[0:R, 0:4, :].rearrange("p c t -> p (c t)"), rn[0:R, 4:8, :].rearrange("p c t -> p (c t)"), osb[0:R, :]]
        tA, tW = sp1[0:R, :], bc_sb[0:R, :]
        for j in range(3):
            for i in range(4):
                dma(tW, convw_d[i, j * 512:(j + 1) * 512].partition_broadcast(R))
                if i < 3:
                    dma(tA, cconv[:, i, j * 512:(j + 1) * 512])
                    src = tA
                else:
                    src = xbs[:, j * 512:(j + 1) * 512]
                if i == 0:
                    tt(cres[j], src, tW, ALU.mult)
                else:
                    tt(edec[0:R, :], src, tW, ALU.mult)
                    tt(cres[j], cres[j], edec[0:R, :], ALU.add)
            act(cres[j], cres[j], AF.Silu)
        qk3 = rn[0:R, :, :]
        for c in range(8):
            act(otmp[0:R, 0:128], qk3[:, c, :], AF.Square, accum=oss8[0:R, c:c + 1])
        rstd_from_ss(orn8[0:R, :], oss8[0:R, :], 1.0)
        for c in range(8):
            ts(qk3[:, c, :], qk3[:, c, :], orn8[0:R, c:c + 1], ALU.mult, (128.0 ** -0.5 if c < 4 else 1.0), ALU.mult)
        tt(edec[0:R, :], rn[0:R, 0:4, :].rearrange("p c t -> p (c t)"), rn[0:R, 4:8, :].rearrange("p c t -> p (c t)"), ALU.mult)
        P.op(DVE, lambda e: e.reduce_sum(out=kq[0:R, :], in_=edec[0:R, :].rearrange("p (h t) -> p h t", h=4), axis=AX.X),
             reads=names(edec), writes=names(kq))
        for c in range(8):
            P.op(PE, lambda e, c=c: e.transpose(out=pA[5][:, c * 16:(c + 1) * 16], in_=qk3[:, c, :], identity=ident_f[0:R, 0:R]),
                 reads=names(rn, ident_f), writes=names(pA[5]))
        cp(qTn[:, :, 0:R], pA[5][:, 0:64].rearrange("p (c t) -> p c t", c=4))
        cp(kTn[:, :, 0:R], pA[5][:, 64:128].rearrange("p (c t) -> p c t", c=4))
        kTm = mixT[:, :, :].rearrange("p a b -> p (a b)").rearrange("p (b h t) -> p b h t", b=16, h=4)
        qTm = hT[:, :, :].rearrange("p a b -> p (a b)").rearrange("p (b h t) -> p b h t", b=16, h=4)
        for b in range(R):
            tt(kTm[:, b, :, :], kTn[:, :, 0:R], eyeb4[:, b, :, :], ALU.mult)
            tt(qTm[:, b, :, :], qTn[:, :, 0:R], eyeb4[:, b, :, :], ALU.mult, eng=POOL)
        cp(kdd_s, rn[0:R, 4:8, :].rearrange("p c t -> p (c t)"))
        act(beta[0:R, :], kvg[0:R, 256:260], AF.Sigmoid)
        tt(gtmp[0:R, :], kvg[0:R, 260:264], dtb_bc[0:R, :], ALU.add)
        act(gtmp[0:R, :], gtmp[0:R, :], AF.Exp)
        act(gtmp[0:R, :], gtmp[0:R, :], AF.Ln, bias=onec[0:R, :])
        tt(gval[0:R, :], gtmp[0:R, :], negA[0:R, :], ALU.mult)
        act(egl[0:R, :], gval[0:R, :], AF.Exp)
        egd = s_sb[0:R, 0:64]
        for b in range(R):
            ts(egd[:, b * 4:(b + 1) * 4], egl[0:R, :], ident_f[0:R, b:b + 1], ALU.mult)
        mm(pA[5][:, 128:192], ones_f[0:R, :], egd)
        eg_bc = dtmp[:, 0:64]
        cp(eg_bc, pA[5][:, 128:192])
        Sring = [Sdn, gh1]; Sbring = [Sdnb, qTa]
        for b in range(R):
            Sb_, Sbb = Sring[b % 2], Sbring[b % 2]
            dma(Sb_[:], sdn[b].rearrange("h k v -> k h v"))
            cp(Sbb[:], Sb_[:], eng=ACT)
            for h in range(4):
                mm(pA[h][0:R, 0:128], kTm[:, b, h, :], Sbb[:, h, :], start=(b == 0), stop=(b == R - 1))
        dlt = v1[0:R, 512:1024]
        vS = osb[0:R, :]
        for h in range(4):
            stt(otmp[0:R, 0:128], pA[h][0:R, 0:128], egl[0:R, h:h + 1], vS[:, h * 128:(h + 1) * 128], ALU.mult, ALU.subtract)
            ts(dltf[:, h * 128:(h + 1) * 128], otmp[0:R, 0:128], beta[0:R, h:h + 1], ALU.mult, -1.0, ALU.mult)
        cp(dlt, dltf)
        kmr = [kd1[0:R, :], v1[0:R, 0:512]]
        for b in range(R):
            Sb_, Sbb = Sring[b % 2], Sbring[b % 2]
            dma(Sb_[:], sdn[b].rearrange("h k v -> k h v"))
            cp(Sbb[:], Sb_[:], eng=ACT)
            km = kmr[b % 2]
            ts(km, kdd_s, ident_f[0:R, b:b + 1], ALU.mult)
            for h in range(4):
                mm(pA[h][0:R, 0:128], qTm[:, b, h, :], Sbb[:, h, :], start=(b == 0), stop=(b == R - 1))
                up = pA[4 + h % 2]
                mm(up[:, 0:128], km[:, h * 128:(h + 1) * 128], dlt[:, h * 128:(h + 1) * 128])
                stt(Sb_[:, h, :], Sb_[:, h, :], eg_bc[:, b * 4 + h:b * 4 + h + 1], up[:, 0:128], ALU.mult, ALU.add)
            dma(sdn_o[b].rearrange("h k v -> k h v"), Sb_[:], eng=POOL)
        ogs = sp1[0:R, :]
        for h in range(4):
            ts(otmp[0:R, 0:128], dltf[:, h * 128:(h + 1) * 128], kq[0:R, h:h + 1], ALU.mult)
            stt(ogs[:, h * 128:(h + 1) * 128], pA[h][0:R, 0:128], egl[0:R, h:h + 1], otmp[0:R, 0:128], ALU.mult, ALU.add)
        gated_rms(ogs, 128, 4, onb_bc, szb, 512, R=R)
        for c in range(4):
            P.op(PE, lambda e, c=c: e.transpose(out=pA[5][:, c * 16:(c + 1) * 16], in_=qs[0:R, c * 128:(c + 1) * 128], identity=ident_f[0:R, 0:R]),
                 reads=names(la, ident_f), writes=names(pA[5]))
        qblk = Nfull[:, :].rearrange("p (b q) -> p b q", b=16)
        memset(Nfull[:], 0.0)
        for c in range(4):
            for half in range(2):
                hs = slice(half * 64, half * 64 + 64)
                cp(qblk[hs, :, half * 4 + c], pA[5][hs, c * 16:(c + 1) * 16])
        scs = [xt[1][0:8, :].rearrange("p (b t) -> p b t", b=8), rn[0:8, :, :]]
        kst = [cacc[:, 0, :], cacc[:, 1, :]]
        for b in range(R):
            dma(kst[b % 2], sk[b])
            P.op(PE, lambda e, b=b: e.transpose(out=pA[4][:, (b % 2) * 128:(b % 2 + 1) * 128], in_=kst[b % 2], identity=ident_f[:]),
                 reads=names(cacc, ident_f), writes=names(pA[4]))
            cp(Tm[:], pA[4][:, (b % 2) * 128:(b % 2 + 1) * 128])
            mm(pA[3][0:8, 0:128], qblk[:, b, :], Tm[:])
            cp(scs[b // 8][:, b % 8, :], pA[3][0:8, 0:128])
        mx8 = sb("mx8", [8, 16]); nmx8 = sb("nmx8", [8, 16]); rs8 = sb("rs8", [8, 16]); es8 = sb("es8", [8, 16])
        for g in range(2):
            P.op(DVE, lambda e, g=g: e.reduce_max(out=mx8[:, g * 8:(g + 1) * 8], in_=scs[g], axis=AX.X),
                 reads=names(scs[g]), writes=names(mx8))
        ts(nmx8[:], mx8[:], sinkc[:], ALU.max, -1.0, ALU.mult)
        pbs = p_sb[0:8, 0:128]
        ohb = x1t[0:8, :].rearrange("p (b d) -> p b d", b=16)
        for b in range(R):
            act(pbs, scs[b // 8][:, b % 8, :], AF.Exp, bias=nmx8[:, b:b + 1], accum=rs8[:, b:b + 1])
            tr(pTb[1][:, 0, 0:8], pbs, ident_b[0:8, 0:8])
            cp(pT_sb[:, 0, 0:8], pTb[1][:, 0, 0:8], eng=ACT)
            dma(kst[b % 2], sv[b])
            cp(Xm[:], kst[b % 2], eng=POOL)
            mm(pA[3][0:8, 128:256], pT_sb[:, 0, 0:8], Xm[:])
            ts(otmp[0:8, 0:64], pA[3][0:8, 192:256], evm[:, 1:2], ALU.mult)
            stt(ohb[:, b, :], pA[3][0:8, 128:192], evm[:, 0:1], otmp[0:8, 0:64], ALU.mult, ALU.add)
        for b in range(R):
            act(es8[:, b:b + 1], sinkc[:], AF.Exp, bias=nmx8[:, b:b + 1])
        tt(rs8[:], rs8[:], es8[:], ALU.add)
        recip(rs8[:], rs8[:])
        for b in range(R):
            ts(ohb[:, b, :], ohb[:, b, :], rs8[:, b:b + 1], ALU.mult)
        dma(osc[:, :, :], ohb[:], eng=POOL)
        oas = edec[0:R, :]
        oas4 = oas.rearrange("p (c half d) -> p c half d", c=4, half=2)
        for half in range(2):
            dma(oas4[:, :, half, :], osc[half * 4:(half + 1) * 4].rearrange("c b d -> b c d"))
        tt(mix[0:R, 0:512], oas, sza[0:R, :], ALU.mult)
        xs1 = xt[1][0:R, :]
        out_proj(W0o, xsb, xs1, R)
        dma(xs1d[:, :], xs1, eng=POOL)
        memset(xbT2[0][:], 0.0); memset(xbT2[1][:], 0.0); memset(Sdn[:], 0.0); memset(Sdnb[:], 0.0)
        memset(kTa[2][:], 0.0); memset(vtok[2][:], 0.0)

        cstp = [la, sp1, bc_sb]
        bkA = [pA[0], pA[5]]

        def stageA(it):
                xc = xt[it % 2]
                sza, szb, kvg, qTa, xbT = sza2[it % 2], szb2[it % 2], kvg2[it % 2], qTa2[it % 2], xbT2[it % 2]
                vcur, vprev = vtok[it % 3], vtok[(it + 2) % 3]
                kcur, kprev = kTa[it % 3], kTa[(it + 2) % 3]
                dma(xc[:], xp[it * 128:(it + 1) * 128, :])
                norm_and_transpose(xc[:], 128)
                for n, (c0, cw) in enumerate([(0, 512), (512, 512), (1024, 264)]):
                    for k in range(8):
                        mm(bkA[n % 2][:, 0:cw], hT[:, k, :], W0t[:, k, c0:c0 + cw], start=(k == 0), stop=(k == 7))
                    if n == 0:
                        act(sza[:], bkA[0][:], AF.Silu)
                    elif n == 1:
                        act(szb[:], bkA[1][:], AF.Silu)
                    else:
                        cp(kvg[:], bkA[0][:, 0:264])
                cp(vcur[:], kvg[:, 128:256], eng=POOL)
                for c in range(17):
                    bank = bkA[(c + 1) % 2]; sl_ = slice(((c // 2) % 4) * 128, ((c // 2) % 4 + 1) * 128)
                    for k in range(8):
                        mm(bank[:, sl_], W0f[:, k, c * 128:(c + 1) * 128], hT[:, k, :],
                           start=(k == 0), stop=(k == 7))
                    if c < 4:
                        act(qTa[:, c, :], bank[:, sl_], AF.Copy, scale=0.125)
                    elif c == 4:
                        cp(kcur[:], bank[:, sl_])
                    else:
                        cp(xbT[:, c - 5, 3:131], bank[:, sl_], eng=(ACT if c % 2 else DVE))
                if it == NT - 2:
                    dma(pk[0:112, :], kvg[16:128, 0:128], eng=POOL); dma(pv[0:112, :], kvg[16:128, 128:256], eng=POOL)
                if it == NT - 1:
                    dma(pk[112:128, :], kvg[0:16, 0:128], eng=POOL); dma(pv[112:128, :], kvg[0:16, 128:256], eng=POOL)
                    for c in range(12):
                        P.op(PE, lambda e, c=c: e.transpose(out=pA[0][0:3, (c % 4) * 128:(c % 4 + 1) * 128], in_=xbT[:, c, 16:19], identity=ident_f[:, :]),
                             reads=names(xbT, ident_f), writes=names(pA[0]))
                        if c % 4 == 3:
                            cp(cstp[(c - 3) // 4][0:3, :], pA[0][0:3, :])
                    for j_ in range(3):
                        dma(pconv[:, j_ * 512:(j_ + 1) * 512], cstp[j_][0:3, :], eng=POOL)


        def stage_swa(it):
                xc = xt[it % 2]
                sza, szb, kvg, qTa, xbT = sza2[it % 2], szb2[it % 2], kvg2[it % 2], qTa2[it % 2], xbT2[it % 2]
                vcur, vprev = vtok[it % 3], vtok[(it + 2) % 3]
                kcur, kprev = kTa[it % 3], kTa[(it + 2) % 3]
                for p in range(8):
                    c, half = p // 2, p % 2
                    pr = slice(half * 64, half * 64 + 64)
                    sc = pA[1]
                    mm(sc[:, 0:128], qTa[pr, c, :], kprev[pr, :])
                    mm(sc[:, 128:256], qTa[pr, c, :], kcur[pr, :])
                    tt(s_sb[:], sc[:, 0:256], (band0 if it == 0 else band)[:], ALU.add)
                    rmax(mrow[:], s_sb[:])
                    ts(nmrow[:], mrow[:], sink_bc[:, p:p + 1], ALU.max, -1.0, ALU.mult)
                    act(p_sb[:], s_sb[:], AF.Exp, bias=nmrow[:], accum=rsum[:])
                    act(esk[:], sink_bc[:, p:p + 1], AF.Exp, bias=nmrow[:])
                    tt(rden[:], rsum[:], esk[:], ALU.add)
                    recip(rden[:], rden[:])
                    tr(pTb[1][:, 4, :], p_sb[:, 0:128], ident_b[:])
                    tr(pTb[1][:, 5, :], p_sb[:, 128:256], ident_b[:])
                    cp(pT_sb[:], pTb[1][:, 4:6, :], eng=ACT)
                    ob = pA[1]
                    mm(ob[:, 256:320], pT_sb[:, 0, :], vprev[:, half * 64:half * 64 + 64], start=True, stop=False)
                    mm(ob[:, 256:320], pT_sb[:, 1, :], vcur[:, half * 64:half * 64 + 64], start=False, stop=True)
                    stt(mix[:, p * 64:(p + 1) * 64], ob[:, 256:320], rden[:], sza[:, p * 64:(p + 1) * 64], ALU.mult, ALU.mult)


        def stage_pre(it):
                xc = xt[it % 2]
                sza, szb, kvg, qTa, xbT = sza2[it % 2], szb2[it % 2], kvg2[it % 2], qTa2[it % 2], xbT2[it % 2]
                vcur, vprev = vtok[it % 3], vtok[(it + 2) % 3]
                kcur, kprev = kTa[it % 3], kTa[(it + 2) % 3]
                dma(vld[:], valid[it * 128:(it + 1) * 128, :])
                for c in range(12):
                    act(cacc[:, c, :], xbT[:, c, 0:128], AF.Copy, scale=convw[:, 0, c:c + 1])
                    for i in range(1, 4):
                        stt(cacc[:, c, :], xbT[:, c, i:i + 128], convw[:, i, c:c + 1], cacc[:, c, :], ALU.mult, ALU.add)
                cp(xbT2[(it + 1) % 2][:, :, 0:3], xbT[:, :, 128:131], eng=POOL)
                act(cacc[:], cacc[:], AF.Silu)
                act(sq[:], cacc[:, 0:8, :], AF.Square)
                for c in range(8):
                    mm(pA[3 + c // 4][:, (c % 4) * 128:(c % 4 + 1) * 128], ones_b[:], sq[:, c, :])
                for hh in range(2):
                    act(rn[:, hh * 4:(hh + 1) * 4, :], pA[3 + hh][:], AF.Ln, bias=epsc[:])
                    act(rn[:, hh * 4:(hh + 1) * 4, :], rn[:, hh * 4:(hh + 1) * 4, :], AF.Exp, scale=-0.5)
                stt(qTn[:], cacc[:, 0:4, :], 128.0 ** -0.5, rn[:, 0:4, :], ALU.mult, ALU.mult)
                tt(kTn[:], cacc[:, 4:8, :], rn[:, 4:8, :], ALU.mult)
                act(beta[:], kvg[:, 256:260], AF.Exp, scale=-1.0)
                ts(beta[:], beta[:], 1.0, ALU.add)
                recip(beta[:], beta[:])
                ts(beta[:], beta[:], vld[:], ALU.mult)
                tt(gtmp[:], kvg[:, 260:264], dtb_bc[:], ALU.add)
                act(gtmp[:], gtmp[:], AF.Exp)
                act(gtmp[:], gtmp[:], AF.Ln, bias=onec[:])
                tt(gval[:], gtmp[:], negA[:], ALU.mult)
                ts(gval[:], gval[:], vld[:], ALU.mult)
                mm(pA[2][:, 0:4], triu_f[:], gval[:])
                mm(pA[2][:, 4:8], ones_f[:], gval[:])
                cp(gc[:], pA[2][:, 0:4])
                ts(ngc[:], pA[2][:, 0:4], -1.0, ALU.mult)
                act(egl[:], pA[2][:, 4:8], AF.Exp)
                tt(edk[:], pA[2][:, 4:8], gc[:], ALU.subtract)
                act(edk[:], edk[:], AF.Exp)
                act(bdk[:], gc[:], AF.Exp)
                tt(bdk[:], bdk[:], beta[:], ALU.mult)
                for h in range(4):
                    ts(gh1[:, h, :], ones_f[:], gval[:, h:h + 1], ALU.mult, eng=POOL)

        def G3(bank):
            return bank[:, :].rearrange("p (h t) -> p h t", h=4)

        def bc_mid(m2):
            return m2.unsqueeze(1).to_broadcast([128, 4, 128])

        def bc_in(v2):
            return v2.unsqueeze(2).to_broadcast([128, 4, 128])

        def mm4(bank, lf, rf, **kw):
            for h in range(4):
                mm(bank[:, h * 128:(h + 1) * 128], lf(h), rf(h), **kw)

        def stage_gdn(it):
            Gb, Vb, Kb, Qb = pA[2], pA[3], pA[3], pA[4]
            tb = pTb[1]
            mm4(Gb, lambda h: gh1[:, h, :], lambda h: triu_f[:])
            stt(Dm4[:], G3(Gb), -1.0, bc_in(gc[:, :]), ALU.mult, ALU.add)
            tt(Dm4[:], Dm4[:], bc_mid(nm_ls[:, :]), ALU.add)
            act(Dm4[:], Dm4[:], AF.Exp)
            tt(DTm4[:], G3(Gb), bc_in(ngc[:, :]), ALU.add)
            tt(DTm4[:], DTm4[:], bc_mid(nm_ui[:, :]), ALU.add)
            act(DTm4[:], DTm4[:], AF.Exp)
            act(egcb4[:], G3(Gb), AF.Exp)
            for h in range(4):
                tr(tb[:, h, :], kTn[:, h, :], ident_b[:])
            tt(kbd4[:], tb[:, 0:4, :], bc_in(bdk[:, :]), ALU.mult)
            tt(kdd4[:], tb[:, 0:4, :], bc_in(edk[:, :]), ALU.mult)
            for h in range(4):
                P.op(PE, lambda e, h=h: e.transpose(out=Vb[:, h * 128:(h + 1) * 128], in_=cacc[:, 8 + h, :], identity=ident_f[:]),
                     reads=names(cacc[:, 8 + h, :], ident_f), writes=names(Vb))
            tt(vbeta4[:], G3(Vb), bc_in(beta[:, :]), ALU.mult)
            mm4(Kb, lambda h: kTn[:, h, :], lambda h: kTn[:, h, :])
            mm4(Qb, lambda h: kTn[:, h, :], lambda h: qTn[:, h, :])
            tt(Dm4[:], Dm4[:], bc_in(beta[:, :]), ALU.mult)
            tt(Nfull4[:], G3(Kb), Dm4[:], ALU.mult)
            tt(NnW[0][:], Nfull4[:], bc_mid(cmask_b[:, 0, :]), ALU.mult)
            tt(QKm4[:], G3(Qb), DTm4[:], ALU.mult)
            tt(qdT4[:], qTn[:], egcb4[:], ALU.mult)
            for h in range(4):
                tr(tb[:, h, :], NnW[0][:, h, :], ident_b[:])
            cp(NtW[0][:], tb[:, 0:4, :], eng=ACT)
            stt(RrW[0][:], NtW[0][:], -1.0, bc_mid(ident_b[:, :]), ALU.mult, ALU.add)
            X, Y, Z = pA[2], pA[3], pA[4]
            ci = 0
            for lvl in range(3):
                mm4(X, lambda h: NtW[ci][:, h, :], lambda h: NnW[ci][:, h, :])
                if lvl < 2:
                    mm4(Y, lambda h: NnW[ci][:, h, :], lambda h: NtW[ci][:, h, :])
                cp(NnW[1 - ci][:], G3(X))
                if lvl < 2:
                    cp(NtW[1 - ci][:], G3(Y), eng=ACT)
                ci = 1 - ci
                mm4(Z, lambda h: NnW[ci][:, h, :], lambda h: RrW[lvl % 2][:, h, :])
                tt(RrW[1 - lvl % 2][:], G3(Z), RrW[lvl % 2][:], ALU.add)
            ri = 1
            for lv in range(3):
                tt(Lm4[:], Nfull4[:], bc_mid(cmask_b[:, 1 + lv, :]), ALU.mult)
                for h in range(4):
                    tr(tb[:, h, :], RrW[ri][:, h, :], ident_b[:])
                cp(Tm4[:], tb[:, 0:4, :], eng=ACT)
                mm4(X, lambda h: Lm4[:, h, :], lambda h: RrW[ri][:, h, :])
                cp(Xm4[:], G3(X))
                mm4(Y, lambda h: Tm4[:, h, :], lambda h: Xm4[:, h, :])
                tt(RrW[1 - ri][:], RrW[ri][:], G3(Y), ALU.subtract)
                ri = 1 - ri
            Tt = RrW[ri]
            mm4(X, lambda h: kbd4[:, h, :], lambda h: Tt[:, h, :])
            ts(nwT4[:], G3(X), -1.0, ALU.mult)
            for h in range(4):
                mm(Y[:, h * 128:(h + 1) * 128], Tt[:, h, :], vbeta4[:, h, :], start=True, stop=False)
                mm(Y[:, h * 128:(h + 1) * 128], nwT4[:, h, :], Sdnb[:, h, :], start=False, stop=True)
            cp(vnew4[:], G3(Y))
            for h in range(4):
                mm(Z[:, h * 128:(h + 1) * 128], qdT4[:, h, :], Sdnb[:, h, :], start=True, stop=False)
                mm(Z[:, h * 128:(h + 1) * 128], QKm4[:, h, :], vnew4[:, h, :], start=False, stop=True)
            cp(osb[:, :], Z[:, :], eng=ACT)
            Wb = pA[2]
            mm4(Wb, lambda h: kdd4[:, h, :], lambda h: vnew4[:, h, :])
            tt(Sdn[:], Sdn[:], bc_in(egl[:, :]), ALU.mult)
            tt(Sdn[:], Sdn[:], G3(Wb), ALU.add)
            cp(Sdnb[:], Sdn[:], eng=ACT)

        def stage_post(it):
                xc = xt[it % 2]
                sza, szb, kvg, qTa, xbT = sza2[it % 2], szb2[it % 2], kvg2[it % 2], qTa2[it % 2], xbT2[it % 2]
                vcur, vprev = vtok[it % 3], vtok[(it + 2) % 3]
                kcur, kprev = kTa[it % 3], kTa[(it + 2) % 3]
                gated_rms(osb, 128, 4, onb_bc, szb, 512)
                out_proj(W0o, xc, x1t, 128, banks=(pA[1], pA[2]))
                dma(x1d[it * 128:(it + 1) * 128, :], x1t[:], eng=POOL)


        def stream(fn, *a):
            lst = P.begin(); fn(*a); P.end()
            return lst

        P.ops.extend(stream(stageA, 0))
        for it in range(NT):
            nxt = stream(stageA, it + 1) if it + 1 < NT else []
            n1, n2 = len(nxt) // 5, (4 * len(nxt)) // 5
            P.interleave([nxt[:n1], stream(stage_pre, it)])
            _skip = os.environ.get("KSKIP", "")
            _swa = [] if "swa" in _skip else stream(stage_swa, it)
            _gdn = [] if "gdn" in _skip else stream(stage_gdn, it)
            P.interleave([nxt[n1:n2], _swa, _gdn])
            P.interleave([nxt[n2:], stream(stage_post, it)])

        load_w(W1t, w1t_d, 2560, g1); load_w(W1f, w1f_d, 1040, g1); load_w(W1o, w1o_d, D, None)

        memset(sq[:], 0.0)
        for b in range(R):
            memset(eyeb4[:, b, :, b:b + 1], 1.0)
        xs1b = xt[0][0:R, :]
        dma(xs1b, xs1d[:, :])
        norm_and_transpose(xs1b, R)
        for k in range(8):
            mm(pA[5][0:16, 0:R], W1f[:, k, 1024:1040], hT[:, k, 0:R], start=(k == 0), stop=(k == 7))
        cp(gk_aug[0:16, 0:R], pA[5][0:16, 0:R])
        mm(pA[4][0:R, :], gk_aug[0:17, 0:R], wgk[:, :])
        act(sp1[0:R, :], pA[4][0:R, :], AF.Exp, scale=-1.0)
        act(sp1[0:R, :], sp1[0:R, :], AF.Ln, bias=onec[0:R, :])
        act(la[0:R, :], sp1[0:R, :], AF.Exp, scale=-1.0 / 16.0)
        for k in range(8):
            mm(pA[0][0:R, :], hT[:, k, 0:R], W1f[:, k, 0:512], start=(k == 0), stop=(k == 7))
        stt(bc_sb[0:R, :], pA[0][0:R, :], 128.0 ** -0.5, la[0:R, :], ALU.mult, ALU.mult)
        for k in range(8):
            mm(pA[1][0:R, :], hT[:, k, 0:R], W1t[:, k, 0:512], start=(k == 0), stop=(k == 7))
        cp(edec[0:R, :], pA[1][0:R, :])
        ts(sp1[0:R, :], pA[0][0:R, :], 128.0 ** -0.5, ALU.mult)
        tt(sp1[0:R, :], sp1[0:R, :], edec[0:R, :], ALU.mult)
        P.op(DVE, lambda e: e.reduce_sum(out=kq[0:R, :], in_=sp1[0:R, :].rearrange("p (h t) -> p h t", h=4), axis=AX.X),
             reads=names(sp1), writes=names(kq))
        kb1 = kd1[0:R, :]
        cp(kb1, edec[0:R, :])
        vS1 = o1[0:R, :]
        for n in range(4):
            bank = pA[2 + n % 2]
            for k in range(8):
                mm(bank[0:R, :], hT[:, k, 0:R], W1t[:, k, 512 + n * 512:1024 + n * 512], start=(k == 0), stop=(k == 7))
            if n < 2:
                cp(vS1[:, n * 512:(n + 1) * 512], bank[0:R, :])
            else:
                act(sz1[0:R, (n - 2) * 512:(n - 1) * 512], bank[0:R, :], AF.Silu)
        vb1 = mix[0:R, :]
        cp(vb1, vS1)
        for h in range(4):
            P.op(PE, lambda e, h=h: e.transpose(out=pA[5][:, h * 16:(h + 1) * 16], in_=bc_sb[0:R, h * 128:(h + 1) * 128], identity=ident_f[0:R, 0:R]),
                 reads=names(bc_sb, ident_f), writes=names(pA[5]))
            P.op(PE, lambda e, h=h: e.transpose(out=pA[5][:, 64 + h * 16:64 + (h + 1) * 16], in_=la[0:R, h * 128:(h + 1) * 128], identity=ident_f[0:R, 0:R]),
                 reads=names(la, ident_f), writes=names(pA[5]))
        cp(qTn[:, :, 0:R], pA[5][:, 0:64].rearrange("p (c t) -> p c t", c=4))
        aT = dtmp[:, 0:64]
        cp(aT, pA[5][:, 64:128])
        qTm1 = hT[:, :, :].rearrange("p a b -> p (a b)").rearrange("p (b h t) -> p b h t", b=16, h=4)
        for b in range(R):
            tt(qTm1[:, b, :, :], qTn[:, :, 0:R], eyeb4[:, b, :, :], ALU.mult)
        Sr1 = [Sg[:], rn[:, :, :].rearrange("p (h a) t -> p h (a t)", h=4)]
        Sb1 = [Sgb[:], v1[:, :].rearrange("p (h v) -> p h v", h=4)]
        kmr1 = [QKm1[0:R, :], vbeta[0:R, :]]
        for b in range(R):
            Sb_, Sbb = Sr1[b % 2], Sb1[b % 2]
            dma(Sb_, sgla[b].rearrange("h k v -> k h v"))
            cp(Sbb, Sb_, eng=ACT)
            for h in range(4):
                mm(pA[h][0:R, 0:256], qTm1[:, b, h, :], Sbb[:, h, :], start=(b == 0), stop=(b == R - 1))
                km = kmr1[(b * 4 + h) % 2]
                ts(km, kb1[:, h * 128:(h + 1) * 128], ident_f[0:R, b:b + 1], ALU.mult)
                up = pA[4 + h % 2]
                mm(up[:, 0:256], km, vb1[:, h * 256:(h + 1) * 256])
                stt(Sb_[:, h, :], Sb_[:, h, :], aT[:, h * 16 + b:h * 16 + b + 1], up[:, 0:256], ALU.mult, ALU.add)
            dma(sgla_o[b].rearrange("h k v -> k h v"), Sb_, eng=POOL)
        og1 = x1t[0:R, :]
        for h in range(4):
            ts(otmp[0:R, :], vS1[:, h * 256:(h + 1) * 256], kq[0:R, h:h + 1], ALU.mult)
            tt(og1[:, h * 256:(h + 1) * 256], pA[h][0:R, 0:256], otmp[0:R, :], ALU.add)
        gated_rms(og1, 256, 4, onc_bc, sz1, 0, R=R)
        xs2 = xt[1][0:R, :]
        out_proj(W1o, xs1b, xs2, R)
        act(junk[0:R, :], xs2, AF.Square, accum=ss[0:R, :])
        rstd_from_ss(rstd[0:R, :], ss[0:R, :], 1.0 / D)
        stt(og1, xs2, rstd[0:R, :], fn_bc[0:R, :], ALU.mult, ALU.mult)
        dma(ys[:, :], og1, eng=POOL)
        memset(Sg[:], 0.0); memset(Sgb[:], 0.0)

        r1b = xbT2[0][:, :, :].rearrange("p c t -> p (c t)").bitcast(BF16)
        r2b = xbT2[1][:, :, :].rearrange("p c t -> p (c t)").bitcast(BF16)
        f512 = lambda t_: t_[:, :, :].rearrange("p h t -> p (h t)")
        L1S = [dict(hb=hb, hT=hT, la=la, sp1=sp1, bc_sb=bc_sb, edec=edec, ebT=ebT, einvT=einvT, qdT1=qdT1, kinvT1=kinvT1,
                    kd1=kd1, v1=v1, sz1=sz1, gk_aug=gk_aug),
               dict(hb=r1b[:, 0:1024], hT=r1b[:, 1024:2048].rearrange("p (k t) -> p k t", k=8),
                    la=f512(Dm4), sp1=f512(DTm4), bc_sb=f512(egcb4), edec=f512(rn[:, 0:4, :]), ebT=rn[:, 4:8, :], einvT=gh1[:, :, :],
                    qdT1=NnW[0][:, :, :], kinvT1=NnW[1][:, :, :], kd1=f512(NtW[0]), v1=r2b[:, 0:1024], sz1=r2b[:, 1024:2048],
                    gk_aug=RrW[0][0:32, 0, :])]
        memset(RrW[0][0:32, 0, :], 1.0)
        for it in range(NT):
            _S = L1S[it % 2]
            hb, hT, la, sp1, bc_sb, edec, ebT, einvT = _S["hb"], _S["hT"], _S["la"], _S["sp1"], _S["bc_sb"], _S["edec"], _S["ebT"], _S["einvT"]
            qdT1, kinvT1, kd1, v1, sz1, gk_aug = _S["qdT1"], _S["kinvT1"], _S["kd1"], _S["v1"], _S["sz1"], _S["gk_aug"]
            junk = hb
            x1t = xt[it % 2]
            dma(x1t[:], x1d[it * 128:(it + 1) * 128, :])
            dma(vld[:], valid[it * 128:(it + 1) * 128, :])
            norm_and_transpose(x1t[:], 128)
            for k in range(8):
                mm(pA[2][0:16, 0:128], W1f[:, k, 1024:1040], hT[:, k, :], start=(k == 0), stop=(k == 7))
            cp(gk_aug[0:16, :], pA[2][0:16, 0:128])
            mm(pA[1][:], gk_aug[0:17, :], wgk[:, :])
            act(sp1[:], pA[1][:], AF.Exp, scale=-1.0)
            act(sp1[:], sp1[:], AF.Ln, bias=onec[:])
            ts(la[:], sp1[:], vld[:], ALU.mult, -1.0 / 16.0, ALU.mult)
            mm(pA[0][:], triu_f[:], la[:])
            mm(pA[1][:], ones_f[:], la[:])
            cp(bc_sb[:], pA[0][:], eng=ACT)
            tt(edec[:], pA[1][:], bc_sb[:], ALU.subtract)
            act(edec[:], edec[:], AF.Exp)
            for h in range(4):
                mm(pA[2][:, h * 128:(h + 1) * 128], la[:, h * 128:(h + 1) * 128], triu_f[:])
            act(ebT[:], pA[2][:], AF.Exp)
            act(einvT[:], pA[2][:], AF.Exp, scale=-1.0)
            for c in range(8):
                bank = pA[c // 4]
                for k in range(8):
                    mm(bank[:, (c % 4) * 128:(c % 4 + 1) * 128], W1f[:, k, c * 128:(c + 1) * 128], hT[:, k, :],
                       start=(k == 0), stop=(k == 7))
            stt(qdT1[:], pA[0][:], 128.0 ** -0.5, ebT[:], ALU.mult, ALU.mult)
            tt(kinvT1[:], pA[1][:], einvT[:], ALU.mult)
            kraw = cacc[:, 8:12, :]
            cp(kraw, pA[1][:, :].rearrange("p (h t) -> p h t", h=4), eng=ACT)
            for h in range(4):
                P.op(PE, lambda e, h=h, kraw=kraw: e.transpose(out=pA[2][:, h * 128:(h + 1) * 128], in_=kraw[:, h, :], identity=ident_f[:]),
                     reads=names(kraw[:, h, :], ident_f), writes=names(pA[2]))
            tt(kd1[:], pA[2][:], edec[:], ALU.mult)
            for n in range(1, 5):
                bank = pA[n % 3]
                for k in range(8):
                    mm(bank[:], hT[:, k, :], W1t[:, k, n * 512:(n + 1) * 512], start=(k == 0), stop=(k == 7))
                if n < 3:
                    cp(v1[:, (n - 1) * 512:n * 512], bank[:], eng=ACT)
                else:
                    act(sz1[:, (n - 3) * 512:(n - 2) * 512], bank[:], AF.Silu)
            for h in range(4):
                wk = pA[3 + h % 2]
                mm(wk[:, 0:128], kinvT1[:, h, :], qdT1[:, h, :])
                tt(QKm1[:], wk[:, 0:128], triu_f[:], ALU.mult)
                ob = pA[5]
                mm(ob[:, 0:256], QKm1[:], v1[:, h * 256:(h + 1) * 256], start=True, stop=False)
                mm(ob[:, 0:256], qdT1[:, h, :], Sgb[:, h, :], start=False, stop=True)
                cp(o1[:, h * 256:(h + 1) * 256], ob[:, 0:256], eng=ACT)
                mm(wk[:, 256:512], kd1[:, h * 128:(h + 1) * 128], v1[:, h * 256:(h + 1) * 256])
                stt(Sg[:, h, :], Sg[:, h, :], ebT[:, h, 127:128], wk[:, 256:512], ALU.mult, ALU.add)
                cp(Sgb[:, h, :], Sg[:, h, :], eng=ACT)
            gated_rms(o1, 256, 4, onc_bc, sz1, 0)
            out_proj(W1o, x1t, x2t, 128, banks=(pA[3], pA[4]))
            act(junk[:], x2t[:], AF.Square, accum=ss[:])
            rstd_from_ss(rstd[:], ss[:], 1.0 / D)
            stt(yt[:], x2t[:], rstd[:], fn_bc[:], ALU.mult, ALU.mult)
            dma(yp[it * 128:(it + 1) * 128, :], yt[:], eng=POOL)

        _S = L1S[0]
        hb, hT, la, sp1, bc_sb, edec, ebT, einvT = _S["hb"], _S["hT"], _S["la"], _S["sp1"], _S["bc_sb"], _S["edec"], _S["ebT"], _S["einvT"]
        qdT1, kinvT1, kd1, v1, sz1, gk_aug = _S["qdT1"], _S["kinvT1"], _S["kd1"], _S["v1"], _S["sz1"], _S["gk_aug"]
        junk = hb
        dma(pdn.rearrange("h k v -> k h v"), Sdn[:], eng=POOL)
        dma(pgla.rearrange("h k v -> k h v"), Sg[:], eng=POOL)

        global _P
        _P = P
        P.warm_ap = ident_b[:]
        n = P.emit(nc, st)
    return nc


_NC = None


def _get_nc():
    global _NC
    if _NC is None:
        _NC = build_nc()
    return _NC


def kernel(x_prompt, x_sample, cache_swa_k, cache_swa_v, state_dn_conv, state_dn, state_gla,
           meta_tokens, norm_ab, w_in_ab, sink_a, conv_b, a_log_b, dt_bias_b, onorm_b, w_out_ab,
           norm_c, w_in_c, w_gk_up, b_gk, onorm_c, w_out_c, final_norm):
    f = lambda a: np.ascontiguousarray(np.asarray(a, dtype=np.float32))
    x_prompt, x_sample = f(x_prompt), f(x_sample)
    w0, w1 = f(w_in_ab)[0], f(w_in_c)[0]
    o = np.cumsum([0, 512, 128, 128, 512, 1536, 512, 4, 4])
    qa, ka, va, za, xb, zb, bb, ab = [w0[:, o[i]:o[i + 1]] for i in range(8)]
    perm = np.concatenate([np.arange(h * 64, h * 64 + 64) for h in HP])
    qa_p, za_p = qa[:, perm], za[:, perm]
    w0t = np.concatenate([za_p, zb, ka, va, bb, ab], axis=1)
    w0f = np.concatenate([qa_p, ka, xb], axis=1)
    w0o = f(w_out_ab)[0].copy()
    w0o[0:512] = w0o[0:512][perm]
    o1 = np.cumsum([0, 512, 512, 1024, 1024, 16])
    qc, kc, vc, zc, gkl = [w1[:, o1[i]:o1[i + 1]] for i in range(5)]
    w1t = np.concatenate([kc, vc, zc], axis=1)
    w1f = np.concatenate([qc, kc, gkl], axis=1)
    wgk = np.concatenate([f(w_gk_up)[0], f(b_gk)[0][None, :]], axis=0)
    sink_p = f(sink_a)[0][HP][None, :]
    common = dict(
        w0t=f(w0t), w0f=f(w0f), w0o=f(w0o), w1t=f(w1t), w1f=f(w1f), w1o=f(w_out_c)[0], wgk=f(wgk),
        g0=f(norm_ab)[0][:, None], g1=f(norm_c)[0][:, None], fn=f(final_norm)[None, :],
        sink=f(sink_p), convw=f(conv_b)[0], alog=f(a_log_b), dtb=f(dt_bias_b), onb=f(onorm_b), onc=f(onorm_c),
    )
    ii = np.arange(128)
    same = lambda b: ((ii[:, None] // b) == (ii[None, :] // b)).astype(np.float32)
    cmask = np.stack([same(16), same(32) - same(16), same(64) - same(32), same(128) - same(64)])
    common["cmask"] = cmask
    meta = f(meta_tokens)
    NT = NT_FULL; NTOK = NT * 128
    valid = np.zeros((NTOK, 1), np.float32); valid[:8208] = 1.0
    in_maps = []
    for c in range(8):
        b = c // 4
        xpc = np.zeros((NTOK, D), np.float32)
        xpc[:16] = meta; xpc[16:8208] = x_prompt[b]
        sl = slice(c * SB_, (c + 1) * SB_)
        m = dict(common)
        m.update(xp=xpc, valid=valid, xs=f(x_sample[sl, 0, :]),
                 ck=f(np.asarray(cache_swa_k)[0, sl].reshape(SB_, 128, 128)),
                 cv=f(np.asarray(cache_swa_v)[0, sl].reshape(SB_, 128, 128)),
                 cconv=f(np.asarray(state_dn_conv)[0, sl]), sdn=f(np.asarray(state_dn)[0, sl]),
                 sgla=f(np.asarray(state_gla)[0, sl]))
        in_maps.append(m)
    nc = _get_nc()
    res = run_bass_kernel_spmd(nc, in_maps, core_ids=list(range(8))).results
    R = lambda k, cs: [np.asarray(res[c][k]) for c in cs]
    y_prompt = np.stack([r[16:8208] for r in R("yp", [0, 4])])
    y_sample = np.concatenate(R("ys", range(8)))[:, None, :]
    pk = np.stack(R("pk", [0, 4])).reshape(1, 2, 128, 2, 64)
    pv = np.stack(R("pv", [0, 4])).reshape(1, 2, 128, 2, 64)
    pconv = np.stack(R("pconv", [0, 4]))[None]
    pdn = np.stack(R("pdn", [0, 4]))[None]
    pgla = np.stack(R("pgla", [0, 4]))[None]
    sk = np.concatenate(R("sk", range(8))).reshape(1, 128, 128, 2, 64)
    sv = np.concatenate(R("sv", range(8))).reshape(1, 128, 128, 2, 64)
    sconv = np.concatenate(R("sconv", range(8)))[None]
    sdn_o = np.concatenate(R("sdn_o", range(8)))[None]
    sgla_o = np.concatenate(R("sgla_o", range(8)))[None]
    outs = (y_prompt, y_sample, pk, pv, pconv, pdn, pgla, sk, sv, sconv, sdn_o, sgla_o)
    return tuple(np.ascontiguousarray(a, dtype=np.float32) for a in outs)
```
